# Optimizing a Trainium2 kernel written in Bass

```python
import math
import jax, jax.numpy as jnp
from jax import lax
import numpy as np

D_MODEL = 1024
BATCH = 2
SEQ = 8192
DEPTH = 2

GRID_W = 64
CTX_LEN = 256
ROPE_THETA = 10000.0
EPS = 1e-6
Q_BLOCK = 128

MLA_HEADS = 4
MLA_Q_RANK = 256
MLA_KV_RANK = 128
MLA_NOPE = 64
MLA_ROPE = 32
MLA_V = 64
DIFF_HEADS = 4
DIFF_QK = 32
DIFF_V = 2 * DIFF_QK
GQA_HEADS = 8
GQA_KV_HEADS = 2
GQA_DIM = 64
GQA_GROUP = GQA_HEADS // GQA_KV_HEADS

MIX_WIDTH = MLA_HEADS * MLA_V + DIFF_HEADS * DIFF_V + GQA_HEADS * GQA_DIM
FFN_HIDDEN = -(-8 * D_MODEL // (3 * 256)) * 256

IN_SIZES = (
    MLA_Q_RANK,
    MLA_KV_RANK + MLA_ROPE,
    DIFF_HEADS * 2 * DIFF_QK,
    DIFF_HEADS * 2 * DIFF_QK,
    DIFF_HEADS * DIFF_V,
    GQA_HEADS * GQA_DIM,
    GQA_KV_HEADS * GQA_DIM,
    GQA_KV_HEADS * GQA_DIM,
)
IN_WIDTH = sum(IN_SIZES)

MLA_SCALE = 1.0 / math.sqrt(MLA_NOPE + MLA_ROPE)
DIFF_SCALE = 1.0 / math.sqrt(DIFF_QK)
GQA_SCALE = 1.0 / math.sqrt(GQA_DIM)

kernel_name = "hymba_style_mla_diff_gqa_dit_block"


def _rms(x, g):
    xf = x.astype(jnp.float32)
    y = xf * lax.rsqrt(jnp.mean(xf * xf, axis=-1, keepdims=True) + EPS)
    return (y * g.astype(jnp.float32)).astype(x.dtype)


def _axial_rope_tables(row, col, rot_dim):
    quarter = rot_dim // 4
    inv = ROPE_THETA ** (-jnp.arange(quarter, dtype=jnp.float32) / quarter)
    ang = jnp.concatenate([row.astype(jnp.float32)[:, None] * inv,
                           col.astype(jnp.float32)[:, None] * inv], axis=-1)
    return jnp.cos(ang), jnp.sin(ang)


def _rope(x, tables):
    if tables is None:
        return x
    cos, sin = tables
    half = x.shape[-1] // 2
    xf = x.astype(jnp.float32)
    x1, x2 = xf[..., :half], xf[..., half:]
    c = cos[None, :, None, :]
    s = sin[None, :, None, :]
    return jnp.concatenate([x1 * c - x2 * s, x2 * c + x1 * s], axis=-1).astype(x.dtype)


def _hf(t):
    return t.transpose(0, 2, 1, 3)


def _sdpa(q, k, v, scale):
    s = jnp.einsum('bhgqd,bhkd->bhgqk', q, k).astype(jnp.float32) * scale
    p = jax.nn.softmax(s, axis=-1).astype(v.dtype)
    return jnp.einsum('bhgqk,bhkd->bhgqd', p, v)


def _latent_attention(q, k, v, scale):
    B, Hk, G, S, d = q.shape
    nb = S // Q_BLOCK
    qb = q.reshape(B, Hk, G, nb, Q_BLOCK, d).transpose(3, 0, 1, 2, 4, 5)
    ob = lax.map(lambda qi: _sdpa(qi, k, v, scale), qb)
    return ob.transpose(1, 2, 3, 0, 4, 5).reshape(B, Hk, G, S, ob.shape[-1])


def _project(h, w_in, g_mla_q, w_mla_qb, g_mla_kv, w_mla_kvb, g_gqa_q, g_gqa_k,
             rope_small, rope_large):
    B, T, _ = h.shape
    p = h @ w_in
    idx = np.cumsum(IN_SIZES)[:-1].tolist()
    q_a, kv_a, dq, dk, dv, gq, gk, gv = jnp.split(p, idx, axis=-1)

    q = (_rms(q_a, g_mla_q) @ w_mla_qb).reshape(B, T, MLA_HEADS, MLA_NOPE + MLA_ROPE)
    q_nope, q_pe = q[..., :MLA_NOPE], _rope(q[..., MLA_NOPE:], rope_small)
    c_kv = kv_a[..., :MLA_KV_RANK]
    k_pe = _rope(kv_a[..., MLA_KV_RANK:][:, :, None, :], rope_small)
    kv = (_rms(c_kv, g_mla_kv) @ w_mla_kvb).reshape(B, T, MLA_HEADS, MLA_NOPE + MLA_V)
    k_nope, v_mla = kv[..., :MLA_NOPE], kv[..., MLA_NOPE:]
    q_mla = jnp.concatenate([q_nope, q_pe], axis=-1)
    k_mla = jnp.concatenate([k_nope, jnp.broadcast_to(k_pe, (B, T, MLA_HEADS, MLA_ROPE))], axis=-1)
    mla = (_hf(q_mla)[:, :, None], _hf(k_mla), _hf(v_mla))

    q_d = _rope(dq.reshape(B, T, DIFF_HEADS * 2, DIFF_QK), rope_small)
    k_d = _rope(dk.reshape(B, T, DIFF_HEADS * 2, DIFF_QK), rope_small)
    v_d = jnp.repeat(dv.reshape(B, T, DIFF_HEADS, DIFF_V), 2, axis=2)
    diff = (_hf(q_d)[:, :, None], _hf(k_d), _hf(v_d))

    q_g = _rope(_rms(gq.reshape(B, T, GQA_HEADS, GQA_DIM), g_gqa_q), rope_large)
    k_g = _rope(_rms(gk.reshape(B, T, GQA_KV_HEADS, GQA_DIM), g_gqa_k), rope_large)
    v_g = gv.reshape(B, T, GQA_KV_HEADS, GQA_DIM)
    q_g = q_g.reshape(B, T, GQA_KV_HEADS, GQA_GROUP, GQA_DIM).transpose(0, 2, 3, 1, 4)
    gqa = (q_g, _hf(k_g), _hf(v_g))
    return (mla, diff, gqa)


def _merge(o_mla, o_diff, o_gqa, lam, lam_init, g_diff_sub):
    B, _, _, T, _ = o_mla.shape
    y_mla = o_mla[:, :, 0].transpose(0, 2, 1, 3).reshape(B, T, MLA_HEADS * MLA_V)
    od = o_diff[:, :, 0].reshape(B, DIFF_HEADS, 2, T, DIFF_V)
    d = od[:, :, 0] - lam.astype(od.dtype) * od[:, :, 1]
    d = _rms(d, g_diff_sub) * (1.0 - lam_init)
    y_diff = d.transpose(0, 2, 1, 3).reshape(B, T, DIFF_HEADS * DIFF_V)
    y_gqa = o_gqa.transpose(0, 3, 1, 2, 4).reshape(B, T, GQA_HEADS * GQA_DIM)
    return jnp.concatenate([y_mla, y_diff, y_gqa], axis=-1)


def _swiglu(h, w_gate, w_up, w_down):
    return (jax.nn.silu(h @ w_gate) * (h @ w_up)) @ w_down


def setup_inputs(seed: int = 0) -> dict:
    key = jax.random.key(seed)
    ks = iter(jax.random.split(key, 32))

    def nrm(shape, scale):
        return jax.random.normal(next(ks), shape, jnp.float32) * scale

    def gain(shape):
        return 1.0 + 0.05 * jax.random.normal(next(ks), shape, jnp.float32)

    D = D_MODEL
    return {
        "x": nrm((BATCH, SEQ, D), 1.0),
        "c": nrm((BATCH, D), 1.0),
        "ctx": nrm((BATCH, CTX_LEN, D), 1.0),
        "c_ctx": nrm((D,), 1.0),
        "w_ada": nrm((DEPTH, D, 6 * D), 0.5 * D ** -0.5),
        "b_ada": nrm((DEPTH, 6 * D), 0.01),
        "g_attn_pre": gain((DEPTH, D)),
        "g_attn_post": gain((DEPTH, D)),
        "w_in": nrm((DEPTH, D, IN_WIDTH), D ** -0.5),
        "g_mla_q": gain((DEPTH, MLA_Q_RANK)),
        "w_mla_qb": nrm((DEPTH, MLA_Q_RANK, MLA_HEADS * (MLA_NOPE + MLA_ROPE)), MLA_Q_RANK ** -0.5),
        "g_mla_kv": gain((DEPTH, MLA_KV_RANK)),
        "w_mla_kvb": nrm((DEPTH, MLA_KV_RANK, MLA_HEADS * (MLA_NOPE + MLA_V)), MLA_KV_RANK ** -0.5),
        "lambda_q1": nrm((DEPTH, DIFF_QK), 0.1),
        "lambda_k1": nrm((DEPTH, DIFF_QK), 0.1),
        "lambda_q2": nrm((DEPTH, DIFF_QK), 0.1),
        "lambda_k2": nrm((DEPTH, DIFF_QK), 0.1),
        "g_diff_sub": gain((DEPTH, DIFF_V)),
        "g_gqa_q": gain((DEPTH, GQA_DIM)),
        "g_gqa_k": gain((DEPTH, GQA_DIM)),
        "w_out": nrm((DEPTH, MIX_WIDTH, D), MIX_WIDTH ** -0.5),
        "g_ffn_pre": gain((DEPTH, D)),
        "g_ffn_post": gain((DEPTH, D)),
        "w_ffn_gate": nrm((DEPTH, D, FFN_HIDDEN), D ** -0.5),
        "w_ffn_up": nrm((DEPTH, D, FFN_HIDDEN), D ** -0.5),
        "w_ffn_down": nrm((DEPTH, FFN_HIDDEN, D), FFN_HIDDEN ** -0.5),
    }


def reference(x, c, ctx, c_ctx, w_ada, b_ada, g_attn_pre, g_attn_post, w_in, g_mla_q, w_mla_qb,
              g_mla_kv, w_mla_kvb, lambda_q1, lambda_k1, lambda_q2, lambda_k2, g_diff_sub,
              g_gqa_q, g_gqa_k, w_out, g_ffn_pre, g_ffn_post, w_ffn_gate, w_ffn_up, w_ffn_down):
    n_tok = x.shape[1]
    rows = n_tok // GRID_W
    row = jnp.repeat(jnp.arange(rows, dtype=jnp.int32), GRID_W)
    col = jnp.tile(jnp.arange(GRID_W, dtype=jnp.int32), rows)
    rope_small = _axial_rope_tables(row, col, MLA_ROPE)
    rope_large = _axial_rope_tables(row, col, GQA_DIM)

    silu_c = jax.nn.silu(c)
    silu_cc = jax.nn.silu(c_ctx)
    xc = ctx
    scales = (MLA_SCALE, DIFF_SCALE, GQA_SCALE)

    for l in range(DEPTH):
        last = l == DEPTH - 1
        mod = (silu_c @ w_ada[l] + b_ada[l])[:, None, :]
        mod_c = (silu_cc @ w_ada[l] + b_ada[l])[None, None, :]
        sh_a, sc_a, gt_a, sh_f, sc_f, gt_f = jnp.split(mod, 6, axis=-1)
        csh_a, csc_a, cgt_a, csh_f, csc_f, cgt_f = jnp.split(mod_c, 6, axis=-1)

        lam_init = 0.8 - 0.6 * math.exp(-0.3 * l)
        lam = (jnp.exp(jnp.sum(lambda_q1[l].astype(jnp.float32) * lambda_k1[l].astype(jnp.float32)))
               - jnp.exp(jnp.sum(lambda_q2[l].astype(jnp.float32) * lambda_k2[l].astype(jnp.float32)))
               + lam_init)

        h = _rms(x, g_attn_pre[l]) * (1.0 + sc_a) + sh_a
        hc = _rms(xc, g_attn_pre[l]) * (1.0 + csc_a) + csh_a
        proj_args = (w_in[l], g_mla_q[l], w_mla_qb[l], g_mla_kv[l], w_mla_kvb[l], g_gqa_q[l], g_gqa_k[l])
        lat = _project(h, *proj_args, rope_small, rope_large)
        cxt = _project(hc, *proj_args, None, None)

        o_lat = []
        for (q, k, v), (qc, kc, vc), s in zip(lat, cxt, scales):
            k_all = jnp.concatenate([k, kc], axis=2)
            v_all = jnp.concatenate([v, vc], axis=2)
            o_lat.append(_latent_attention(q, k_all, v_all, s))
        y = _merge(o_lat[0], o_lat[1], o_lat[2], lam, lam_init, g_diff_sub[l]) @ w_out[l]
        x = x + gt_a * _rms(y, g_attn_post[l])

        if not last:
            o_ctx = [_sdpa(qc, kc, vc, s) for (qc, kc, vc), s in zip(cxt, scales)]
            yc = _merge(o_ctx[0], o_ctx[1], o_ctx[2], lam, lam_init, g_diff_sub[l]) @ w_out[l]
            xc = xc + cgt_a * _rms(yc, g_attn_post[l])

        hf = _rms(x, g_ffn_pre[l]) * (1.0 + sc_f) + sh_f
        x = x + gt_f * _rms(_swiglu(hf, w_ffn_gate[l], w_ffn_up[l], w_ffn_down[l]), g_ffn_post[l])
        if not last:
            hfc = _rms(xc, g_ffn_pre[l]) * (1.0 + csc_f) + csh_f
            xc = xc + cgt_f * _rms(_swiglu(hfc, w_ffn_gate[l], w_ffn_up[l], w_ffn_down[l]), g_ffn_post[l])

    return x
```

```python
import bisect
import math
from contextlib import ExitStack

import numpy as np
import concourse.bass as bass
import concourse.mybir as mybir
from concourse.bass_utils import run_bass_kernel_spmd

F32 = mybir.dt.float32
BF16 = mybir.dt.bfloat16
AF = mybir.ActivationFunctionType
ALU = mybir.AluOpType
EPS = 1e-6
NV = 216
SC_MLA = 1.0 / math.sqrt(96.0)
SC_DIFF = 1.0 / math.sqrt(32.0)
SC_GQA = 1.0 / 8.0


class Trk:
    __slots__ = ("w", "r")

    def __init__(self):
        self.w = None
        self.r = []


class Slot:
    pass


class Sched:
    COMPUTE = ("pe", "act", "dve", "pool")

    def __init__(self, nc, es):
        self.nc = nc
        self.es = es
        self.eng = {"pe": nc.tensor, "act": nc.scalar, "dve": nc.vector, "pool": nc.gpsimd, "sp": nc.sync}
        self.sem = {e: es.enter_context(nc.semaphore("s_" + e)) for e in self.COMPUTE}
        self.n = {e: 0 for e in self.eng}
        self.last = {e: None for e in self.eng}
        self.sig = {e: [] for e in self.COMPUTE}
        self.known = {e: {} for e in self.eng}
        self.slots = []
        self.pool = []
        self.pidx = 0
        self.nwait = 0

    def _sigval(self, e, idx):
        s = self.sig[e]
        p = bisect.bisect_left(s, idx)
        if p < len(s):
            return p + 1
        li = self.n[e] - 1
        assert li >= idx
        self.last[e].then_inc(self.sem[e], 1)
        s.append(li)
        return len(s)

    def _wait(self, q, dep):
        if dep[0] == "c":
            _, e, idx = dep
            if e == q and q == "pe":
                return
            key = ("c", e)
            kv = self.known[q].get(key, 0)
            s = self.sig[e]
            p = bisect.bisect_left(s, idx)
            if p < len(s) and p + 1 <= kv:
                return
            v = self._sigval(e, idx)
            if v <= kv:
                return
            self.eng[q].wait_ge(self.sem[e], v)
            self.nwait += 1
            self.known[q][key] = v
        else:
            _, sl, val = dep
            val = sl.cnt
            key = ("d", id(sl))
            if self.known[q].get(key, 0) >= val:
                return
            self.eng[q].wait_ge(sl.sem, val)
            self.nwait += 1
            self.known[q][key] = val

    def _deps(self, q, reads, writes):
        for t in reads:
            if t.w is not None:
                self._wait(q, t.w)
        for t in writes:
            if t.w is not None:
                self._wait(q, t.w)
            for d in t.r:
                self._wait(q, d)

    def op(self, q, fn, reads=(), writes=(), sig=False):
        self._deps(q, reads, writes)
        ins = fn(self.eng[q])
        idx = self.n[q]
        self.n[q] += 1
        self.last[q] = ins
        if sig:
            ins.then_inc(self.sem[q], 1)
            self.sig[q].append(idx)
        me = ("c", q, idx)
        for t in reads:
            t.r.append(me)
        for t in writes:
            t.w = me
            t.r = []
        return ins

    def pslot(self):
        sl = Slot()
        sl.sem = self.es.enter_context(self.nc.semaphore("d%d" % len(self.slots)))
        sl.cnt = 0
        self.slots.append(sl)
        return sl

    def slot(self):
        if self.pidx >= len(self.pool):
            self.pool.append(self.pslot())
        sl = self.pool[self.pidx]
        self.pidx += 1
        return sl

    def phase_reset(self):
        self.barrier()
        self.pidx = 0

    def dma(self, q, sl, out, in_, reads=(), writes=()):
        self._deps(q, reads, writes)
        ins = self.eng[q].dma_start(out=out, in_=in_)
        ins.then_inc(sl.sem, 16)
        sl.cnt += 16
        me = ("d", sl, sl.cnt)
        for t in reads:
            t.r.append(me)
        for t in writes:
            t.w = me
            t.r = []
        return ins

    def barrier(self):
        for q in self.eng:
            for e in self.COMPUTE:
                if e != q and self.n[e] > 0:
                    self._wait(q, ("c", e, self.n[e] - 1))
            for sl in self.slots:
                if sl.cnt > 0:
                    self._wait(q, ("d", sl, sl.cnt))


def build(NG=4, NL=2, dbg=False):
    nc = bass.Bass("TRN2", target_bir_lowering=False)
    NTOK = NG * 512
    NTT = NG * 4
    NKR = NG * 4
    mult, add = ALU.mult, ALU.add

    def din(name, shape, dt=F32):
        return nc.dram_tensor(name, shape, dt, kind="ExternalInput").ap()

    x_d = din("x_own", [NTOK, 1024])
    ctx_d = din("ctx_b", [256, 1024])
    cvec_d = din("cvec", [128, 16])
    wada_d = din("w_ada_r", [NL, 12, 128, 8 * 512])
    colv_d = din("colv", [NL, 128, NV])
    wA_d = din("w_inA", [NL, 128, 8, 1344])
    wQ_d = din("w_inQ", [NL, 128, 8, 1792])
    wqb_d = din("w_qb_r", [NL, 128, 2, 768])
    wkvb_d = din("w_kvb_r", [NL, 128, 512])
    wout_d = din("w_out_r", [NL, 128, 8, 1024])
    wg_d = din("wg_r", [NL, 11, 128, 2048])
    wu_d = din("wu_r", [NL, 11, 128, 2048])
    wd_d = din("wd_r", [NL, 11, 128, 2048])
    rope_d = din("rope", [4, 128, NTOK + 256])
    const_d = din("consts", [128, 384])
    out_d = nc.dram_tensor("out", [NTOK, 1024], F32, kind="ExternalOutput").ap()

    KB = [0, 128, 288, 416, 544, 672]
    k_loc = [[nc.dram_tensor("k_loc%d_%d" % (l, s_), [KB[s_ + 1] - KB[s_], NTOK], BF16) for s_ in range(5)] for l in range(NL)]
    k_all = [[nc.dram_tensor("k_all%d_%d" % (l, s_), [4 * (KB[s_ + 1] - KB[s_]), NTOK], BF16) for s_ in range(5)] for l in range(NL)]
    v_loc = [[nc.dram_tensor("v_loc%d_%d" % (l, s_), [256, NKR * 66], BF16) for s_ in range(5)] for l in range(NL)]
    v_all = [[nc.dram_tensor("v_all%d_%d" % (l, s_), [4 * 256, NKR * 66], BF16) for s_ in range(5)] for l in range(NL)]

    wb = []
    for l in range(NL):
        wb.append(dict(
            wA=nc.dram_tensor("wA_b%d" % l, [128, 8 * 1344], BF16), wkvb=nc.dram_tensor("wkvb_b%d" % l, [128, 512], BF16),
            wQ=nc.dram_tensor("wQ_b%d" % l, [128, 8 * 1792], BF16), wqb=nc.dram_tensor("wqb_b%d" % l, [128, 2 * 768], BF16),
            wout=nc.dram_tensor("wout_b%d" % l, [128, 8 * 1024], BF16),
            wg=nc.dram_tensor("wg_b%d" % l, [11, 128, 2048], BF16), wu=nc.dram_tensor("wu_b%d" % l, [11, 128, 2048], BF16),
            wd=nc.dram_tensor("wd_b%d" % l, [11, 128, 2048], BF16)))

    def kblk(r0):
        for s_ in range(5):
            if KB[s_] <= r0 < KB[s_ + 1]:
                return s_, r0 - KB[s_], KB[s_ + 1] - KB[s_]
        raise ValueError
    kc_loc = [nc.dram_tensor("kc_loc%d" % l, [672, 256], BF16) for l in range(NL)]
    vc_loc = [nc.dram_tensor("vc_loc%d" % l, [1280, 2 * 66], BF16) for l in range(NL)]

    with ExitStack() as es:
        S = Sched(nc, es)

        uid = [0]

        def sb(name, shape, dt, stack=None):
            uid[0] += 1
            return (stack or es).enter_context(nc.sbuf_tensor("sb%d_%s" % (uid[0], name), shape, dt))

        xs = sb("xs", [128, NTT + 2, 1024], F32)
        T_x = [Trk() for _ in range(NTT + 2)]
        cst = sb("cst", [128, 384], F32)
        T_c = Trk()
        ident = cst[:, 0:128]
        onesf = cst[:, 128:256]
        bd64 = cst[:, 256:384]
        colv = sb("colv", [128, NL, NV], F32)
        T_colv = Trk()
        modT = sb("modT", [128, NL * 96], F32)
        T_mod = Trk()
        misc = sb("misc", [128, 64], F32)
        T_misc = Trk()
        A_t = sb("A_t", [128, 2, 2, 8], F32)
        Gc_t = sb("Gc_t", [128, 2, 2, 8], F32)
        T_A = Trk()
        QT = [sb("QT%d" % i, [128, 512], BF16) for i in range(20)]
        T_QT = [Trk() for _ in range(20)]
        yT = sb("yT", [128, 8, 512], BF16)
        T_yT = [Trk() for _ in range(16)]

        psd = [es.enter_context(nc.psum_tensor("psd%d" % i, [128, 1024], F32)) for i in range(4)]
        ps = [psd[i // 2][:, (i % 2) * 512:(i % 2 + 1) * 512] for i in range(8)]
        T_ps = [Trk() for _ in range(8)]
        bank_rr = [0]

        def nb(lo=0, hi=8):
            i = lo + (bank_rr[0] % (hi - lo))
            bank_rr[0] += 1
            return ps[i], T_ps[i]

        dbg_done = set()

        def dump(name, src_ap, shape, dt):
            if not dbg or name in dbg_done:
                return
            dbg_done.add(name)
            d = nc.dram_tensor("dbg_" + name, shape, dt, kind="ExternalOutput").ap()
            S.phase_reset()
            sl = S.slot()
            S.dma("sp", sl, d, src_ap)
            S.barrier()

        sl_misc = S.pslot()
        sl_x = S.pslot()
        sl_out = S.pslot()

        def mcol(l, c, j):
            o = (l * 48 + c) * 2 + j
            return modT[:, o:o + 1]

        S.dma("sp", sl_misc, cst[:], const_d, writes=[T_c])
        S.dma("sp", sl_misc, colv[:], colv_d.rearrange("l p n -> p l n"), writes=[T_colv])
        for g4 in range(NG):
            S.dma("sp", sl_x, xs[:, g4 * 4:(g4 + 1) * 4, :], x_d[g4 * 512:(g4 + 1) * 512, :].rearrange("(t p) d -> p t d", p=128),
                  writes=T_x[g4 * 4:(g4 + 1) * 4])
        S.dma("sp", sl_x, xs[:, NTT:NTT + 2, :], ctx_d.rearrange("(t p) d -> p t d", p=128), writes=T_x[NTT:NTT + 2])
        S.op("pool", lambda e: e.memset(misc[:, 0:1], -0.5), writes=[T_misc])
        for i in range(4, 20):
            S.op("pool", lambda e: e.memset(QT[i][:], 0.0), writes=[T_QT[i]])
        mhalf = misc[:, 0:1]

        T_wb = [dict((k_, Trk()) for k_ in wb[l]) for l in range(NL)]
        sl_wb = [dict((k_, S.pslot()) for k_ in wb[l]) for l in range(NL)]
        def convert(l, names):
            def cv(name, dst, src, extra=()):
                S.dma("pool", sl_wb[l][name], dst, src, reads=list(extra), writes=[T_wb[l][name]])
            for name in names:
                if name == "wA":
                    for k in range(8):
                        cv("wA", wb[l]["wA"].ap()[:, k * 1344:(k + 1) * 1344], wA_d[l, :, k, :])
                elif name == "wkvb":
                    cv("wkvb", wb[l]["wkvb"].ap(), wkvb_d[l])
                elif name == "wQ":
                    for k in range(8):
                        cv("wQ", wb[l]["wQ"].ap()[:, k * 1792:(k + 1) * 1792], wQ_d[l, :, k, :])
                elif name == "wqb":
                    for c in range(2):
                        cv("wqb", wb[l]["wqb"].ap()[:, c * 768:(c + 1) * 768], wqb_d[l, :, c, :])
                elif name == "wout":
                    for k in range(8):
                        cv("wout", wb[l]["wout"].ap()[:, k * 1024:(k + 1) * 1024], wout_d[l, :, k, :])
                elif name.startswith("ffn"):
                    j0, j1 = {"ffn": (0, 11), "ffn0": (0, 4), "ffn1": (4, 8), "ffn2": (8, 11)}[name]
                    for jj in range(j0, j1):
                        cv("wg", wb[l]["wg"].ap()[jj], wg_d[l, jj])
                        cv("wu", wb[l]["wu"].ap()[jj], wu_d[l, jj])
                    for jj in range(j0, j1):
                        cv("wd", wb[l]["wd"].ap()[jj], wd_d[l, jj])

        convert(0, ["wA", "wkvb"])

        def phase_M(l, after=None):
            S.phase_reset()
            with ExitStack() as ph:
                scv = sb("scv", [128, 16], F32, ph)
                T_scv = Trk()
                wad = [sb("wad%d" % i, [128, 8, 512], F32, ph) for i in range(3)]
                T_wad = [Trk(), Trk(), Trk()]
                sl_wad = [S.slot(), S.slot(), S.slot()]
                sl_c = S.slot()
                S.dma("sp", sl_c, scv[:], cvec_d, writes=[T_scv])
                S.op("act", lambda e: e.activation(out=scv[:], in_=scv[:], func=AF.Silu), reads=[T_scv], writes=[T_scv])
                modrow = sb("modrow", [2, 6144], F32, ph)
                T_mr = Trk()
                for cb in range(12):
                    bi = cb % 3
                    S.dma("sp", sl_wad[bi], wad[bi][:], wada_d[l, cb].rearrange("p (k n) -> p k n", k=8), writes=[T_wad[bi]])
                    if cb == 11 and after is not None:
                        after(T_wad[bi])
                    pr, tpr = nb()
                    for k in range(8):
                        S.op("pe", lambda e: e.matmul(pr[0:2, :], scv[:, 2 * k:2 * k + 2], wad[bi][:, k, :], start=(k == 0), stop=(k == 7)),
                             reads=[T_wad[bi], T_scv], writes=[tpr])
                    S.op("dve", lambda e: e.tensor_copy(out=modrow[0:2, cb * 512:(cb + 1) * 512], in_=pr[0:2, :]), reads=[tpr], writes=[T_mr])
                pm, tpm = nb()
                for c in range(48):
                    S.op("pe", lambda e: e.matmul(pm[:, 2 * c:2 * c + 2], modrow[0:2, c * 128:(c + 1) * 128], ident[0:2, 0:2], start=True, stop=True),
                         reads=[T_mr, T_c], writes=[tpm])
                for j in range(2):
                    S.op("dve", lambda e: e.tensor_tensor(
                        out=modT[:, l * 96:(l + 1) * 96].rearrange("p (c j) -> p c j", j=2)[:, :, j],
                        in0=pm[:, 0:96].rearrange("p (c j) -> p c j", j=2)[:, :, j],
                        in1=colv[:, l, 32:80], op=add), reads=[tpm, T_colv], writes=[T_mod])
                S.barrier()

        def rest_conversions(t_last):
            S._deps("pool", [t_last], [])
            convert(0, ["wQ", "wqb"])

        phase_M(0, after=rest_conversions)

        def layer_consts(l):
            lam_init = 0.8 - 0.6 * math.exp(-0.3 * l)
            for w in range(2):
                for j in range(2):
                    sc0 = (l * 48 + (1 + 3 * w) * 8) * 2
                    gt0 = (l * 48 + (2 + 3 * w) * 8) * 2
                    sc_v = modT[:, sc0:sc0 + 16].rearrange("p (k j) -> p k j", j=2)[:, :, j]
                    gt_v = modT[:, gt0:gt0 + 16].rearrange("p (k j) -> p k j", j=2)[:, :, j]
                    gpre = colv[:, l, 16 * w:16 * w + 8]
                    gpost = colv[:, l, 16 * w + 8:16 * w + 16]
                    S.op("dve", lambda e: e.scalar_tensor_tensor(out=A_t[:, w, j, :], in0=sc_v, scalar=1.0, in1=gpre,
                                                                  op0=add, op1=mult), reads=[T_mod, T_colv], writes=[T_A])
                    S.op("dve", lambda e: e.tensor_tensor(out=Gc_t[:, w, j, :], in0=gt_v, in1=gpost, op=mult),
                         reads=[T_mod, T_colv], writes=[T_A])
            S.op("dve", lambda e: e.tensor_tensor(out=misc[:, 8:40], in0=colv[:, l, 88:120], in1=colv[:, l, 120:152], op=mult),
                 reads=[T_colv, T_misc], writes=[T_misc])
            S.op("dve", lambda e: e.reduce_sum(out=misc[:, 4:5], in_=misc[:, 8:40], axis=mybir.AxisListType.X),
                 reads=[T_misc], writes=[T_misc])
            S.op("dve", lambda e: e.tensor_tensor(out=misc[:, 8:40], in0=colv[:, l, 152:184], in1=colv[:, l, 184:216], op=mult),
                 reads=[T_colv, T_misc], writes=[T_misc])
            S.op("dve", lambda e: e.reduce_sum(out=misc[:, 5:6], in_=misc[:, 8:40], axis=mybir.AxisListType.X),
                 reads=[T_misc], writes=[T_misc])
            S.op("act", lambda e: e.activation(out=misc[:, 4:6], in_=misc[:, 4:6], func=AF.Exp), reads=[T_misc], writes=[T_misc])
            S.op("dve", lambda e: e.scalar_tensor_tensor(out=misc[:, 1:2], in0=misc[:, 5:6], scalar=-lam_init, in1=misc[:, 4:5],
                                                          op0=add, op1=ALU.subtract), reads=[T_misc], writes=[T_misc])
            S.op("dve", lambda e: e.tensor_scalar(out=misc[:, 2:3], in0=colv[:, l, 87:88], scalar1=(1.0 - lam_init), scalar2=None,
                                                  op0=mult), reads=[T_colv, T_misc], writes=[T_misc])

        class G:
            pass

        groups = []
        for g in range(NG):
            gr = G()
            gr.tts = [g * 4 + i for i in range(4)]
            gr.T = 512
            gr.j = 0
            gr.tab0 = g * 512
            gr.g = g
            gr.ctx = False
            groups.append(gr)
        cg = G()
        cg.tts = [NTT, NTT + 1]
        cg.T = 256
        cg.j = 1
        cg.tab0 = NTOK
        cg.g = None
        cg.ctx = True

        def prep_hT(ph_t, l, gr, w):
            hT, T_hT, xn, T_xn, junk, T_junk, st, T_st = ph_t
            for i, tt in enumerate(gr.tts):
                b = i % 2
                s_ = st[i % 4]
                ts_ = T_st[i % 4]
                S.op("act", lambda e: e.activation(out=junk[:], in_=xs[:, tt, :], func=AF.Square, accum_out=s_[:, 0:1]),
                     reads=[T_x[tt]], writes=[T_junk, ts_])
                S.op("act", lambda e: e.activation(out=s_[:, 1:2], in_=s_[:, 0:1], func=AF.Ln, scale=1.0 / 1024, bias=EPS),
                     reads=[ts_], writes=[ts_])
                S.op("act", lambda e: e.activation(out=s_[:, 2:3], in_=s_[:, 1:2], func=AF.Exp, scale=-0.5),
                     reads=[ts_], writes=[ts_])
                S.op("dve", lambda e: e.tensor_scalar(out=xn[b][:], in0=xs[:, tt, :], scalar1=s_[:, 2:3], scalar2=None, op0=mult),
                     reads=[T_x[tt], ts_], writes=[T_xn[b]])
                for half in range(2):
                    pt, tpt = nb()
                    for q in range(4):
                        k = half * 4 + q
                        S.op("pe", lambda e: e.transpose(pt[:, q * 128:(q + 1) * 128], xn[b][:, k * 128:(k + 1) * 128], ident),
                             reads=[T_xn[b], T_c], writes=[tpt])
                    for q in range(4):
                        k = half * 4 + q
                        acol = A_t[:, w, gr.j, k:k + 1]
                        bcol = mcol(l, (3 * w) * 8 + k, gr.j)
                        o_ = hT[:, k, i * 128:(i + 1) * 128]
                        i_ = pt[:, q * 128:(q + 1) * 128]
                        if q % 2 == 0:
                            S.op("act", lambda e: e.activation(out=o_, in_=i_, func=AF.Identity, scale=acol, bias=bcol),
                                 reads=[tpt, T_A, T_mod], writes=[T_hT])
                        else:
                            S.op("dve", lambda e: e.tensor_scalar(out=o_, in0=i_, scalar1=acol, scalar2=bcol, op0=mult, op1=add),
                                 reads=[tpt, T_A, T_mod], writes=[T_hT])

        def alloc_prep(ph):
            hT = sb("hT", [128, 8, 512], BF16, ph)
            xn = [sb("xn%d" % i, [128, 1024], F32, ph) for i in range(2)]
            junk = sb("junk", [128, 1024], BF16, ph)
            st = [sb("st%d" % i, [128, 4], F32, ph) for i in range(4)]
            return (hT, Trk(), xn, [Trk(), Trk()], junk, Trk(), st, [Trk() for _ in range(4)])

        def proj_block(W, T_W, c0, wd, hT, T_hT, T):
            pb, tpb = nb()
            for k in range(8):
                S.op("pe", lambda e: e.matmul(pb[0:wd, 0:T], W[:, k, c0:c0 + wd], hT[:, k, 0:T], start=(k == 0), stop=(k == 7)),
                     reads=[T_W, T_hT], writes=[tpb])
            return pb, tpb

        class Tmp:
            def __init__(self, ph, n, name, dt=F32):
                self.t = [sb("%s%d" % (name, i), [128, 512], dt, ph) for i in range(n)]
                self.k = [Trk() for _ in range(n)]
                self.i = 0

            def get(self):
                i = self.i % len(self.t)
                self.i += 1
                return self.t[i], self.k[i]

        def rstd_bcast(tmp, src_list, lhs, n_feat, T, rows=128):
            pst, tpst = nb()
            for i, (p_, t_) in enumerate(src_list):
                sq, tsq = tmp.get()
                S.op("act", lambda e: e.activation(out=sq[0:rows, 0:T], in_=p_[0:rows, 0:T], func=AF.Square), reads=[t_], writes=[tsq])
                S.op("pe", lambda e: e.matmul(pst[0:rows, 0:T], lhs[0:rows, 0:rows], sq[0:rows, 0:T], start=(i == 0), stop=(i == len(src_list) - 1)),
                     reads=[tsq, T_c], writes=[tpst])
            r, tr = tmp.get()
            S.op("act", lambda e: e.activation(out=r[0:rows, 0:T], in_=pst[0:rows, 0:T], func=AF.Ln, scale=1.0 / n_feat, bias=EPS),
                 reads=[tpst], writes=[tr])
            S.op("act", lambda e: e.activation(out=r[0:rows, 0:T], in_=r[0:rows, 0:T], func=AF.Exp, scale=-0.5), reads=[tr], writes=[tr])
            return r, tr

        def rope_comb(tmp, out_ap, t_out, a, ta, b_, tb, Ct, St, T_tab, r0, r1, T):
            t1, tt1 = tmp.get()
            t2, tt2 = tmp.get()
            S.op("dve", lambda e: e.tensor_tensor(out=t1[r0:r1, 0:T], in0=a[r0:r1, 0:T], in1=Ct[r0:r1, 0:T], op=mult),
                 reads=[ta, T_tab], writes=[tt1])
            S.op("dve", lambda e: e.tensor_tensor(out=t2[r0:r1, 0:T], in0=b_[r0:r1, 0:T], in1=St[r0:r1, 0:T], op=mult),
                 reads=[tb, T_tab], writes=[tt2])
            S.op("dve", lambda e: e.tensor_tensor(out=out_ap, in0=t1[r0:r1, 0:T], in1=t2[r0:r1, 0:T], op=add),
                 reads=[tt1, tt2], writes=[t_out])

        def load_tabs(tabs, T_tab, sl_tab, gr):
            for ti in range(4):
                S.dma("sp", sl_tab, tabs[ti][:, 0:gr.T], rope_d[ti, :, gr.tab0:gr.tab0 + gr.T], writes=[T_tab])

        def gqa_block(tmp, pb, tpb, pbs, tpbs, gcol, gscol, tabs, T_tab, out_ap, t_out, T):
            r, tr = rstd_bcast(tmp, [(pb, tpb)], bd64, 64.0, T)
            xg, txg = tmp.get()
            xw, txw = tmp.get()
            S.op("dve", lambda e: e.scalar_tensor_tensor(out=xg[:, 0:T], in0=pb[:, 0:T], scalar=gcol, in1=r[:, 0:T], op0=mult, op1=mult),
                 reads=[tpb, tr, T_colv], writes=[txg])
            S.op("dve", lambda e: e.scalar_tensor_tensor(out=xw[:, 0:T], in0=pbs[:, 0:T], scalar=gscol, in1=r[:, 0:T], op0=mult, op1=mult),
                 reads=[tpbs, tr, T_colv], writes=[txw])
            rope_comb(tmp, out_ap, t_out, xg, txg, xw, txw, tabs[2], tabs[3], T_tab, 0, 128, T)

        def phase_A(l, glist):
            S.phase_reset()
            with ExitStack() as ph:
                prep = alloc_prep(ph)
                hT, T_hT = prep[0], prep[1]
                wA = sb("wA", [128, 8, 1344], BF16, ph)
                T_wA = Trk()
                wkvb = sb("wkvb", [128, 512], BF16, ph)
                T_wkvb = Trk()
                sl_w = S.slot()
                S.dma("sp", sl_w, wA[:], wb[l]["wA"].ap().rearrange("p (k n) -> p k n", k=8), reads=[T_wb[l]["wA"]], writes=[T_wA])
                S.dma("sp", sl_w, wkvb[:], wb[l]["wkvb"].ap(), reads=[T_wb[l]["wkvb"]], writes=[T_wkvb])
                tabs = [sb("tab%d" % i, [128, 512], F32, ph) for i in range(4)]
                T_tab = Trk()
                sl_tab = S.slot()
                tmp = Tmp(ph, 8, "tmpA")
                ckvn = sb("ckvn", [128, 512], BF16, ph)
                T_ckvn = Trk()
                kst = Tmp(ph, 3, "kst", BF16)
                sl_kst = [S.slot() for _ in range(3)]
                vst = sb("vst", [128, 10, 4, 66], BF16, ph)
                T_vst = Trk()
                sl_vst = S.slot()
                S.op("dve", lambda e: e.memset(vst[:, :, :, 64:65], 1.0), writes=[T_vst])
                S.op("dve", lambda e: e.memset(vst[:, :, :, 65:66], 0.0), writes=[T_vst])
                T_dram = Trk()

                def kout(tile_, ttile, idx, r0, nrows, gr, prow0=0):
                    if gr.ctx:
                        dst = kc_loc[l].ap()[r0:r0 + nrows, 0:256]
                    else:
                        s_, off, _ = kblk(r0)
                        dst = k_loc[l][s_].ap()[off:off + nrows, gr.g * 512:(gr.g + 1) * 512]
                    S.dma("sp", sl_kst[idx], dst, tile_[prow0:prow0 + nrows, 0:gr.T], reads=[ttile], writes=[])

                for gr in glist:
                    T = gr.T
                    nt = len(gr.tts)
                    prep_hT(prep, l, gr, 0)
                    load_tabs(tabs, T_tab, sl_tab, gr)
                    pb, tpb = proj_block(wA, T_wA, 0, 128, hT, T_hT, T)
                    r, tr = rstd_bcast(tmp, [(pb, tpb)], onesf, 128.0, T)
                    S.op("dve", lambda e: e.scalar_tensor_tensor(out=ckvn[:, 0:T], in0=pb[:, 0:T], scalar=colv[:, l, 82:83], in1=r[:, 0:T],
                                                                  op0=mult, op1=mult), reads=[tpb, tr, T_colv], writes=[T_ckvn])
                    for hp in range(2):
                        pk, tpk = nb()
                        S.op("pe", lambda e: e.matmul(pk[:, 0:T], wkvb[:, hp * 128:(hp + 1) * 128], ckvn[:, 0:T], start=True, stop=True),
                             reads=[T_wkvb, T_ckvn], writes=[tpk])
                        idx = kst.i % 3
                        ks, tks = kst.get()
                        S.op("act", lambda e: e.activation(out=ks[:, 0:T], in_=pk[:, 0:T], func=AF.Copy), reads=[tpk], writes=[tks])
                        kout(ks, tks, idx, hp * 128, 128, gr)
                    for i in range(nt):
                        pv, tpv = nb()
                        S.op("pe", lambda e: e.matmul(pv[:, 0:256], ckvn[:, i * 128:(i + 1) * 128], wkvb[:, 256:512], start=True, stop=True),
                             reads=[T_wkvb, T_ckvn], writes=[tpv])
                        S.op("dve", lambda e: e.tensor_copy(out=vst[:, 0:4, i, 0:64], in_=pv[:, 0:256].rearrange("p (u d) -> p u d", d=64)),
                             reads=[tpv], writes=[T_vst])
                    pb, tpb = proj_block(wA, T_wA, 128, 32, hT, T_hT, T)
                    pbs, tpbs = proj_block(wA, T_wA, 160, 32, hT, T_hT, T)
                    idx = kst.i % 3
                    ks, tks = kst.get()
                    rope_comb(tmp, ks[0:32, 0:T], tks, pb, tpb, pbs, tpbs, tabs[0], tabs[1], T_tab, 0, 32, T)
                    kout(ks, tks, idx, 256, 32, gr)
                    for b in range(2):
                        pb, tpb = proj_block(wA, T_wA, 192 + b * 256, 128, hT, T_hT, T)
                        pbs, tpbs = proj_block(wA, T_wA, 320 + b * 256, 128, hT, T_hT, T)
                        idx = kst.i % 3
                        ks, tks = kst.get()
                        rope_comb(tmp, ks[:, 0:T], tks, pb, tpb, pbs, tpbs, tabs[0], tabs[1], T_tab, 0, 128, T)
                        kout(ks, tks, idx, 288 + b * 128, 128, gr)
                    pb, tpb = proj_block(wA, T_wA, 704, 128, hT, T_hT, T)
                    pbs, tpbs = proj_block(wA, T_wA, 832, 128, hT, T_hT, T)
                    idx = kst.i % 3
                    ks, tks = kst.get()
                    gqa_block(tmp, pb, tpb, pbs, tpbs, colv[:, l, 85:86], colv[:, l, 86:87], tabs, T_tab, ks[:, 0:T], tks, T)
                    kout(ks, tks, idx, 544, 128, gr)
                    for i in range(nt):
                        pv, tpv = nb()
                        for k in range(8):
                            S.op("pe", lambda e: e.matmul(pv[:, 0:384], hT[:, k, i * 128:(i + 1) * 128], wA[:, k, 960:1344],
                                                         start=(k == 0), stop=(k == 7)), reads=[T_wA, T_hT], writes=[tpv])
                        S.op("dve", lambda e: e.tensor_copy(out=vst[:, 4:10, i, 0:64], in_=pv[:, 0:384].rearrange("p (u d) -> p u d", d=64)),
                             reads=[tpv], writes=[T_vst])
                    if gr.ctx:
                        dst = vc_loc[l].ap().rearrange("(u p) (t c) -> p u t c", p=128, c=66)
                        S.dma("sp", sl_vst, dst, vst[:, :, 0:2, :], reads=[T_vst])
                    else:
                        for s_ in range(5):
                            dst = v_loc[l][s_].ap().rearrange("(u p) (t c) -> p u t c", p=128, c=66)[:, :, gr.g * 4:(gr.g + 1) * 4, :]
                            S.dma("sp", sl_vst, dst, vst[:, 2 * s_:2 * s_ + 2, :, :], reads=[T_vst])
                S.barrier()

        cc_sem = es.enter_context(nc.semaphore("cc"))
        cc_cnt = [0]

        def gather(l):
            S.barrier()
            for (a, b_) in list(zip(k_loc[l], k_all[l])) + list(zip(v_loc[l], v_all[l])):
                nc.gpsimd.collective_compute("AllGather", ALU.bypass, replica_groups=[[0, 1, 2, 3], [4, 5, 6, 7]],
                                             ins=[a.ap().opt()], outs=[b_.ap().opt()]).then_inc(cc_sem, 1)
                cc_cnt[0] += 1

        def gather_wait():
            for q in S.eng:
                S.eng[q].wait_ge(cc_sem, cc_cnt[0])

        def phase_B1(l, gr):
            S.phase_reset()
            with ExitStack() as ph:
                prep = alloc_prep(ph)
                hT, T_hT = prep[0], prep[1]
                wQ = sb("wQ", [128, 8, 1792], BF16, ph)
                T_wQ = Trk()
                wqb = sb("wqb", [128, 2, 768], BF16, ph)
                T_wqb = Trk()
                sl_w = S.slot()
                S.dma("sp", sl_w, wQ[:], wb[l]["wQ"].ap().rearrange("p (k n) -> p k n", k=8), reads=[T_wb[l]["wQ"]], writes=[T_wQ])
                S.dma("sp", sl_w, wqb[:], wb[l]["wqb"].ap().rearrange("p (c n) -> p c n", c=2), reads=[T_wb[l]["wqb"]], writes=[T_wqb])
                tabs = [sb("tab%d" % i, [128, 512], F32, ph) for i in range(4)]
                T_tab = Trk()
                sl_tab = S.slot()
                tmp = Tmp(ph, 8, "tmpB")
                qan = sb("qan", [128, 2, 512], BF16, ph)
                T_qan = Trk()
                qfull = Tmp(ph, 2, "qfull", BF16)
                T = gr.T
                prep_hT(prep, l, gr, 0)
                load_tabs(tabs, T_tab, sl_tab, gr)
                p0, tp0 = proj_block(wQ, T_wQ, 0, 128, hT, T_hT, T)
                p1, tp1 = proj_block(wQ, T_wQ, 128, 128, hT, T_hT, T)
                r, tr = rstd_bcast(tmp, [(p0, tp0), (p1, tp1)], onesf, 256.0, T)
                for c, (p_, t_) in enumerate(((p0, tp0), (p1, tp1))):
                    S.op("dve", lambda e: e.scalar_tensor_tensor(out=qan[:, c, 0:T], in0=p_[:, 0:T], scalar=colv[:, l, 80 + c:81 + c], in1=r[:, 0:T],
                                                                  op0=mult, op1=mult), reads=[t_, tr, T_colv], writes=[T_qan])
                for h in range(4):
                    pb, tpb = nb()
                    pbs, tpbs = nb()
                    for c in range(2):
                        S.op("pe", lambda e: e.matmul(pb[0:96, 0:T], wqb[:, c, h * 192:h * 192 + 96], qan[:, c, 0:T], start=(c == 0), stop=(c == 1)),
                             reads=[T_wqb, T_qan], writes=[tpb])
                    for c in range(2):
                        S.op("pe", lambda e: e.matmul(pbs[0:96, 0:T], wqb[:, c, h * 192 + 96:h * 192 + 192], qan[:, c, 0:T], start=(c == 0), stop=(c == 1)),
                             reads=[T_wqb, T_qan], writes=[tpbs])
                    S.op("act", lambda e: e.activation(out=QT[h][0:64, 0:T], in_=pb[0:64, 0:T], func=AF.Copy), reads=[tpb], writes=[T_QT[h]])
                    rope_comb(tmp, QT[h][64:96, 0:T], T_QT[h], pb, tpb, pbs, tpbs, tabs[0], tabs[1], T_tab, 64, 96, T)
                for b in range(2):
                    pb, tpb = proj_block(wQ, T_wQ, 256 + b * 256, 128, hT, T_hT, T)
                    pbs, tpbs = proj_block(wQ, T_wQ, 384 + b * 256, 128, hT, T_hT, T)
                    qf, tqf = qfull.get()
                    rope_comb(tmp, qf[:, 0:T], tqf, pb, tpb, pbs, tpbs, tabs[0], tabs[1], T_tab, 0, 128, T)
                    for j in range(4):
                        m = b * 4 + j
                        S.op("pool", lambda e: e.tensor_copy(out=QT[4 + m][j * 32:(j + 1) * 32, 0:T], in_=qf[j * 32:(j + 1) * 32, 0:T]),
                             reads=[tqf], writes=[T_QT[4 + m]])
                for i in range(4):
                    pb, tpb = proj_block(wQ, T_wQ, 768 + i * 256, 128, hT, T_hT, T)
                    pbs, tpbs = proj_block(wQ, T_wQ, 896 + i * 256, 128, hT, T_hT, T)
                    qf, tqf = qfull.get()
                    gqa_block(tmp, pb, tpb, pbs, tpbs, colv[:, l, 83:84], colv[:, l, 84:85], tabs, T_tab, qf[:, 0:T], tqf, T)
                    for j in range(2):
                        hq = j * 4 + i
                        S.op("pool", lambda e: e.tensor_copy(out=QT[12 + hq][j * 64:(j + 1) * 64, 0:T], in_=qf[j * 64:(j + 1) * 64, 0:T]),
                             reads=[tqf], writes=[T_QT[12 + hq]])
                S.barrier()

        def phase_B2(l, gr):
            S.phase_reset()
            with ExitStack() as ph:
                NSL = 4
                kslot = [sb("kslot%d" % i, [128, NTOK], BF16, ph) for i in range(NSL)]
                vslot = [sb("vslot%d" % i, [128, NKR, 128], BF16, ph) for i in range(NSL)]
                T_ks = [Trk() for _ in range(NSL)]
                T_vs = [Trk() for _ in range(NSL)]
                sl_kv = [S.slot() for _ in range(NSL)]
                for i in range(NSL):
                    S.op("pool", lambda e: e.memset(vslot[i][:, :, 64:128], 0.0), writes=[T_vs[i]])
                Pt = [sb("Pt%d" % i, [128, 2, 512], BF16, ph) for i in range(3)]
                T_P = [Trk() for _ in range(3)]
                osb = Tmp(ph, 3, "osb")
                rec = Tmp(ph, 2, "rec")
                on_ = Tmp(ph, 3, "on")
                dtm = Tmp(ph, 3, "dtm")
                T = gr.T
                ring = [0]
                tcount = [0]
                obank = [0]
                pending = []
                chunks = [("ctx",)] if gr.ctx else [("r", 0), ("r", 1), ("r", 2), ("r", 3), ("ctx",)]

                units = []
                for h in range(4):
                    units.append(dict(q=h, pb=0, d=96, kr=[(h * 64, 64, 0), (256, 32, 64)], vu=h, sc=SC_MLA, kind="plain", u=h))
                for m in range(8):
                    units.append(dict(q=4 + m, pb=0, d=128, kr=[(288 + (m // 4) * 128, 128, 0)], vu=4 + m // 2, sc=SC_DIFF,
                                      kind="diff%d" % (m % 2), u=4 + m // 2))
                for hq in range(8):
                    kvh = hq // 4
                    units.append(dict(q=12 + hq, pb=0, d=128, kr=[(544, 128, 0)], vu=8 + kvh, sc=SC_GQA, kind="plain", u=8 + hq))

                def load_chunk(un, ch):
                    si = ring[0] % NSL
                    ring[0] += 1
                    if ch[0] == "r":
                        rk = ch[1]
                        for (r0, nr, p0) in un["kr"]:
                            s_, off, bn = kblk(r0)
                            S.dma("sp", sl_kv[si], kslot[si][p0:p0 + nr, 0:NTOK], k_all[l][s_].ap()[rk * bn + off:rk * bn + off + nr, :], writes=[T_ks[si]])
                        vu = un["vu"]
                        vsrc = v_all[l][vu // 2].ap()[rk * 256 + (vu % 2) * 128:rk * 256 + (vu % 2 + 1) * 128, :].rearrange("p (t c) -> p t c", c=66)
                        S.dma("sp", sl_kv[si], vslot[si][:, 0:NKR, 0:66], vsrc, writes=[T_vs[si]])
                        return si, NKR
                    else:
                        for (r0, nr, p0) in un["kr"]:
                            S.dma("sp", sl_kv[si], kslot[si][p0:p0 + nr, 0:256], kc_loc[l].ap()[r0:r0 + nr, :], writes=[T_ks[si]])
                        vsrc = vc_loc[l].ap()[un["vu"] * 128:(un["vu"] + 1) * 128, :].rearrange("p (t c) -> p t c", c=66)
                        S.dma("sp", sl_kv[si], vslot[si][:, 0:2, 0:66], vsrc, writes=[T_vs[si]])
                        return si, 2

                if l == 0 and gr.g is not None and NL > 1:
                    if gr.g == 0:
                        convert(0, ["wout", "ffn"])
                    if NG >= 4:
                        sched = {1: ["wA", "wkvb", "wQ", "wqb", "wout"], 2: ["ffn0", "ffn1"], 3: ["ffn2"]}.get(gr.g, [])
                    else:
                        sched = ["wA", "wkvb", "wQ", "wqb", "wout", "ffn"] if gr.g == 0 else []
                    convert(1, sched)
                dstate = {}

                def finalize(un, po, tpo):
                    u = un["u"]
                    ob, tob = osb.get()
                    rc, trc = rec.get()
                    S.op("dve", lambda e: e.reciprocal(out=rc[64:65, 0:T], in_=po[64:65, 0:T]), reads=[tpo], writes=[trc])
                    S.op("act", lambda e: e.activation(out=ob[0:64, 0:T], in_=po[0:64, 0:T], func=AF.Copy), reads=[tpo], writes=[tob])
                    pbc, tpbc = po, tpo
                    S.op("pe", lambda e: e.matmul(pbc[0:64, 0:T], onesf[64:65, 0:64], rc[64:65, 0:T], start=True, stop=True),
                         reads=[trc, T_c], writes=[tpbc])
                    hp = (u % 2) * 64
                    if un["kind"] == "plain":
                        S.op("dve", lambda e: e.tensor_tensor(out=yT[hp:hp + 64, u // 2, 0:T], in0=ob[0:64, 0:T], in1=pbc[0:64, 0:T], op=mult),
                             reads=[tob, tpbc], writes=[T_yT[u]])
                        return
                    o_n, ton = on_.get()
                    S.op("dve", lambda e: e.tensor_tensor(out=o_n[0:64, 0:T], in0=ob[0:64, 0:T], in1=pbc[0:64, 0:T], op=mult),
                         reads=[tob, tpbc], writes=[ton])
                    if un["kind"] == "diff0":
                        dstate[u] = (o_n, ton)
                        return
                    o1, to1 = dstate.pop(u)
                    dd, tdd = dtm.get()
                    S.op("dve", lambda e: e.scalar_tensor_tensor(out=dd[0:64, 0:T], in0=o_n[0:64, 0:T], scalar=misc[0:64, 1:2], in1=o1[0:64, 0:T],
                                                                  op0=mult, op1=add), reads=[ton, to1, T_misc], writes=[tdd])
                    sq, tsq = dtm.get()
                    S.op("act", lambda e: e.activation(out=sq[0:64, 0:T], in_=dd[0:64, 0:T], func=AF.Square), reads=[tdd], writes=[tsq])
                    pss, tpss = po, tpo
                    S.op("pe", lambda e: e.matmul(pss[0:64, 0:T], onesf[0:64, 0:64], sq[0:64, 0:T], start=True, stop=True),
                         reads=[tsq, T_c], writes=[tpss])
                    rr, trr = dtm.get()
                    S.op("act", lambda e: e.activation(out=rr[0:64, 0:T], in_=pss[0:64, 0:T], func=AF.Ln, scale=1.0 / 64, bias=EPS),
                         reads=[tpss], writes=[trr])
                    S.op("act", lambda e: e.activation(out=rr[0:64, 0:T], in_=rr[0:64, 0:T], func=AF.Exp, scale=-0.5), reads=[trr], writes=[trr])
                    S.op("dve", lambda e: e.scalar_tensor_tensor(out=yT[hp:hp + 64, u // 2, 0:T], in0=dd[0:64, 0:T], scalar=misc[0:64, 2:3], in1=rr[0:64, 0:T],
                                                                  op0=mult, op1=mult), reads=[tdd, trr, T_misc], writes=[T_yT[u]])

                work = [(un, ch) for un in units for ch in chunks]
                loaded = {}
                nxt = [0]

                def ensure(idx):
                    idx = min(idx, len(work) - 1)
                    while nxt[0] <= idx:
                        loaded[nxt[0]] = load_chunk(*work[nxt[0]])
                        nxt[0] += 1

                for ui, un in enumerate(units):
                    tl = []
                    for ci, ch in enumerate(chunks):
                        wi = ui * len(chunks) + ci
                        for t in range(0, NKR if ch[0] == "r" else 2, 2):
                            tl.append((wi, t))
                    po, tpo = ps[6 + obank[0] % 2], T_ps[6 + obank[0] % 2]
                    obank[0] += 1
                    pb_, d_ = un["pb"], un["d"]
                    qt = QT[un["q"]]
                    tq = T_QT[un["q"]]
                    tp = (pb_, 0) if pb_ == 96 else None

                    def qk(n):
                        wi, t = tl[n]
                        ensure(wi + 2)
                        si = loaded[wi][0]
                        b2 = (tcount[0] + n) % 3
                        kw = {"tile_position": tp} if tp else {}
                        for h_ in range(2):
                            S.op("pe", lambda e: e.matmul(ps[2 * b2 + h_][:, 0:T], kslot[si][pb_:pb_ + d_, (t + h_) * 128:(t + h_ + 1) * 128],
                                                         qt[pb_:pb_ + d_, 0:T], start=True, stop=True, **kw),
                                 reads=[T_ks[si], tq], writes=[T_ps[2 * b2 + h_]], sig=(h_ == 1))

                    n_t = len(tl)
                    qk(0)
                    if n_t > 1:
                        qk(1)
                    for n in range(n_t):
                        if n + 2 < n_t:
                            qk(n + 2)
                        wi, t = tl[n]
                        si = loaded[wi][0]
                        b2 = (tcount[0] + n) % 3
                        b3 = (tcount[0] + n) % 3
                        S.op("act", lambda e: e.activation(out=Pt[b3][:, :, 0:T], in_=psd[b2][:, :].rearrange("p (h n) -> p h n", h=2)[:, :, 0:T],
                                                           func=AF.Exp, scale=un["sc"]),
                             reads=[T_ps[2 * b2], T_ps[2 * b2 + 1]], writes=[T_P[b3]])
                        for h_ in range(2):
                            S.op("pe", lambda e: e.matmul(po[:, 0:T], vslot[si][:, t + h_, :], Pt[b3][:, h_, 0:T],
                                                         start=(n == 0 and h_ == 0), stop=(n == n_t - 1 and h_ == 1)),
                                 reads=[T_vs[si], T_P[b3]], writes=[tpo])
                        if n == min(1, n_t - 1) and pending:
                            f = pending.pop(0)
                            f()
                    tcount[0] += n_t
                    pending.append(lambda un=un, po=po, tpo=tpo: finalize(un, po, tpo))
                while pending:
                    pending.pop(0)()
                S.barrier()

        def phase_B3(l, gr):
            S.phase_reset()
            with ExitStack() as ph:
                prep = alloc_prep(ph)
                hT, T_hT = prep[0], prep[1]
                wout = sb("wout", [128, 8, 1024], BF16, ph)
                T_wout = Trk()
                sl_w = S.slot()
                S.dma("sp", sl_w, wout[:], wb[l]["wout"].ap().rearrange("p (k n) -> p k n", k=8), reads=[T_wb[l]["wout"]], writes=[T_wout])
                NR = 5
                ringb = [sb("fring%d" % i, [128, 2048], BF16, ph) for i in range(NR)]
                T_ring = [Trk() for _ in range(NR)]
                sl_ring = [S.slot() for _ in range(NR)]
                rcount = [0]
                actT = sb("actT", [128, 22, 512], BF16, ph)
                T_act = [Trk() for _ in range(11)]
                Gbc = [sb("Gbc%d" % i, [128, 1024], F32, ph) for i in range(2)]
                T_G = [Trk(), Trk()]
                tmp = Tmp(ph, 4, "tmpF")
                dg = Tmp(ph, 2, "dg")
                st2 = [sb("st2_%d" % i, [128, 8], F32, ph) for i in range(2)]
                T_st2 = [Trk(), Trk()]
                T = gr.T
                nt = len(gr.tts)
                for w in range(2):
                    for half in range(2):
                        pg, tpg = nb()
                        for q in range(4):
                            k = half * 4 + q
                            d_, td_ = dg.get()
                            S.op("dve", lambda e: e.tensor_scalar(out=d_[:, 0:128], in0=ident, scalar1=Gc_t[:, w, gr.j, k:k + 1], scalar2=None, op0=mult),
                                 reads=[T_c, T_A], writes=[td_])
                            S.op("pe", lambda e: e.matmul(pg[:, q * 128:(q + 1) * 128], onesf, d_[:, 0:128], start=True, stop=True),
                                 reads=[td_, T_c], writes=[tpg])
                        S.op("act", lambda e: e.activation(out=Gbc[w][:, half * 512:(half + 1) * 512], in_=pg[:, :], func=AF.Copy),
                             reads=[tpg], writes=[T_G[w]])

                def post(w, i, tt, pa, tpa, pb2, tpb2):
                    s_ = st2[i % 2]
                    ts_ = T_st2[i % 2]
                    jk, tjk = tmp.get()
                    S.op("act", lambda e: e.activation(out=jk[:, :], in_=pa[:, :], func=AF.Square, accum_out=s_[:, 0:1]), reads=[tpa], writes=[tjk, ts_])
                    jk2, tjk2 = tmp.get()
                    S.op("act", lambda e: e.activation(out=jk2[:, :], in_=pb2[:, :], func=AF.Square, accum_out=s_[:, 1:2]), reads=[tpb2], writes=[tjk2, ts_])
                    S.op("dve", lambda e: e.tensor_tensor(out=s_[:, 2:3], in0=s_[:, 0:1], in1=s_[:, 1:2], op=add), reads=[ts_], writes=[ts_])
                    S.op("act", lambda e: e.activation(out=s_[:, 3:4], in_=s_[:, 2:3], func=AF.Ln, scale=1.0 / 1024, bias=EPS),
                         reads=[ts_], writes=[ts_])
                    S.op("act", lambda e: e.activation(out=s_[:, 4:5], in_=s_[:, 3:4], func=AF.Exp, scale=-0.5), reads=[ts_], writes=[ts_])
                    for half, (p_, t_) in enumerate(((pa, tpa), (pb2, tpb2))):
                        tm, ttm = tmp.get()
                        S.op("dve", lambda e: e.scalar_tensor_tensor(out=tm[:, :], in0=p_[:, :], scalar=s_[:, 4:5], in1=Gbc[w][:, half * 512:(half + 1) * 512],
                                                                      op0=mult, op1=mult), reads=[t_, ts_, T_G[w]], writes=[ttm])
                        S.op("pool", lambda e: e.tensor_tensor(out=xs[:, tt, half * 512:(half + 1) * 512], in0=xs[:, tt, half * 512:(half + 1) * 512],
                                                               in1=tm[:, :], op=add), reads=[ttm], writes=[T_x[tt]])

                for i, tt in enumerate(gr.tts):
                    pa, tpa = nb()
                    pb2, tpb2 = nb()
                    for k in range(8):
                        for (p_, t_, half) in ((pa, tpa, 0), (pb2, tpb2, 1)):
                            S.op("pe", lambda e: e.matmul(p_[:, :], yT[:, k, i * 128:(i + 1) * 128], wout[:, k, half * 512:(half + 1) * 512],
                                                         start=(k == 0), stop=(k == 7)), reads=[T_yT[2 * k], T_yT[2 * k + 1], T_wout], writes=[t_])
                    post(0, i, tt, pa, tpa, pb2, tpb2)
                if l == 0:
                    dump("xa", xs[:, 0:4, :], [128, 4, 1024], F32)
                prep_hT(prep, l, gr, 1)

                def ring_load(name, jj):
                    ri = rcount[0] % NR
                    rcount[0] += 1
                    S.dma("sp", sl_ring[ri], ringb[ri][:], wb[l][name].ap()[jj], reads=[T_wb[l][name]], writes=[T_ring[ri]])
                    return ri

                for jj in range(11):
                    rg = ring_load("wg", jj)
                    ru = ring_load("wu", jj)
                    wgv = ringb[rg][:].rearrange("p (k n) -> p k n", k=8)
                    wuv = ringb[ru][:].rearrange("p (k n) -> p k n", k=8)
                    for s_ in range(2):
                        j = jj * 2 + s_
                        pg, tpg = nb()
                        pu, tpu = nb()
                        for k in range(8):
                            S.op("pe", lambda e: e.matmul(pg[:, 0:T], wgv[:, k, s_ * 128:(s_ + 1) * 128], hT[:, k, 0:T], start=(k == 0), stop=(k == 7)),
                                 reads=[T_ring[rg], T_hT], writes=[tpg])
                        for k in range(8):
                            S.op("pe", lambda e: e.matmul(pu[:, 0:T], wuv[:, k, s_ * 128:(s_ + 1) * 128], hT[:, k, 0:T], start=(k == 0), stop=(k == 7)),
                                 reads=[T_ring[ru], T_hT], writes=[tpu])
                        sg, tsg = tmp.get()
                        S.op("act", lambda e: e.activation(out=sg[:, 0:T], in_=pg[:, 0:T], func=AF.Silu), reads=[tpg], writes=[tsg])
                        S.op("dve", lambda e: e.tensor_tensor(out=actT[:, j, 0:T], in0=sg[:, 0:T], in1=pu[:, 0:T], op=mult),
                             reads=[tsg, tpu], writes=[T_act[jj]])
                rds = []
                for jj in range(11):
                    rds.append(None)
                for i0 in range(0, nt, 4):
                    tis = list(range(i0, min(i0 + 4, nt)))
                    acc = {}
                    for i in tis:
                        acc[i] = ((ps[2 * i], T_ps[2 * i]), (ps[2 * i + 1], T_ps[2 * i + 1]))
                    for jj in range(11):
                        rd = ring_load("wd", jj)
                        wdv = ringb[rd][:].rearrange("p (s n) -> p s n", s=2)
                        for s_ in range(2):
                            j = jj * 2 + s_
                            for i in tis:
                                for half in range(2):
                                    p_, t_ = acc[i][half]
                                    S.op("pe", lambda e: e.matmul(p_[:, :], actT[:, j, i * 128:(i + 1) * 128], wdv[:, s_, half * 512:(half + 1) * 512],
                                                                 start=(j == 0), stop=(j == 21)), reads=[T_act[jj], T_ring[rd]], writes=[t_])
                    for i in tis:
                        (pa, tpa), (pb2, tpb2) = acc[i]
                        post(1, i, gr.tts[i], pa, tpa, pb2, tpb2)
                S.barrier()

        S.barrier()
        for l in range(NL):
            last = l == NL - 1
            layer_consts(l)
            phase_A(l, [cg] + groups)
            gather(l)
            if l + 1 < NL:
                phase_M(l + 1)
            if l == 0:
                dump("kc", kc_loc[0].ap(), [672, 256], BF16)
                dump("vc", vc_loc[0].ap(), [1280, 132], BF16)
                dump("modT", modT[:], [128, NL * 96], F32)
                dump("misc", misc[:], [128, 64], F32)
            glist = groups + ([] if last else [cg])
            waited = False
            for gr in glist:
                phase_B1(l, gr)
                if not gr.ctx and not waited:
                    gather_wait()
                    waited = True
                if l == 0:
                    for i in range(20):
                        dump("QT%d" % i, QT[i][:], [128, 512], BF16)
                phase_B2(l, gr)
                if l == 0:
                    dump("yT", yT[:], [128, 8, 512], BF16)
                phase_B3(l, gr)
                if l == 0:
                    dump("x1", xs[:, 0:4, :], [128, 4, 1024], F32)
        S.barrier()
        for g4 in range(NG):
            S.dma("sp", sl_out, out_d[g4 * 512:(g4 + 1) * 512, :].rearrange("(t p) d -> p t d", p=128), xs[:, g4 * 4:(g4 + 1) * 4, :],
                  reads=T_x[g4 * 4:(g4 + 1) * 4])
        S.barrier()
    return nc


def _swap(w, r):
    n = w.shape[-1]
    idx = np.arange(n).reshape(-1, r)
    idx = np.concatenate([idx[:, r // 2:], idx[:, :r // 2]], axis=1).reshape(-1)
    return w[..., idx]


def _rope_tabs(pos, n_ctx):
    row = (pos // 64).astype(np.float32)
    col = (pos % 64).astype(np.float32)

    def tab(rot):
        quarter = rot // 4
        inv = (np.float32(10000.0) ** (-np.arange(quarter, dtype=np.float32) / np.float32(quarter))).astype(np.float32)
        ang = np.concatenate([row[:, None] * inv, col[:, None] * inv], axis=-1).astype(np.float32)
        return np.cos(ang).astype(np.float32), np.sin(ang).astype(np.float32)

    out = []
    for rot in (32, 64):
        c, s = tab(rot)
        C = np.concatenate([c, c], 1).T
        Sg = np.concatenate([-s, s], 1).T
        C = np.tile(C, (128 // rot, 1))
        Sg = np.tile(Sg, (128 // rot, 1))
        C = np.concatenate([C, np.ones((128, n_ctx), np.float32)], 1)
        Sg = np.concatenate([Sg, np.zeros((128, n_ctx), np.float32)], 1)
        out += [C, Sg]
    return np.ascontiguousarray(np.stack(out, 0), dtype=np.float32)


def _chunk_rows(w):
    L, R, N = w.shape
    return np.ascontiguousarray(w.reshape(L, R // 128, 128, N).transpose(0, 2, 1, 3))


def prep_shared(inp):
    f = lambda a: np.asarray(a, dtype=np.float32)
    L = f(inp["w_ada"]).shape[0]
    w_in = f(inp["w_in"])
    qa, ckv, kpe = w_in[..., 0:256], w_in[..., 256:384], w_in[..., 384:416]
    dq, dk, dv = w_in[..., 416:672], w_in[..., 672:928], w_in[..., 928:1184]
    gq, gk, gv = w_in[..., 1184:1696], w_in[..., 1696:1824], w_in[..., 1824:1952]
    A = [ckv, kpe, _swap(kpe, 32)]
    for b in range(2):
        blk = dk[..., b * 128:(b + 1) * 128]
        A += [blk, _swap(blk, 32)]
    A += [gk, _swap(gk, 64), dv, gv]
    wA = _chunk_rows(np.concatenate(A, -1))
    Q = [qa]
    for b in range(2):
        blk = dq[..., b * 128:(b + 1) * 128]
        Q += [blk, _swap(blk, 32)]
    for i in range(4):
        blk = np.concatenate([gq[..., i * 64:(i + 1) * 64], gq[..., (4 + i) * 64:(5 + i) * 64]], -1)
        Q += [blk, _swap(blk, 64)]
    wQ = _chunk_rows(np.concatenate(Q, -1))
    wqb = f(inp["w_mla_qb"])
    qb = []
    for h in range(4):
        blk = wqb[..., h * 96:(h + 1) * 96]
        qb += [blk, np.concatenate([blk[..., 0:64], _swap(blk[..., 64:96], 32)], -1)]
    wqb_r = _chunk_rows(np.concatenate(qb, -1))
    wkvb = f(inp["w_mla_kvb"]).reshape(L, 128, 4, 128)
    wkvb_r = np.ascontiguousarray(np.concatenate([wkvb[..., 0:64].reshape(L, 128, 256), wkvb[..., 64:128].reshape(L, 128, 256)], -1))
    wout_r = _chunk_rows(f(inp["w_out"]))
    w_ada = f(inp["w_ada"])
    wada_r = np.ascontiguousarray(w_ada.reshape(L, 8, 128, 12, 512).transpose(0, 3, 2, 1, 4).reshape(L, 12, 128, 4096))
    wg = f(inp["w_ffn_gate"]).reshape(L, 8, 128, 11, 256).transpose(0, 3, 2, 1, 4).reshape(L, 11, 128, 2048)
    wu = f(inp["w_ffn_up"]).reshape(L, 8, 128, 11, 256).transpose(0, 3, 2, 1, 4).reshape(L, 11, 128, 2048)
    wd = f(inp["w_ffn_down"]).reshape(L, 11, 2, 128, 1024).transpose(0, 1, 3, 2, 4).reshape(L, 11, 128, 2048)
    colv = np.zeros((L, 128, NV), np.float32)

    def cols(v):
        return v.reshape(L, -1, 128).transpose(0, 2, 1)

    colv[:, :, 0:8] = cols(f(inp["g_attn_pre"]))
    colv[:, :, 8:16] = cols(f(inp["g_attn_post"]))
    colv[:, :, 16:24] = cols(f(inp["g_ffn_pre"]))
    colv[:, :, 24:32] = cols(f(inp["g_ffn_post"]))
    colv[:, :, 32:80] = cols(f(inp["b_ada"]))
    colv[:, :, 80:82] = cols(f(inp["g_mla_q"]))
    colv[:, :, 82] = f(inp["g_mla_kv"])
    ggq, ggk = f(inp["g_gqa_q"]), f(inp["g_gqa_k"])
    colv[:, :, 83] = np.tile(ggq, (1, 2))
    colv[:, :, 84] = np.tile(_swap(ggq, 64), (1, 2))
    colv[:, :, 85] = np.tile(ggk, (1, 2))
    colv[:, :, 86] = np.tile(_swap(ggk, 64), (1, 2))
    colv[:, :, 87] = np.tile(f(inp["g_diff_sub"]), (1, 2))
    for i, nm in enumerate(("lambda_q1", "lambda_k1", "lambda_q2", "lambda_k2")):
        colv[:, :, 88 + 32 * i:120 + 32 * i] = f(inp[nm])[:, None, :]
    consts = np.zeros((128, 384), np.float32)
    consts[:, 0:128] = np.eye(128, dtype=np.float32)
    consts[:, 128:256] = 1.0
    consts[0:64, 256:320] = 1.0
    consts[64:128, 320:384] = 1.0
    return dict(w_ada_r=wada_r, colv=colv, w_inA=wA, w_inQ=wQ, w_qb_r=wqb_r, w_kvb_r=wkvb_r, w_out_r=wout_r,
                wg_r=np.ascontiguousarray(wg), wu_r=np.ascontiguousarray(wu), wd_r=np.ascontiguousarray(wd), consts=consts)


def make_in_maps(inp, NG):
    shared = prep_shared(inp)
    x = np.asarray(inp["x"], np.float32)
    c = np.asarray(inp["c"], np.float32)
    ctx = np.asarray(inp["ctx"], np.float32)
    c_ctx = np.asarray(inp["c_ctx"], np.float32)
    NTOK = NG * 512
    maps = []
    for core in range(8):
        b, q = core // 4, core % 4
        m = dict(shared)
        m["x_own"] = np.ascontiguousarray(x[b, q * NTOK:(q + 1) * NTOK, :])
        m["ctx_b"] = np.ascontiguousarray(ctx[b])
        cv = np.stack([c[b].reshape(8, 128).T, c_ctx.reshape(8, 128).T], -1)
        m["cvec"] = np.ascontiguousarray(cv.reshape(128, 16))
        m["rope"] = _rope_tabs(np.arange(q * NTOK, (q + 1) * NTOK), 256)
        maps.append(m)
    return maps


_NC_CACHE = {}


def run(inp, NG, ret_res=False):
    if NG not in _NC_CACHE:
        _NC_CACHE[NG] = build(NG)
    nc = _NC_CACHE[NG]
    maps = make_in_maps(inp, NG)
    res = run_bass_kernel_spmd(nc, maps, core_ids=list(range(8)))
    if ret_res:
        return res
    NTOK = NG * 512
    out = np.zeros((2, 4 * NTOK, 1024), np.float32)
    for core in range(8):
        b, q = core // 4, core % 4
        out[b, q * NTOK:(q + 1) * NTOK, :] = res.results[core]["out"]
    return out


def kernel(**inputs):
    return run(inputs, 4)
```

```python
import bisect
import math
from contextlib import ExitStack

import numpy as np
import concourse.bass as bass
import concourse.mybir as mybir
from concourse.bass_utils import run_bass_kernel_spmd

F32 = mybir.dt.float32
BF16 = mybir.dt.bfloat16
AF = mybir.ActivationFunctionType
ALU = mybir.AluOpType
EPS = 1e-6
NV = 216
SC_MLA = 1.0 / math.sqrt(96.0)
SC_DIFF = 1.0 / math.sqrt(32.0)
SC_GQA = 1.0 / 8.0


class Trk:
    __slots__ = ("w", "r")

    def __init__(self):
        self.w = None
        self.r = []


class Slot:
    pass


class Sched:
    COMPUTE = ("pe", "act", "dve", "pool")

    def __init__(self, nc, es):
        self.nc = nc
        self.es = es
        self.eng = {"pe": nc.tensor, "act": nc.scalar, "dve": nc.vector, "pool": nc.gpsimd, "sp": nc.sync}
        self.sem = {e: es.enter_context(nc.semaphore("s_" + e)) for e in self.COMPUTE}
        self.n = {e: 0 for e in self.eng}
        self.last = {e: None for e in self.eng}
        self.sig = {e: [] for e in self.COMPUTE}
        self.known = {e: {} for e in self.eng}
        self.slots = []
        self.pool = []
        self.pidx = 0
        self.nwait = 0

    def _sigval(self, e, idx):
        s = self.sig[e]
        p = bisect.bisect_left(s, idx)
        if p < len(s):
            return p + 1
        li = self.n[e] - 1
        assert li >= idx
        self.last[e].then_inc(self.sem[e], 1)
        s.append(li)
        return len(s)

    def _wait(self, q, dep):
        if dep[0] == "c":
            _, e, idx = dep
            if e == q and q == "pe":
                return
            key = ("c", e)
            kv = self.known[q].get(key, 0)
            s = self.sig[e]
            p = bisect.bisect_left(s, idx)
            if p < len(s) and p + 1 <= kv:
                return
            v = self._sigval(e, idx)
            if v <= kv:
                return
            self.eng[q].wait_ge(self.sem[e], v)
            self.nwait += 1
            self.known[q][key] = v
        else:
            _, sl, val = dep
            val = sl.cnt
            key = ("d", id(sl))
            if self.known[q].get(key, 0) >= val:
                return
            self.eng[q].wait_ge(sl.sem, val)
            self.nwait += 1
            self.known[q][key] = val

    def _deps(self, q, reads, writes):
        for t in reads:
            if t.w is not None:
                self._wait(q, t.w)
        for t in writes:
            if t.w is not None:
                self._wait(q, t.w)
            for d in t.r:
                self._wait(q, d)

    def op(self, q, fn, reads=(), writes=(), sig=False):
        self._deps(q, reads, writes)
        ins = fn(self.eng[q])
        idx = self.n[q]
        self.n[q] += 1
        self.last[q] = ins
        if sig:
            ins.then_inc(self.sem[q], 1)
            self.sig[q].append(idx)
        me = ("c", q, idx)
        for t in reads:
            t.r.append(me)
        for t in writes:
            t.w = me
            t.r = []
        return ins

    def pslot(self):
        sl = Slot()
        sl.sem = self.es.enter_context(self.nc.semaphore("d%d" % len(self.slots)))
        sl.cnt = 0
        self.slots.append(sl)
        return sl

    def slot(self):
        if self.pidx >= len(self.pool):
            self.pool.append(self.pslot())
        sl = self.pool[self.pidx]
        self.pidx += 1
        return sl

    def phase_reset(self):
        self.barrier()
        self.pidx = 0

    def dma(self, q, sl, out, in_, reads=(), writes=()):
        self._deps(q, reads, writes)
        ins = self.eng[q].dma_start(out=out, in_=in_)
        ins.then_inc(sl.sem, 16)
        sl.cnt += 16
        me = ("d", sl, sl.cnt)
        for t in reads:
            t.r.append(me)
        for t in writes:
            t.w = me
            t.r = []
        return ins

    def barrier(self):
        for q in self.eng:
            for e in self.COMPUTE:
                if e != q and self.n[e] > 0:
                    self._wait(q, ("c", e, self.n[e] - 1))
            for sl in self.slots:
                if sl.cnt > 0:
                    self._wait(q, ("d", sl, sl.cnt))


def build(NG=4, NL=2, dbg=False):
    nc = bass.Bass("TRN2", target_bir_lowering=False)
    NTOK = NG * 512
    NTT = NG * 4
    NKR = NG * 4
    mult, add = ALU.mult, ALU.add

    def din(name, shape, dt=F32):
        return nc.dram_tensor(name, shape, dt, kind="ExternalInput").ap()

    x_d = din("x_own", [NTOK, 1024])
    ctx_d = din("ctx_b", [256, 1024])
    cvec_d = din("cvec", [128, 16])
    wada_d = din("w_ada_r", [NL, 12, 128, 8 * 512])
    colv_d = din("colv", [NL, 128, NV])
    wA_d = din("w_inA", [NL, 128, 8, 1344])
    wQ_d = din("w_inQ", [NL, 128, 8, 1792])
    wqb_d = din("w_qb_r", [NL, 128, 2, 768])
    wkvb_d = din("w_kvb_r", [NL, 128, 512])
    wout_d = din("w_out_r", [NL, 128, 8, 1024])
    wg_d = din("wg_r", [NL, 11, 128, 2048])
    wu_d = din("wu_r", [NL, 11, 128, 2048])
    wd_d = din("wd_r", [NL, 11, 128, 2048])
    rope_d = din("rope", [4, 128, NTOK + 256])
    const_d = din("consts", [128, 384])
    out_d = nc.dram_tensor("out", [NTOK, 1024], F32, kind="ExternalOutput").ap()

    KB = [0, 128, 288, 416, 544, 672]
    k_loc = [[nc.dram_tensor("k_loc%d_%d" % (l, s_), [KB[s_ + 1] - KB[s_], NTOK], BF16) for s_ in range(5)] for l in range(NL)]
    k_all = [[nc.dram_tensor("k_all%d_%d" % (l, s_), [4 * (KB[s_ + 1] - KB[s_]), NTOK], BF16) for s_ in range(5)] for l in range(NL)]
    v_loc = [[nc.dram_tensor("v_loc%d_%d" % (l, s_), [256, NKR * 66], BF16) for s_ in range(5)] for l in range(NL)]
    v_all = [[nc.dram_tensor("v_all%d_%d" % (l, s_), [4 * 256, NKR * 66], BF16) for s_ in range(5)] for l in range(NL)]

    wb = []
    for l in range(NL):
        wb.append(dict(
            wA=nc.dram_tensor("wA_b%d" % l, [128, 8 * 1344], BF16), wkvb=nc.dram_tensor("wkvb_b%d" % l, [128, 512], BF16),
            wQ=nc.dram_tensor("wQ_b%d" % l, [128, 8 * 1792], BF16), wqb=nc.dram_tensor("wqb_b%d" % l, [128, 2 * 768], BF16),
            wout=nc.dram_tensor("wout_b%d" % l, [128, 8 * 1024], BF16),
            wg=nc.dram_tensor("wg_b%d" % l, [11, 128, 2048], BF16), wu=nc.dram_tensor("wu_b%d" % l, [11, 128, 2048], BF16),
            wd=nc.dram_tensor("wd_b%d" % l, [11, 128, 2048], BF16)))

    def kblk(r0):
        for s_ in range(5):
            if KB[s_] <= r0 < KB[s_ + 1]:
                return s_, r0 - KB[s_], KB[s_ + 1] - KB[s_]
        raise ValueError
    kc_loc = [nc.dram_tensor("kc_loc%d" % l, [672, 256], BF16) for l in range(NL)]
    vc_loc = [nc.dram_tensor("vc_loc%d" % l, [1280, 2 * 66], BF16) for l in range(NL)]

    with ExitStack() as es:
        S = Sched(nc, es)

        uid = [0]

        def sb(name, shape, dt, stack=None):
            uid[0] += 1
            return (stack or es).enter_context(nc.sbuf_tensor("sb%d_%s" % (uid[0], name), shape, dt))

        xs = sb("xs", [128, NTT + 2, 1024], F32)
        T_x = [Trk() for _ in range(NTT + 2)]
        cst = sb("cst", [128, 384], F32)
        T_c = Trk()
        ident = cst[:, 0:128]
        onesf = cst[:, 128:256]
        bd64 = cst[:, 256:384]
        colv = sb("colv", [128, NL, NV], F32)
        T_colv = Trk()
        modT = sb("modT", [128, NL * 96], F32)
        T_mod = Trk()
        misc = sb("misc", [128, 64], F32)
        T_misc = Trk()
        A_t = sb("A_t", [128, 2, 2, 8], F32)
        Gc_t = sb("Gc_t", [128, 2, 2, 8], F32)
        T_A = Trk()
        QT = [sb("QT%d" % i, [128, 512], BF16) for i in range(20)]
        T_QT = [Trk() for _ in range(20)]
        yT = sb("yT", [128, 8, 512], BF16)
        T_yT = [Trk() for _ in range(16)]

        psd = [es.enter_context(nc.psum_tensor("psd%d" % i, [128, 1024], F32)) for i in range(4)]
        ps = [psd[i // 2][:, (i % 2) * 512:(i % 2 + 1) * 512] for i in range(8)]
        T_ps = [Trk() for _ in range(8)]
        bank_rr = [0]

        def nb(lo=0, hi=8):
            i = lo + (bank_rr[0] % (hi - lo))
            bank_rr[0] += 1
            return ps[i], T_ps[i]

        dbg_done = set()

        def dump(name, src_ap, shape, dt):
            if not dbg or name in dbg_done:
                return
            dbg_done.add(name)
            d = nc.dram_tensor("dbg_" + name, shape, dt, kind="ExternalOutput").ap()
            S.phase_reset()
            sl = S.slot()
            S.dma("sp", sl, d, src_ap)
            S.barrier()

        sl_misc = S.pslot()
        sl_x = S.pslot()
        sl_out = S.pslot()

        def mcol(l, c, j):
            o = (l * 48 + c) * 2 + j
            return modT[:, o:o + 1]

        S.dma("sp", sl_misc, cst[:], const_d, writes=[T_c])
        S.dma("sp", sl_misc, colv[:], colv_d.rearrange("l p n -> p l n"), writes=[T_colv])
        for g4 in range(NG):
            S.dma("sp", sl_x, xs[:, g4 * 4:(g4 + 1) * 4, :], x_d[g4 * 512:(g4 + 1) * 512, :].rearrange("(t p) d -> p t d", p=128),
                  writes=T_x[g4 * 4:(g4 + 1) * 4])
        S.dma("sp", sl_x, xs[:, NTT:NTT + 2, :], ctx_d.rearrange("(t p) d -> p t d", p=128), writes=T_x[NTT:NTT + 2])
        S.op("pool", lambda e: e.memset(misc[:, 0:1], -0.5), writes=[T_misc])
        for i in range(4, 20):
            S.op("pool", lambda e: e.memset(QT[i][:], 0.0), writes=[T_QT[i]])
        mhalf = misc[:, 0:1]

        T_wb = [dict((k_, Trk()) for k_ in wb[l]) for l in range(NL)]
        sl_wb = [dict((k_, S.pslot()) for k_ in wb[l]) for l in range(NL)]
        def convert(l, names):
            def cv(name, dst, src, extra=()):
                S.dma("pool", sl_wb[l][name], dst, src, reads=list(extra), writes=[T_wb[l][name]])
            for name in names:
                if name == "wA":
                    for k in range(8):
                        cv("wA", wb[l]["wA"].ap()[:, k * 1344:(k + 1) * 1344], wA_d[l, :, k, :])
                elif name == "wkvb":
                    cv("wkvb", wb[l]["wkvb"].ap(), wkvb_d[l])
                elif name == "wQ":
                    for k in range(8):
                        cv("wQ", wb[l]["wQ"].ap()[:, k * 1792:(k + 1) * 1792], wQ_d[l, :, k, :])
                elif name == "wqb":
                    for c in range(2):
                        cv("wqb", wb[l]["wqb"].ap()[:, c * 768:(c + 1) * 768], wqb_d[l, :, c, :])
                elif name == "wout":
                    for k in range(8):
                        cv("wout", wb[l]["wout"].ap()[:, k * 1024:(k + 1) * 1024], wout_d[l, :, k, :])
                elif name.startswith("ffn"):
                    j0, j1 = {"ffn": (0, 11), "ffn0": (0, 4), "ffn1": (4, 8), "ffn2": (8, 11)}[name]
                    for jj in range(j0, j1):
                        cv("wg", wb[l]["wg"].ap()[jj], wg_d[l, jj])
                        cv("wu", wb[l]["wu"].ap()[jj], wu_d[l, jj])
                    for jj in range(j0, j1):
                        cv("wd", wb[l]["wd"].ap()[jj], wd_d[l, jj])

        conv_queue = []

        def convert_lazy(l, names):
            real = S.dma

            def rec(*a, **k):
                conv_queue.append((a, k))
            S.dma = rec
            try:
                convert(l, names)
            finally:
                S.dma = real

        def conv_pop(n=1):
            for _ in range(n):
                if conv_queue:
                    a, k = conv_queue.pop(0)
                    S.dma(*a, **k)

        convert(0, ["wA", "wkvb"])

        def phase_M(l, after=None):
            S.phase_reset()
            with ExitStack() as ph:
                scv = sb("scv", [128, 16], F32, ph)
                T_scv = Trk()
                wad = [sb("wad%d" % i, [128, 8, 512], F32, ph) for i in range(3)]
                T_wad = [Trk(), Trk(), Trk()]
                sl_wad = [S.slot(), S.slot(), S.slot()]
                sl_c = S.slot()
                S.dma("sp", sl_c, scv[:], cvec_d, writes=[T_scv])
                S.op("act", lambda e: e.activation(out=scv[:], in_=scv[:], func=AF.Silu), reads=[T_scv], writes=[T_scv])
                modrow = sb("modrow", [2, 6144], F32, ph)
                T_mr = Trk()
                for cb in range(12):
                    bi = cb % 3
                    S.dma("sp", sl_wad[bi], wad[bi][:], wada_d[l, cb].rearrange("p (k n) -> p k n", k=8), writes=[T_wad[bi]])
                    if cb == 11 and after is not None:
                        after(T_wad[bi])
                    pr, tpr = nb()
                    for k in range(8):
                        S.op("pe", lambda e: e.matmul(pr[0:2, :], scv[:, 2 * k:2 * k + 2], wad[bi][:, k, :], start=(k == 0), stop=(k == 7)),
                             reads=[T_wad[bi], T_scv], writes=[tpr])
                    S.op("dve", lambda e: e.tensor_copy(out=modrow[0:2, cb * 512:(cb + 1) * 512], in_=pr[0:2, :]), reads=[tpr], writes=[T_mr])
                pm, tpm = nb()
                for c in range(48):
                    S.op("pe", lambda e: e.matmul(pm[:, 2 * c:2 * c + 2], modrow[0:2, c * 128:(c + 1) * 128], ident[0:2, 0:2], start=True, stop=True),
                         reads=[T_mr, T_c], writes=[tpm])
                for j in range(2):
                    S.op("dve", lambda e: e.tensor_tensor(
                        out=modT[:, l * 96:(l + 1) * 96].rearrange("p (c j) -> p c j", j=2)[:, :, j],
                        in0=pm[:, 0:96].rearrange("p (c j) -> p c j", j=2)[:, :, j],
                        in1=colv[:, l, 32:80], op=add), reads=[tpm, T_colv], writes=[T_mod])
                S.barrier()

        def rest_conversions(t_last):
            S._deps("pool", [t_last], [])
            convert(0, ["wQ", "wqb"])

        phase_M(0, after=rest_conversions)

        def layer_consts(l):
            lam_init = 0.8 - 0.6 * math.exp(-0.3 * l)
            for w in range(2):
                for j in range(2):
                    sc0 = (l * 48 + (1 + 3 * w) * 8) * 2
                    gt0 = (l * 48 + (2 + 3 * w) * 8) * 2
                    sc_v = modT[:, sc0:sc0 + 16].rearrange("p (k j) -> p k j", j=2)[:, :, j]
                    gt_v = modT[:, gt0:gt0 + 16].rearrange("p (k j) -> p k j", j=2)[:, :, j]
                    gpre = colv[:, l, 16 * w:16 * w + 8]
                    gpost = colv[:, l, 16 * w + 8:16 * w + 16]
                    S.op("dve", lambda e: e.scalar_tensor_tensor(out=A_t[:, w, j, :], in0=sc_v, scalar=1.0, in1=gpre,
                                                                  op0=add, op1=mult), reads=[T_mod, T_colv], writes=[T_A])
                    S.op("dve", lambda e: e.tensor_tensor(out=Gc_t[:, w, j, :], in0=gt_v, in1=gpost, op=mult),
                         reads=[T_mod, T_colv], writes=[T_A])
            S.op("dve", lambda e: e.tensor_tensor(out=misc[:, 8:40], in0=colv[:, l, 88:120], in1=colv[:, l, 120:152], op=mult),
                 reads=[T_colv, T_misc], writes=[T_misc])
            S.op("dve", lambda e: e.reduce_sum(out=misc[:, 4:5], in_=misc[:, 8:40], axis=mybir.AxisListType.X),
                 reads=[T_misc], writes=[T_misc])
            S.op("dve", lambda e: e.tensor_tensor(out=misc[:, 8:40], in0=colv[:, l, 152:184], in1=colv[:, l, 184:216], op=mult),
                 reads=[T_colv, T_misc], writes=[T_misc])
            S.op("dve", lambda e: e.reduce_sum(out=misc[:, 5:6], in_=misc[:, 8:40], axis=mybir.AxisListType.X),
                 reads=[T_misc], writes=[T_misc])
            S.op("act", lambda e: e.activation(out=misc[:, 4:6], in_=misc[:, 4:6], func=AF.Exp), reads=[T_misc], writes=[T_misc])
            S.op("dve", lambda e: e.scalar_tensor_tensor(out=misc[:, 1:2], in0=misc[:, 5:6], scalar=-lam_init, in1=misc[:, 4:5],
                                                          op0=add, op1=ALU.subtract), reads=[T_misc], writes=[T_misc])
            S.op("dve", lambda e: e.tensor_scalar(out=misc[:, 2:3], in0=colv[:, l, 87:88], scalar1=(1.0 - lam_init), scalar2=None,
                                                  op0=mult), reads=[T_colv, T_misc], writes=[T_misc])

        class G:
            pass

        groups = []
        for g in range(NG):
            gr = G()
            gr.tts = [g * 4 + i for i in range(4)]
            gr.T = 512
            gr.j = 0
            gr.tab0 = g * 512
            gr.g = g
            gr.ctx = False
            groups.append(gr)
        cg = G()
        cg.tts = [NTT, NTT + 1]
        cg.T = 256
        cg.j = 1
        cg.tab0 = NTOK
        cg.g = None
        cg.ctx = True

        def prep_hT(ph_t, l, gr, w):
            hT, T_hT, xn, T_xn, junk, T_junk, st, T_st = ph_t
            for i, tt in enumerate(gr.tts):
                b = i % 2
                s_ = st[i % 4]
                ts_ = T_st[i % 4]
                S.op("act", lambda e: e.activation(out=junk[:], in_=xs[:, tt, :], func=AF.Square, accum_out=s_[:, 0:1]),
                     reads=[T_x[tt]], writes=[T_junk, ts_])
                S.op("act", lambda e: e.activation(out=s_[:, 1:2], in_=s_[:, 0:1], func=AF.Ln, scale=1.0 / 1024, bias=EPS),
                     reads=[ts_], writes=[ts_])
                S.op("act", lambda e: e.activation(out=s_[:, 2:3], in_=s_[:, 1:2], func=AF.Exp, scale=-0.5),
                     reads=[ts_], writes=[ts_])
                S.op("dve", lambda e: e.tensor_scalar(out=xn[b][:], in0=xs[:, tt, :], scalar1=s_[:, 2:3], scalar2=None, op0=mult),
                     reads=[T_x[tt], ts_], writes=[T_xn[b]])
                for half in range(2):
                    pt, tpt = nb()
                    for q in range(4):
                        k = half * 4 + q
                        S.op("pe", lambda e: e.transpose(pt[:, q * 128:(q + 1) * 128], xn[b][:, k * 128:(k + 1) * 128], ident),
                             reads=[T_xn[b], T_c], writes=[tpt])
                    for q in range(4):
                        k = half * 4 + q
                        acol = A_t[:, w, gr.j, k:k + 1]
                        bcol = mcol(l, (3 * w) * 8 + k, gr.j)
                        o_ = hT[:, k, i * 128:(i + 1) * 128]
                        i_ = pt[:, q * 128:(q + 1) * 128]
                        if q % 2 == 0:
                            S.op("act", lambda e: e.activation(out=o_, in_=i_, func=AF.Identity, scale=acol, bias=bcol),
                                 reads=[tpt, T_A, T_mod], writes=[T_hT])
                        else:
                            S.op("dve", lambda e: e.tensor_scalar(out=o_, in0=i_, scalar1=acol, scalar2=bcol, op0=mult, op1=add),
                                 reads=[tpt, T_A, T_mod], writes=[T_hT])

        def alloc_prep(ph):
            hT = sb("hT", [128, 8, 512], BF16, ph)
            xn = [sb("xn%d" % i, [128, 1024], F32, ph) for i in range(2)]
            junk = sb("junk", [128, 1024], BF16, ph)
            st = [sb("st%d" % i, [128, 4], F32, ph) for i in range(4)]
            return (hT, Trk(), xn, [Trk(), Trk()], junk, Trk(), st, [Trk() for _ in range(4)])

        def proj_block(W, T_W, c0, wd, hT, T_hT, T):
            pb, tpb = nb()
            for k in range(8):
                S.op("pe", lambda e: e.matmul(pb[0:wd, 0:T], W[:, k, c0:c0 + wd], hT[:, k, 0:T], start=(k == 0), stop=(k == 7)),
                     reads=[T_W, T_hT], writes=[tpb])
            return pb, tpb

        class Tmp:
            def __init__(self, ph, n, name, dt=F32):
                self.t = [sb("%s%d" % (name, i), [128, 512], dt, ph) for i in range(n)]
                self.k = [Trk() for _ in range(n)]
                self.i = 0

            def get(self):
                i = self.i % len(self.t)
                self.i += 1
                return self.t[i], self.k[i]

        def rstd_bcast(tmp, src_list, lhs, n_feat, T, rows=128):
            pst, tpst = nb()
            for i, (p_, t_) in enumerate(src_list):
                sq, tsq = tmp.get()
                S.op("act", lambda e: e.activation(out=sq[0:rows, 0:T], in_=p_[0:rows, 0:T], func=AF.Square), reads=[t_], writes=[tsq])
                S.op("pe", lambda e: e.matmul(pst[0:rows, 0:T], lhs[0:rows, 0:rows], sq[0:rows, 0:T], start=(i == 0), stop=(i == len(src_list) - 1)),
                     reads=[tsq, T_c], writes=[tpst])
            r, tr = tmp.get()
            S.op("act", lambda e: e.activation(out=r[0:rows, 0:T], in_=pst[0:rows, 0:T], func=AF.Ln, scale=1.0 / n_feat, bias=EPS),
                 reads=[tpst], writes=[tr])
            S.op("act", lambda e: e.activation(out=r[0:rows, 0:T], in_=r[0:rows, 0:T], func=AF.Exp, scale=-0.5), reads=[tr], writes=[tr])
            return r, tr

        def rope_comb(tmp, out_ap, t_out, a, ta, b_, tb, Ct, St, T_tab, r0, r1, T):
            t1, tt1 = tmp.get()
            t2, tt2 = tmp.get()
            S.op("dve", lambda e: e.tensor_tensor(out=t1[r0:r1, 0:T], in0=a[r0:r1, 0:T], in1=Ct[r0:r1, 0:T], op=mult),
                 reads=[ta, T_tab], writes=[tt1])
            S.op("dve", lambda e: e.tensor_tensor(out=t2[r0:r1, 0:T], in0=b_[r0:r1, 0:T], in1=St[r0:r1, 0:T], op=mult),
                 reads=[tb, T_tab], writes=[tt2])
            S.op("dve", lambda e: e.tensor_tensor(out=out_ap, in0=t1[r0:r1, 0:T], in1=t2[r0:r1, 0:T], op=add),
                 reads=[tt1, tt2], writes=[t_out])

        def load_tabs(tabs, T_tab, sl_tab, gr):
            for ti in range(4):
                S.dma("sp", sl_tab, tabs[ti][:, 0:gr.T], rope_d[ti, :, gr.tab0:gr.tab0 + gr.T], writes=[T_tab])

        def gqa_block(tmp, pb, tpb, pbs, tpbs, gcol, gscol, tabs, T_tab, out_ap, t_out, T):
            r, tr = rstd_bcast(tmp, [(pb, tpb)], bd64, 64.0, T)
            xg, txg = tmp.get()
            xw, txw = tmp.get()
            S.op("dve", lambda e: e.scalar_tensor_tensor(out=xg[:, 0:T], in0=pb[:, 0:T], scalar=gcol, in1=r[:, 0:T], op0=mult, op1=mult),
                 reads=[tpb, tr, T_colv], writes=[txg])
            S.op("dve", lambda e: e.scalar_tensor_tensor(out=xw[:, 0:T], in0=pbs[:, 0:T], scalar=gscol, in1=r[:, 0:T], op0=mult, op1=mult),
                 reads=[tpbs, tr, T_colv], writes=[txw])
            rope_comb(tmp, out_ap, t_out, xg, txg, xw, txw, tabs[2], tabs[3], T_tab, 0, 128, T)

        def phase_A(l, glist):
            S.phase_reset()
            with ExitStack() as ph:
                prep = alloc_prep(ph)
                hT, T_hT = prep[0], prep[1]
                wA = sb("wA", [128, 8, 1344], BF16, ph)
                T_wA = Trk()
                wkvb = sb("wkvb", [128, 512], BF16, ph)
                T_wkvb = Trk()
                sl_w = S.slot()
                S.dma("sp", sl_w, wA[:], wb[l]["wA"].ap().rearrange("p (k n) -> p k n", k=8), reads=[T_wb[l]["wA"]], writes=[T_wA])
                S.dma("sp", sl_w, wkvb[:], wb[l]["wkvb"].ap(), reads=[T_wb[l]["wkvb"]], writes=[T_wkvb])
                tabs = [sb("tab%d" % i, [128, 512], F32, ph) for i in range(4)]
                T_tab = Trk()
                sl_tab = S.slot()
                tmp = Tmp(ph, 8, "tmpA")
                ckvn = sb("ckvn", [128, 512], BF16, ph)
                T_ckvn = Trk()
                kst = Tmp(ph, 3, "kst", BF16)
                sl_kst = [S.slot() for _ in range(3)]
                vst = sb("vst", [128, 10, 4, 66], BF16, ph)
                T_vst = Trk()
                sl_vst = S.slot()
                S.op("dve", lambda e: e.memset(vst[:, :, :, 64:65], 1.0), writes=[T_vst])
                S.op("dve", lambda e: e.memset(vst[:, :, :, 65:66], 0.0), writes=[T_vst])
                T_dram = Trk()

                def kout(tile_, ttile, idx, r0, nrows, gr, prow0=0):
                    if gr.ctx:
                        dst = kc_loc[l].ap()[r0:r0 + nrows, 0:256]
                    else:
                        s_, off, _ = kblk(r0)
                        dst = k_loc[l][s_].ap()[off:off + nrows, gr.g * 512:(gr.g + 1) * 512]
                    S.dma("sp", sl_kst[idx], dst, tile_[prow0:prow0 + nrows, 0:gr.T], reads=[ttile], writes=[])

                for gr in glist:
                    T = gr.T
                    nt = len(gr.tts)
                    prep_hT(prep, l, gr, 0)
                    load_tabs(tabs, T_tab, sl_tab, gr)
                    pb, tpb = proj_block(wA, T_wA, 0, 128, hT, T_hT, T)
                    r, tr = rstd_bcast(tmp, [(pb, tpb)], onesf, 128.0, T)
                    S.op("dve", lambda e: e.scalar_tensor_tensor(out=ckvn[:, 0:T], in0=pb[:, 0:T], scalar=colv[:, l, 82:83], in1=r[:, 0:T],
                                                                  op0=mult, op1=mult), reads=[tpb, tr, T_colv], writes=[T_ckvn])
                    for hp in range(2):
                        pk, tpk = nb()
                        S.op("pe", lambda e: e.matmul(pk[:, 0:T], wkvb[:, hp * 128:(hp + 1) * 128], ckvn[:, 0:T], start=True, stop=True),
                             reads=[T_wkvb, T_ckvn], writes=[tpk])
                        idx = kst.i % 3
                        ks, tks = kst.get()
                        S.op("act", lambda e: e.activation(out=ks[:, 0:T], in_=pk[:, 0:T], func=AF.Copy), reads=[tpk], writes=[tks])
                        kout(ks, tks, idx, hp * 128, 128, gr)
                    for i in range(nt):
                        pv, tpv = nb()
                        S.op("pe", lambda e: e.matmul(pv[:, 0:256], ckvn[:, i * 128:(i + 1) * 128], wkvb[:, 256:512], start=True, stop=True),
                             reads=[T_wkvb, T_ckvn], writes=[tpv])
                        S.op("dve", lambda e: e.tensor_copy(out=vst[:, 0:4, i, 0:64], in_=pv[:, 0:256].rearrange("p (u d) -> p u d", d=64)),
                             reads=[tpv], writes=[T_vst])
                    pb, tpb = proj_block(wA, T_wA, 128, 32, hT, T_hT, T)
                    pbs, tpbs = proj_block(wA, T_wA, 160, 32, hT, T_hT, T)
                    idx = kst.i % 3
                    ks, tks = kst.get()
                    rope_comb(tmp, ks[0:32, 0:T], tks, pb, tpb, pbs, tpbs, tabs[0], tabs[1], T_tab, 0, 32, T)
                    kout(ks, tks, idx, 256, 32, gr)
                    for b in range(2):
                        pb, tpb = proj_block(wA, T_wA, 192 + b * 256, 128, hT, T_hT, T)
                        pbs, tpbs = proj_block(wA, T_wA, 320 + b * 256, 128, hT, T_hT, T)
                        idx = kst.i % 3
                        ks, tks = kst.get()
                        rope_comb(tmp, ks[:, 0:T], tks, pb, tpb, pbs, tpbs, tabs[0], tabs[1], T_tab, 0, 128, T)
                        kout(ks, tks, idx, 288 + b * 128, 128, gr)
                    pb, tpb = proj_block(wA, T_wA, 704, 128, hT, T_hT, T)
                    pbs, tpbs = proj_block(wA, T_wA, 832, 128, hT, T_hT, T)
                    idx = kst.i % 3
                    ks, tks = kst.get()
                    gqa_block(tmp, pb, tpb, pbs, tpbs, colv[:, l, 85:86], colv[:, l, 86:87], tabs, T_tab, ks[:, 0:T], tks, T)
                    kout(ks, tks, idx, 544, 128, gr)
                    for i in range(nt):
                        pv, tpv = nb()
                        for k in range(8):
                            S.op("pe", lambda e: e.matmul(pv[:, 0:384], hT[:, k, i * 128:(i + 1) * 128], wA[:, k, 960:1344],
                                                         start=(k == 0), stop=(k == 7)), reads=[T_wA, T_hT], writes=[tpv])
                        S.op("dve", lambda e: e.tensor_copy(out=vst[:, 4:10, i, 0:64], in_=pv[:, 0:384].rearrange("p (u d) -> p u d", d=64)),
                             reads=[tpv], writes=[T_vst])
                    if gr.ctx:
                        dst = vc_loc[l].ap().rearrange("(u p) (t c) -> p u t c", p=128, c=66)
                        S.dma("sp", sl_vst, dst, vst[:, :, 0:2, :], reads=[T_vst])
                    else:
                        for s_ in range(5):
                            dst = v_loc[l][s_].ap().rearrange("(u p) (t c) -> p u t c", p=128, c=66)[:, :, gr.g * 4:(gr.g + 1) * 4, :]
                            S.dma("sp", sl_vst, dst, vst[:, 2 * s_:2 * s_ + 2, :, :], reads=[T_vst])
                S.barrier()

        cc_sem = es.enter_context(nc.semaphore("cc"))
        cc_cnt = [0]

        def gather(l):
            S.barrier()
            for (a, b_) in list(zip(k_loc[l], k_all[l])) + list(zip(v_loc[l], v_all[l])):
                nc.gpsimd.collective_compute("AllGather", ALU.bypass, replica_groups=[[0, 1, 2, 3], [4, 5, 6, 7]],
                                             ins=[a.ap().opt()], outs=[b_.ap().opt()]).then_inc(cc_sem, 1)
                cc_cnt[0] += 1

        def gather_wait():
            for q in S.eng:
                S.eng[q].wait_ge(cc_sem, cc_cnt[0])

        def phase_B1(l, gr):
            S.phase_reset()
            with ExitStack() as ph:
                prep = alloc_prep(ph)
                hT, T_hT = prep[0], prep[1]
                wQ = sb("wQ", [128, 8, 1792], BF16, ph)
                T_wQ = Trk()
                wqb = sb("wqb", [128, 2, 768], BF16, ph)
                T_wqb = Trk()
                sl_w = S.slot()
                S.dma("sp", sl_w, wQ[:], wb[l]["wQ"].ap().rearrange("p (k n) -> p k n", k=8), reads=[T_wb[l]["wQ"]], writes=[T_wQ])
                S.dma("sp", sl_w, wqb[:], wb[l]["wqb"].ap().rearrange("p (c n) -> p c n", c=2), reads=[T_wb[l]["wqb"]], writes=[T_wqb])
                tabs = [sb("tab%d" % i, [128, 512], F32, ph) for i in range(4)]
                T_tab = Trk()
                sl_tab = S.slot()
                tmp = Tmp(ph, 8, "tmpB")
                qan = sb("qan", [128, 2, 512], BF16, ph)
                T_qan = Trk()
                qfull = Tmp(ph, 2, "qfull", BF16)
                T = gr.T
                prep_hT(prep, l, gr, 0)
                load_tabs(tabs, T_tab, sl_tab, gr)
                p0, tp0 = proj_block(wQ, T_wQ, 0, 128, hT, T_hT, T)
                p1, tp1 = proj_block(wQ, T_wQ, 128, 128, hT, T_hT, T)
                r, tr = rstd_bcast(tmp, [(p0, tp0), (p1, tp1)], onesf, 256.0, T)
                for c, (p_, t_) in enumerate(((p0, tp0), (p1, tp1))):
                    S.op("dve", lambda e: e.scalar_tensor_tensor(out=qan[:, c, 0:T], in0=p_[:, 0:T], scalar=colv[:, l, 80 + c:81 + c], in1=r[:, 0:T],
                                                                  op0=mult, op1=mult), reads=[t_, tr, T_colv], writes=[T_qan])
                for h in range(4):
                    pb, tpb = nb()
                    pbs, tpbs = nb()
                    for c in range(2):
                        S.op("pe", lambda e: e.matmul(pb[0:96, 0:T], wqb[:, c, h * 192:h * 192 + 96], qan[:, c, 0:T], start=(c == 0), stop=(c == 1)),
                             reads=[T_wqb, T_qan], writes=[tpb])
                    for c in range(2):
                        S.op("pe", lambda e: e.matmul(pbs[0:96, 0:T], wqb[:, c, h * 192 + 96:h * 192 + 192], qan[:, c, 0:T], start=(c == 0), stop=(c == 1)),
                             reads=[T_wqb, T_qan], writes=[tpbs])
                    S.op("act", lambda e: e.activation(out=QT[h][0:64, 0:T], in_=pb[0:64, 0:T], func=AF.Copy), reads=[tpb], writes=[T_QT[h]])
                    rope_comb(tmp, QT[h][64:96, 0:T], T_QT[h], pb, tpb, pbs, tpbs, tabs[0], tabs[1], T_tab, 64, 96, T)
                for b in range(2):
                    pb, tpb = proj_block(wQ, T_wQ, 256 + b * 256, 128, hT, T_hT, T)
                    pbs, tpbs = proj_block(wQ, T_wQ, 384 + b * 256, 128, hT, T_hT, T)
                    qf, tqf = qfull.get()
                    rope_comb(tmp, qf[:, 0:T], tqf, pb, tpb, pbs, tpbs, tabs[0], tabs[1], T_tab, 0, 128, T)
                    for j in range(4):
                        m = b * 4 + j
                        S.op("pool", lambda e: e.tensor_copy(out=QT[4 + m][j * 32:(j + 1) * 32, 0:T], in_=qf[j * 32:(j + 1) * 32, 0:T]),
                             reads=[tqf], writes=[T_QT[4 + m]])
                for i in range(4):
                    pb, tpb = proj_block(wQ, T_wQ, 768 + i * 256, 128, hT, T_hT, T)
                    pbs, tpbs = proj_block(wQ, T_wQ, 896 + i * 256, 128, hT, T_hT, T)
                    qf, tqf = qfull.get()
                    gqa_block(tmp, pb, tpb, pbs, tpbs, colv[:, l, 83:84], colv[:, l, 84:85], tabs, T_tab, qf[:, 0:T], tqf, T)
                    for j in range(2):
                        hq = j * 4 + i
                        S.op("pool", lambda e: e.tensor_copy(out=QT[12 + hq][j * 64:(j + 1) * 64, 0:T], in_=qf[j * 64:(j + 1) * 64, 0:T]),
                             reads=[tqf], writes=[T_QT[12 + hq]])
                S.barrier()

        def phase_B2(l, gr):
            S.phase_reset()
            with ExitStack() as ph:
                NSL = 4
                kslot = [sb("kslot%d" % i, [128, NTOK], BF16, ph) for i in range(NSL)]
                vslot = [sb("vslot%d" % i, [128, NKR, 128], BF16, ph) for i in range(NSL)]
                T_ks = [Trk() for _ in range(NSL)]
                T_vs = [Trk() for _ in range(NSL)]
                sl_kv = [S.slot() for _ in range(NSL)]
                for i in range(NSL):
                    S.op("pool", lambda e: e.memset(vslot[i][:, :, 64:128], 0.0), writes=[T_vs[i]])
                Pt = [sb("Pt%d" % i, [128, 2, 512], BF16, ph) for i in range(3)]
                T_P = [Trk() for _ in range(3)]
                osb = Tmp(ph, 3, "osb")
                rec = Tmp(ph, 2, "rec")
                on_ = Tmp(ph, 3, "on")
                dtm = Tmp(ph, 3, "dtm")
                T = gr.T
                ring = [0]
                tcount = [0]
                obank = [0]
                pending = []
                chunks = [("ctx",)] if gr.ctx else [("r", 0), ("r", 1), ("r", 2), ("r", 3), ("ctx",)]

                units = []
                for h in range(4):
                    units.append(dict(q=h, pb=0, d=96, kr=[(h * 64, 64, 0), (256, 32, 64)], vu=h, sc=SC_MLA, kind="plain", u=h))
                for m in range(8):
                    units.append(dict(q=4 + m, pb=0, d=128, kr=[(288 + (m // 4) * 128, 128, 0)], vu=4 + m // 2, sc=SC_DIFF,
                                      kind="diff%d" % (m % 2), u=4 + m // 2))
                for hq in range(8):
                    kvh = hq // 4
                    units.append(dict(q=12 + hq, pb=0, d=128, kr=[(544, 128, 0)], vu=8 + kvh, sc=SC_GQA, kind="plain", u=8 + hq))

                def load_chunk(un, ch):
                    si = ring[0] % NSL
                    ring[0] += 1
                    if ch[0] == "r":
                        rk = ch[1]
                        for (r0, nr, p0) in un["kr"]:
                            s_, off, bn = kblk(r0)
                            S.dma("sp", sl_kv[si], kslot[si][p0:p0 + nr, 0:NTOK], k_all[l][s_].ap()[rk * bn + off:rk * bn + off + nr, :], writes=[T_ks[si]])
                        vu = un["vu"]
                        vsrc = v_all[l][vu // 2].ap()[rk * 256 + (vu % 2) * 128:rk * 256 + (vu % 2 + 1) * 128, :].rearrange("p (t c) -> p t c", c=66)
                        S.dma("sp", sl_kv[si], vslot[si][:, 0:NKR, 0:66], vsrc, writes=[T_vs[si]])
                        return si, NKR
                    else:
                        for (r0, nr, p0) in un["kr"]:
                            S.dma("sp", sl_kv[si], kslot[si][p0:p0 + nr, 0:256], kc_loc[l].ap()[r0:r0 + nr, :], writes=[T_ks[si]])
                        vsrc = vc_loc[l].ap()[un["vu"] * 128:(un["vu"] + 1) * 128, :].rearrange("p (t c) -> p t c", c=66)
                        S.dma("sp", sl_kv[si], vslot[si][:, 0:2, 0:66], vsrc, writes=[T_vs[si]])
                        return si, 2

                if l == 0 and gr.g is not None and NL > 1:
                    if gr.g == 0:
                        convert_lazy(0, ["wout", "ffn"])
                    if NG >= 4:
                        sched = {1: ["wA", "wkvb", "wQ", "wqb", "wout"], 2: ["ffn0", "ffn1"], 3: ["ffn2"]}.get(gr.g, [])
                    else:
                        sched = ["wA", "wkvb", "wQ", "wqb", "wout", "ffn"] if gr.g == 0 else []
                    convert_lazy(1, sched)
                conv_iv = max(1, (len(units_n := [0] * 20) * (len(chunks) * NKR // 2)) // (len(conv_queue) + 1)) if conv_queue else 0
                pair_ctr = [0]
                dstate = {}

                def finalize(un, po, tpo):
                    u = un["u"]
                    ob, tob = osb.get()
                    rc, trc = rec.get()
                    S.op("dve", lambda e: e.reciprocal(out=rc[64:65, 0:T], in_=po[64:65, 0:T]), reads=[tpo], writes=[trc])
                    S.op("act", lambda e: e.activation(out=ob[0:64, 0:T], in_=po[0:64, 0:T], func=AF.Copy), reads=[tpo], writes=[tob])
                    pbc, tpbc = po, tpo
                    S.op("pe", lambda e: e.matmul(pbc[0:64, 0:T], onesf[64:65, 0:64], rc[64:65, 0:T], start=True, stop=True),
                         reads=[trc, T_c], writes=[tpbc])
                    hp = (u % 2) * 64
                    if un["kind"] == "plain":
                        S.op("dve", lambda e: e.tensor_tensor(out=yT[hp:hp + 64, u // 2, 0:T], in0=ob[0:64, 0:T], in1=pbc[0:64, 0:T], op=mult),
                             reads=[tob, tpbc], writes=[T_yT[u]])
                        return
                    o_n, ton = on_.get()
                    S.op("dve", lambda e: e.tensor_tensor(out=o_n[0:64, 0:T], in0=ob[0:64, 0:T], in1=pbc[0:64, 0:T], op=mult),
                         reads=[tob, tpbc], writes=[ton])
                    if un["kind"] == "diff0":
                        dstate[u] = (o_n, ton)
                        return
                    o1, to1 = dstate.pop(u)
                    dd, tdd = dtm.get()
                    S.op("dve", lambda e: e.scalar_tensor_tensor(out=dd[0:64, 0:T], in0=o_n[0:64, 0:T], scalar=misc[0:64, 1:2], in1=o1[0:64, 0:T],
                                                                  op0=mult, op1=add), reads=[ton, to1, T_misc], writes=[tdd])
                    sq, tsq = dtm.get()
                    S.op("act", lambda e: e.activation(out=sq[0:64, 0:T], in_=dd[0:64, 0:T], func=AF.Square), reads=[tdd], writes=[tsq])
                    pss, tpss = po, tpo
                    S.op("pe", lambda e: e.matmul(pss[0:64, 0:T], onesf[0:64, 0:64], sq[0:64, 0:T], start=True, stop=True),
                         reads=[tsq, T_c], writes=[tpss])
                    rr, trr = dtm.get()
                    S.op("act", lambda e: e.activation(out=rr[0:64, 0:T], in_=pss[0:64, 0:T], func=AF.Ln, scale=1.0 / 64, bias=EPS),
                         reads=[tpss], writes=[trr])
                    S.op("act", lambda e: e.activation(out=rr[0:64, 0:T], in_=rr[0:64, 0:T], func=AF.Exp, scale=-0.5), reads=[trr], writes=[trr])
                    S.op("dve", lambda e: e.scalar_tensor_tensor(out=yT[hp:hp + 64, u // 2, 0:T], in0=dd[0:64, 0:T], scalar=misc[0:64, 2:3], in1=rr[0:64, 0:T],
                                                                  op0=mult, op1=mult), reads=[tdd, trr, T_misc], writes=[T_yT[u]])

                work = [(un, ch) for un in units for ch in chunks]
                loaded = {}
                nxt = [0]

                def ensure(idx):
                    idx = min(idx, len(work) - 1)
                    while nxt[0] <= idx:
                        loaded[nxt[0]] = load_chunk(*work[nxt[0]])
                        nxt[0] += 1

                for ui, un in enumerate(units):
                    tl = []
                    for ci, ch in enumerate(chunks):
                        wi = ui * len(chunks) + ci
                        for t in range(0, NKR if ch[0] == "r" else 2, 2):
                            tl.append((wi, t))
                    po, tpo = ps[6 + obank[0] % 2], T_ps[6 + obank[0] % 2]
                    obank[0] += 1
                    pb_, d_ = un["pb"], un["d"]
                    qt = QT[un["q"]]
                    tq = T_QT[un["q"]]
                    tp = (pb_, 0) if pb_ == 96 else None

                    def qk(n):
                        wi, t = tl[n]
                        ensure(wi + 2)
                        si = loaded[wi][0]
                        b2 = (tcount[0] + n) % 3
                        kw = {"tile_position": tp} if tp else {}
                        for h_ in range(2):
                            S.op("pe", lambda e: e.matmul(ps[2 * b2 + h_][:, 0:T], kslot[si][pb_:pb_ + d_, (t + h_) * 128:(t + h_ + 1) * 128],
                                                         qt[pb_:pb_ + d_, 0:T], start=True, stop=True, **kw),
                                 reads=[T_ks[si], tq], writes=[T_ps[2 * b2 + h_]], sig=(h_ == 1))

                    n_t = len(tl)
                    qk(0)
                    if n_t > 1:
                        qk(1)
                    for n in range(n_t):
                        if n + 2 < n_t:
                            qk(n + 2)
                        wi, t = tl[n]
                        si = loaded[wi][0]
                        b2 = (tcount[0] + n) % 3
                        b3 = (tcount[0] + n) % 3
                        S.op("act", lambda e: e.activation(out=Pt[b3][:, :, 0:T], in_=psd[b2][:, :].rearrange("p (h n) -> p h n", h=2)[:, :, 0:T],
                                                           func=AF.Exp, scale=un["sc"]),
                             reads=[T_ps[2 * b2], T_ps[2 * b2 + 1]], writes=[T_P[b3]])
                        for h_ in range(2):
                            S.op("pe", lambda e: e.matmul(po[:, 0:T], vslot[si][:, t + h_, :], Pt[b3][:, h_, 0:T],
                                                         start=(n == 0 and h_ == 0), stop=(n == n_t - 1 and h_ == 1)),
                                 reads=[T_vs[si], T_P[b3]], writes=[tpo])
                        if n == min(1, n_t - 1) and pending:
                            f = pending.pop(0)
                            f()
                        pair_ctr[0] += 1
                        if conv_iv and conv_queue and pair_ctr[0] % conv_iv == 0:
                            S._wait("pool", ("c", "act", S.n["act"] - 1))
                            conv_pop(1)
                    tcount[0] += n_t
                    pending.append(lambda un=un, po=po, tpo=tpo: finalize(un, po, tpo))
                while pending:
                    pending.pop(0)()
                conv_pop(len(conv_queue))
                S.barrier()

        def phase_B3(l, gr):
            S.phase_reset()
            with ExitStack() as ph:
                prep = alloc_prep(ph)
                hT, T_hT = prep[0], prep[1]
                wout = sb("wout", [128, 8, 1024], BF16, ph)
                T_wout = Trk()
                sl_w = S.slot()
                S.dma("sp", sl_w, wout[:], wb[l]["wout"].ap().rearrange("p (k n) -> p k n", k=8), reads=[T_wb[l]["wout"]], writes=[T_wout])
                NR = 5
                ringb = [sb("fring%d" % i, [128, 2048], BF16, ph) for i in range(NR)]
                T_ring = [Trk() for _ in range(NR)]
                sl_ring = [S.slot() for _ in range(NR)]
                rcount = [0]
                actT = sb("actT", [128, 22, 512], BF16, ph)
                T_act = [Trk() for _ in range(11)]
                Gbc = [sb("Gbc%d" % i, [128, 1024], F32, ph) for i in range(2)]
                T_G = [Trk(), Trk()]
                tmp = Tmp(ph, 4, "tmpF")
                dg = Tmp(ph, 2, "dg")
                st2 = [sb("st2_%d" % i, [128, 8], F32, ph) for i in range(2)]
                T_st2 = [Trk(), Trk()]
                T = gr.T
                nt = len(gr.tts)
                for w in range(2):
                    for half in range(2):
                        pg, tpg = nb()
                        for q in range(4):
                            k = half * 4 + q
                            d_, td_ = dg.get()
                            S.op("dve", lambda e: e.tensor_scalar(out=d_[:, 0:128], in0=ident, scalar1=Gc_t[:, w, gr.j, k:k + 1], scalar2=None, op0=mult),
                                 reads=[T_c, T_A], writes=[td_])
                            S.op("pe", lambda e: e.matmul(pg[:, q * 128:(q + 1) * 128], onesf, d_[:, 0:128], start=True, stop=True),
                                 reads=[td_, T_c], writes=[tpg])
                        S.op("act", lambda e: e.activation(out=Gbc[w][:, half * 512:(half + 1) * 512], in_=pg[:, :], func=AF.Copy),
                             reads=[tpg], writes=[T_G[w]])

                def post(w, i, tt, pa, tpa, pb2, tpb2):
                    s_ = st2[i % 2]
                    ts_ = T_st2[i % 2]
                    jk, tjk = tmp.get()
                    S.op("act", lambda e: e.activation(out=jk[:, :], in_=pa[:, :], func=AF.Square, accum_out=s_[:, 0:1]), reads=[tpa], writes=[tjk, ts_])
                    jk2, tjk2 = tmp.get()
                    S.op("act", lambda e: e.activation(out=jk2[:, :], in_=pb2[:, :], func=AF.Square, accum_out=s_[:, 1:2]), reads=[tpb2], writes=[tjk2, ts_])
                    S.op("dve", lambda e: e.tensor_tensor(out=s_[:, 2:3], in0=s_[:, 0:1], in1=s_[:, 1:2], op=add), reads=[ts_], writes=[ts_])
                    S.op("act", lambda e: e.activation(out=s_[:, 3:4], in_=s_[:, 2:3], func=AF.Ln, scale=1.0 / 1024, bias=EPS),
                         reads=[ts_], writes=[ts_])
                    S.op("act", lambda e: e.activation(out=s_[:, 4:5], in_=s_[:, 3:4], func=AF.Exp, scale=-0.5), reads=[ts_], writes=[ts_])
                    for half, (p_, t_) in enumerate(((pa, tpa), (pb2, tpb2))):
                        tm, ttm = tmp.get()
                        S.op("dve", lambda e: e.scalar_tensor_tensor(out=tm[:, :], in0=p_[:, :], scalar=s_[:, 4:5], in1=Gbc[w][:, half * 512:(half + 1) * 512],
                                                                      op0=mult, op1=mult), reads=[t_, ts_, T_G[w]], writes=[ttm])
                        S.op("pool", lambda e: e.tensor_tensor(out=xs[:, tt, half * 512:(half + 1) * 512], in0=xs[:, tt, half * 512:(half + 1) * 512],
                                                               in1=tm[:, :], op=add), reads=[ttm], writes=[T_x[tt]])

                for i, tt in enumerate(gr.tts):
                    pa, tpa = nb()
                    pb2, tpb2 = nb()
                    for k in range(8):
                        for (p_, t_, half) in ((pa, tpa, 0), (pb2, tpb2, 1)):
                            S.op("pe", lambda e: e.matmul(p_[:, :], yT[:, k, i * 128:(i + 1) * 128], wout[:, k, half * 512:(half + 1) * 512],
                                                         start=(k == 0), stop=(k == 7)), reads=[T_yT[2 * k], T_yT[2 * k + 1], T_wout], writes=[t_])
                    post(0, i, tt, pa, tpa, pb2, tpb2)
                if l == 0:
                    dump("xa", xs[:, 0:4, :], [128, 4, 1024], F32)
                prep_hT(prep, l, gr, 1)

                def ring_load(name, jj):
                    ri = rcount[0] % NR
                    rcount[0] += 1
                    S.dma("sp", sl_ring[ri], ringb[ri][:], wb[l][name].ap()[jj], reads=[T_wb[l][name]], writes=[T_ring[ri]])
                    return ri

                for jj in range(11):
                    rg = ring_load("wg", jj)
                    ru = ring_load("wu", jj)
                    wgv = ringb[rg][:].rearrange("p (k n) -> p k n", k=8)
                    wuv = ringb[ru][:].rearrange("p (k n) -> p k n", k=8)
                    for s_ in range(2):
                        j = jj * 2 + s_
                        pg, tpg = nb()
                        pu, tpu = nb()
                        for k in range(8):
                            S.op("pe", lambda e: e.matmul(pg[:, 0:T], wgv[:, k, s_ * 128:(s_ + 1) * 128], hT[:, k, 0:T], start=(k == 0), stop=(k == 7)),
                                 reads=[T_ring[rg], T_hT], writes=[tpg])
                        for k in range(8):
                            S.op("pe", lambda e: e.matmul(pu[:, 0:T], wuv[:, k, s_ * 128:(s_ + 1) * 128], hT[:, k, 0:T], start=(k == 0), stop=(k == 7)),
                                 reads=[T_ring[ru], T_hT], writes=[tpu])
                        sg, tsg = tmp.get()
                        S.op("act", lambda e: e.activation(out=sg[:, 0:T], in_=pg[:, 0:T], func=AF.Silu), reads=[tpg], writes=[tsg])
                        S.op("dve", lambda e: e.tensor_tensor(out=actT[:, j, 0:T], in0=sg[:, 0:T], in1=pu[:, 0:T], op=mult),
                             reads=[tsg, tpu], writes=[T_act[jj]])
                rds = []
                for jj in range(11):
                    rds.append(None)
                for i0 in range(0, nt, 4):
                    tis = list(range(i0, min(i0 + 4, nt)))
                    acc = {}
                    for i in tis:
                        acc[i] = ((ps[2 * i], T_ps[2 * i]), (ps[2 * i + 1], T_ps[2 * i + 1]))
                    for jj in range(11):
                        rd = ring_load("wd", jj)
                        wdv = ringb[rd][:].rearrange("p (s n) -> p s n", s=2)
                        for s_ in range(2):
                            j = jj * 2 + s_
                            for i in tis:
                                for half in range(2):
                                    p_, t_ = acc[i][half]
                                    S.op("pe", lambda e: e.matmul(p_[:, :], actT[:, j, i * 128:(i + 1) * 128], wdv[:, s_, half * 512:(half + 1) * 512],
                                                                 start=(j == 0), stop=(j == 21)), reads=[T_act[jj], T_ring[rd]], writes=[t_])
                    for i in tis:
                        (pa, tpa), (pb2, tpb2) = acc[i]
                        post(1, i, gr.tts[i], pa, tpa, pb2, tpb2)
                S.barrier()

        S.barrier()
        for l in range(NL):
            last = l == NL - 1
            layer_consts(l)
            phase_A(l, [cg] + groups)
            gather(l)
            if l + 1 < NL:
                phase_M(l + 1)
            if l == 0:
                dump("kc", kc_loc[0].ap(), [672, 256], BF16)
                dump("vc", vc_loc[0].ap(), [1280, 132], BF16)
                dump("modT", modT[:], [128, NL * 96], F32)
                dump("misc", misc[:], [128, 64], F32)
            glist = groups + ([] if last else [cg])
            waited = False
            for gr in glist:
                phase_B1(l, gr)
                if not gr.ctx and not waited:
                    gather_wait()
                    waited = True
                if l == 0:
                    for i in range(20):
                        dump("QT%d" % i, QT[i][:], [128, 512], BF16)
                phase_B2(l, gr)
                if l == 0:
                    dump("yT", yT[:], [128, 8, 512], BF16)
                phase_B3(l, gr)
                if l == 0:
                    dump("x1", xs[:, 0:4, :], [128, 4, 1024], F32)
        S.barrier()
        for g4 in range(NG):
            S.dma("sp", sl_out, out_d[g4 * 512:(g4 + 1) * 512, :].rearrange("(t p) d -> p t d", p=128), xs[:, g4 * 4:(g4 + 1) * 4, :],
                  reads=T_x[g4 * 4:(g4 + 1) * 4])
        S.barrier()
    return nc


def _swap(w, r):
    n = w.shape[-1]
    idx = np.arange(n).reshape(-1, r)
    idx = np.concatenate([idx[:, r // 2:], idx[:, :r // 2]], axis=1).reshape(-1)
    return w[..., idx]


def _rope_tabs(pos, n_ctx):
    row = (pos // 64).astype(np.float32)
    col = (pos % 64).astype(np.float32)

    def tab(rot):
        quarter = rot // 4
        inv = (np.float32(10000.0) ** (-np.arange(quarter, dtype=np.float32) / np.float32(quarter))).astype(np.float32)
        ang = np.concatenate([row[:, None] * inv, col[:, None] * inv], axis=-1).astype(np.float32)
        return np.cos(ang).astype(np.float32), np.sin(ang).astype(np.float32)

    out = []
    for rot in (32, 64):
        c, s = tab(rot)
        C = np.concatenate([c, c], 1).T
        Sg = np.concatenate([-s, s], 1).T
        C = np.tile(C, (128 // rot, 1))
        Sg = np.tile(Sg, (128 // rot, 1))
        C = np.concatenate([C, np.ones((128, n_ctx), np.float32)], 1)
        Sg = np.concatenate([Sg, np.zeros((128, n_ctx), np.float32)], 1)
        out += [C, Sg]
    return np.ascontiguousarray(np.stack(out, 0), dtype=np.float32)


def _chunk_rows(w):
    L, R, N = w.shape
    return np.ascontiguousarray(w.reshape(L, R // 128, 128, N).transpose(0, 2, 1, 3))


def prep_shared(inp):
    f = lambda a: np.asarray(a, dtype=np.float32)
    L = f(inp["w_ada"]).shape[0]
    w_in = f(inp["w_in"])
    qa, ckv, kpe = w_in[..., 0:256], w_in[..., 256:384], w_in[..., 384:416]
    dq, dk, dv = w_in[..., 416:672], w_in[..., 672:928], w_in[..., 928:1184]
    gq, gk, gv = w_in[..., 1184:1696], w_in[..., 1696:1824], w_in[..., 1824:1952]
    A = [ckv, kpe, _swap(kpe, 32)]
    for b in range(2):
        blk = dk[..., b * 128:(b + 1) * 128]
        A += [blk, _swap(blk, 32)]
    A += [gk, _swap(gk, 64), dv, gv]
    wA = _chunk_rows(np.concatenate(A, -1))
    Q = [qa]
    for b in range(2):
        blk = dq[..., b * 128:(b + 1) * 128]
        Q += [blk, _swap(blk, 32)]
    for i in range(4):
        blk = np.concatenate([gq[..., i * 64:(i + 1) * 64], gq[..., (4 + i) * 64:(5 + i) * 64]], -1)
        Q += [blk, _swap(blk, 64)]
    wQ = _chunk_rows(np.concatenate(Q, -1))
    wqb = f(inp["w_mla_qb"])
    qb = []
    for h in range(4):
        blk = wqb[..., h * 96:(h + 1) * 96]
        qb += [blk, np.concatenate([blk[..., 0:64], _swap(blk[..., 64:96], 32)], -1)]
    wqb_r = _chunk_rows(np.concatenate(qb, -1))
    wkvb = f(inp["w_mla_kvb"]).reshape(L, 128, 4, 128)
    wkvb_r = np.ascontiguousarray(np.concatenate([wkvb[..., 0:64].reshape(L, 128, 256), wkvb[..., 64:128].reshape(L, 128, 256)], -1))
    wout_r = _chunk_rows(f(inp["w_out"]))
    w_ada = f(inp["w_ada"])
    wada_r = np.ascontiguousarray(w_ada.reshape(L, 8, 128, 12, 512).transpose(0, 3, 2, 1, 4).reshape(L, 12, 128, 4096))
    wg = f(inp["w_ffn_gate"]).reshape(L, 8, 128, 11, 256).transpose(0, 3, 2, 1, 4).reshape(L, 11, 128, 2048)
    wu = f(inp["w_ffn_up"]).reshape(L, 8, 128, 11, 256).transpose(0, 3, 2, 1, 4).reshape(L, 11, 128, 2048)
    wd = f(inp["w_ffn_down"]).reshape(L, 11, 2, 128, 1024).transpose(0, 1, 3, 2, 4).reshape(L, 11, 128, 2048)
    colv = np.zeros((L, 128, NV), np.float32)

    def cols(v):
        return v.reshape(L, -1, 128).transpose(0, 2, 1)

    colv[:, :, 0:8] = cols(f(inp["g_attn_pre"]))
    colv[:, :, 8:16] = cols(f(inp["g_attn_post"]))
    colv[:, :, 16:24] = cols(f(inp["g_ffn_pre"]))
    colv[:, :, 24:32] = cols(f(inp["g_ffn_post"]))
    colv[:, :, 32:80] = cols(f(inp["b_ada"]))
    colv[:, :, 80:82] = cols(f(inp["g_mla_q"]))
    colv[:, :, 82] = f(inp["g_mla_kv"])
    ggq, ggk = f(inp["g_gqa_q"]), f(inp["g_gqa_k"])
    colv[:, :, 83] = np.tile(ggq, (1, 2))
    colv[:, :, 84] = np.tile(_swap(ggq, 64), (1, 2))
    colv[:, :, 85] = np.tile(ggk, (1, 2))
    colv[:, :, 86] = np.tile(_swap(ggk, 64), (1, 2))
    colv[:, :, 87] = np.tile(f(inp["g_diff_sub"]), (1, 2))
    for i, nm in enumerate(("lambda_q1", "lambda_k1", "lambda_q2", "lambda_k2")):
        colv[:, :, 88 + 32 * i:120 + 32 * i] = f(inp[nm])[:, None, :]
    consts = np.zeros((128, 384), np.float32)
    consts[:, 0:128] = np.eye(128, dtype=np.float32)
    consts[:, 128:256] = 1.0
    consts[0:64, 256:320] = 1.0
    consts[64:128, 320:384] = 1.0
    return dict(w_ada_r=wada_r, colv=colv, w_inA=wA, w_inQ=wQ, w_qb_r=wqb_r, w_kvb_r=wkvb_r, w_out_r=wout_r,
                wg_r=np.ascontiguousarray(wg), wu_r=np.ascontiguousarray(wu), wd_r=np.ascontiguousarray(wd), consts=consts)


def make_in_maps(inp, NG):
    shared = prep_shared(inp)
    x = np.asarray(inp["x"], np.float32)
    c = np.asarray(inp["c"], np.float32)
    ctx = np.asarray(inp["ctx"], np.float32)
    c_ctx = np.asarray(inp["c_ctx"], np.float32)
    NTOK = NG * 512
    maps = []
    for core in range(8):
        b, q = core // 4, core % 4
        m = dict(shared)
        m["x_own"] = np.ascontiguousarray(x[b, q * NTOK:(q + 1) * NTOK, :])
        m["ctx_b"] = np.ascontiguousarray(ctx[b])
        cv = np.stack([c[b].reshape(8, 128).T, c_ctx.reshape(8, 128).T], -1)
        m["cvec"] = np.ascontiguousarray(cv.reshape(128, 16))
        m["rope"] = _rope_tabs(np.arange(q * NTOK, (q + 1) * NTOK), 256)
        maps.append(m)
    return maps


_NC_CACHE = {}


def run(inp, NG, ret_res=False):
    if NG not in _NC_CACHE:
        _NC_CACHE[NG] = build(NG)
    nc = _NC_CACHE[NG]
    maps = make_in_maps(inp, NG)
    res = run_bass_kernel_spmd(nc, maps, core_ids=list(range(8)))
    if ret_res:
        return res
    NTOK = NG * 512
    out = np.zeros((2, 4 * NTOK, 1024), np.float32)
    for core in range(8):
        b, q = core // 4, core % 4
        out[b, q * NTOK:(q + 1) * NTOK, :] = res.results[core]["out"]
    return out


def kernel(**inputs):
    return run(inputs, 4)
```

```python
import bisect
import math
from contextlib import ExitStack

import numpy as np
import concourse.bass as bass
import concourse.mybir as mybir
from concourse.bass_utils import run_bass_kernel_spmd

F32 = mybir.dt.float32
BF16 = mybir.dt.bfloat16
AF = mybir.ActivationFunctionType
ALU = mybir.AluOpType
EPS = 1e-6
NV = 216
SC_MLA = 1.0 / math.sqrt(96.0)
SC_DIFF = 1.0 / math.sqrt(32.0)
SC_GQA = 1.0 / 8.0


class Trk:
    __slots__ = ("w", "r")

    def __init__(self):
        self.w = None
        self.r = []


class Slot:
    pass


class Sched:
    COMPUTE = ("pe", "act", "dve", "pool")

    def __init__(self, nc, es):
        self.nc = nc
        self.es = es
        self.eng = {"pe": nc.tensor, "act": nc.scalar, "dve": nc.vector, "pool": nc.gpsimd, "sp": nc.sync}
        self.sem = {e: es.enter_context(nc.semaphore("s_" + e)) for e in self.COMPUTE}
        self.n = {e: 0 for e in self.eng}
        self.last = {e: None for e in self.eng}
        self.sig = {e: [] for e in self.COMPUTE}
        self.known = {e: {} for e in self.eng}
        self.slots = []
        self.pool = []
        self.pidx = 0
        self.nwait = 0

    def _sigval(self, e, idx):
        s = self.sig[e]
        p = bisect.bisect_left(s, idx)
        if p < len(s):
            return p + 1
        li = self.n[e] - 1
        assert li >= idx
        self.last[e].then_inc(self.sem[e], 1)
        s.append(li)
        return len(s)

    def _wait(self, q, dep):
        if dep[0] == "c":
            _, e, idx = dep
            if e == q and q == "pe":
                return
            key = ("c", e)
            kv = self.known[q].get(key, 0)
            s = self.sig[e]
            p = bisect.bisect_left(s, idx)
            if p < len(s) and p + 1 <= kv:
                return
            v = self._sigval(e, idx)
            if v <= kv:
                return
            self.eng[q].wait_ge(self.sem[e], v)
            self.nwait += 1
            self.known[q][key] = v
        else:
            _, sl, val = dep
            val = sl.cnt
            key = ("d", id(sl))
            if self.known[q].get(key, 0) >= val:
                return
            self.eng[q].wait_ge(sl.sem, val)
            self.nwait += 1
            self.known[q][key] = val

    def _deps(self, q, reads, writes):
        for t in reads:
            if t.w is not None:
                self._wait(q, t.w)
        for t in writes:
            if t.w is not None:
                self._wait(q, t.w)
            for d in t.r:
                self._wait(q, d)

    def op(self, q, fn, reads=(), writes=(), sig=False):
        self._deps(q, reads, writes)
        ins = fn(self.eng[q])
        idx = self.n[q]
        self.n[q] += 1
        self.last[q] = ins
        if sig:
            ins.then_inc(self.sem[q], 1)
            self.sig[q].append(idx)
        me = ("c", q, idx)
        for t in reads:
            t.r.append(me)
        for t in writes:
            t.w = me
            t.r = []
        return ins

    def pslot(self):
        sl = Slot()
        sl.sem = self.es.enter_context(self.nc.semaphore("d%d" % len(self.slots)))
        sl.cnt = 0
        self.slots.append(sl)
        return sl

    def slot(self):
        if self.pidx >= len(self.pool):
            self.pool.append(self.pslot())
        sl = self.pool[self.pidx]
        self.pidx += 1
        return sl

    def phase_reset(self):
        self.barrier()
        self.pidx = 0

    def dma(self, q, sl, out, in_, reads=(), writes=()):
        self._deps(q, reads, writes)
        ins = self.eng[q].dma_start(out=out, in_=in_)
        ins.then_inc(sl.sem, 16)
        sl.cnt += 16
        me = ("d", sl, sl.cnt)
        for t in reads:
            t.r.append(me)
        for t in writes:
            t.w = me
            t.r = []
        return ins

    def barrier(self):
        for q in self.eng:
            for e in self.COMPUTE:
                if e != q and self.n[e] > 0:
                    self._wait(q, ("c", e, self.n[e] - 1))
            for sl in self.slots:
                if sl.cnt > 0:
                    self._wait(q, ("d", sl, sl.cnt))


def build(NG=4, NL=2, dbg=False):
    nc = bass.Bass("TRN2", target_bir_lowering=False)
    NTOK = NG * 512
    NTT = NG * 4
    NKR = NG * 4
    mult, add = ALU.mult, ALU.add

    def din(name, shape, dt=F32):
        return nc.dram_tensor(name, shape, dt, kind="ExternalInput").ap()

    x_d = din("x_own", [NTOK, 1024])
    ctx_d = din("ctx_b", [256, 1024])
    cvec_d = din("cvec", [128, 16])
    wada_d = din("w_ada_r", [NL, 12, 128, 8 * 512])
    colv_d = din("colv", [NL, 128, NV])
    wA_d = din("w_inA", [NL, 128, 8, 1344])
    wQ_d = din("w_inQ", [NL, 128, 8, 1792])
    wqb_d = din("w_qb_r", [NL, 128, 2, 768])
    wkvb_d = din("w_kvb_r", [NL, 128, 512])
    wout_d = din("w_out_r", [NL, 128, 8, 1024])
    wg_d = din("wg_r", [NL, 11, 128, 2048])
    wu_d = din("wu_r", [NL, 11, 128, 2048])
    wd_d = din("wd_r", [NL, 11, 128, 2048])
    rope_d = din("rope", [4, 128, NTOK + 256])
    const_d = din("consts", [128, 384])
    out_d = nc.dram_tensor("out", [NTOK, 1024], F32, kind="ExternalOutput").ap()

    KB = [0, 128, 288, 416, 544, 672]
    k_loc = [[nc.dram_tensor("k_loc%d_%d" % (l, s_), [KB[s_ + 1] - KB[s_], NTOK], BF16) for s_ in range(5)] for l in range(NL)]
    k_all = [[nc.dram_tensor("k_all%d_%d" % (l, s_), [4 * (KB[s_ + 1] - KB[s_]), NTOK], BF16) for s_ in range(5)] for l in range(NL)]
    v_loc = [[nc.dram_tensor("v_loc%d_%d" % (l, s_), [256, NKR * 66], BF16) for s_ in range(5)] for l in range(NL)]
    v_all = [[nc.dram_tensor("v_all%d_%d" % (l, s_), [4 * 256, NKR * 66], BF16) for s_ in range(5)] for l in range(NL)]

    wb = []
    for l in range(NL):
        wb.append(dict(
            wA=nc.dram_tensor("wA_b%d" % l, [128, 8 * 1344], BF16), wkvb=nc.dram_tensor("wkvb_b%d" % l, [128, 512], BF16),
            wQ=nc.dram_tensor("wQ_b%d" % l, [128, 8 * 1792], BF16), wqb=nc.dram_tensor("wqb_b%d" % l, [128, 2 * 768], BF16),
            wout=nc.dram_tensor("wout_b%d" % l, [128, 8 * 1024], BF16),
            wg=nc.dram_tensor("wg_b%d" % l, [11, 128, 2048], BF16), wu=nc.dram_tensor("wu_b%d" % l, [11, 128, 2048], BF16),
            wd=nc.dram_tensor("wd_b%d" % l, [11, 128, 2048], BF16)))

    def kblk(r0):
        for s_ in range(5):
            if KB[s_] <= r0 < KB[s_ + 1]:
                return s_, r0 - KB[s_], KB[s_ + 1] - KB[s_]
        raise ValueError
    kc_loc = [nc.dram_tensor("kc_loc%d" % l, [672, 256], BF16) for l in range(NL)]
    vc_loc = [nc.dram_tensor("vc_loc%d" % l, [1280, 2 * 66], BF16) for l in range(NL)]

    with ExitStack() as es:
        S = Sched(nc, es)

        uid = [0]

        def sb(name, shape, dt, stack=None):
            uid[0] += 1
            return (stack or es).enter_context(nc.sbuf_tensor("sb%d_%s" % (uid[0], name), shape, dt))

        xs = sb("xs", [128, NTT + 2, 1024], F32)
        T_x = [Trk() for _ in range(NTT + 2)]
        cst = sb("cst", [128, 384], F32)
        T_c = Trk()
        ident = cst[:, 0:128]
        onesf = cst[:, 128:256]
        bd64 = cst[:, 256:384]
        colv = sb("colv", [128, NL, NV], F32)
        T_colv = Trk()
        modT = sb("modT", [128, NL * 96], F32)
        T_mod = Trk()
        misc = sb("misc", [128, 64], F32)
        T_misc = Trk()
        A_t = sb("A_t", [128, 2, 2, 8], F32)
        Gc_t = sb("Gc_t", [128, 2, 2, 8], F32)
        T_A = Trk()
        QT = [sb("QT%d" % i, [128, 512], BF16) for i in range(20)]
        T_QT = [Trk() for _ in range(20)]
        yT = sb("yT", [128, 8, 512], BF16)
        T_yT = [Trk() for _ in range(16)]

        psd = [es.enter_context(nc.psum_tensor("psd%d" % i, [128, 1024], F32)) for i in range(4)]
        ps = [psd[i // 2][:, (i % 2) * 512:(i % 2 + 1) * 512] for i in range(8)]
        T_ps = [Trk() for _ in range(8)]
        bank_rr = [0]

        def nb(lo=0, hi=8):
            i = lo + (bank_rr[0] % (hi - lo))
            bank_rr[0] += 1
            return ps[i], T_ps[i]

        dbg_done = set()

        def dump(name, src_ap, shape, dt):
            if not dbg or name in dbg_done:
                return
            dbg_done.add(name)
            d = nc.dram_tensor("dbg_" + name, shape, dt, kind="ExternalOutput").ap()
            S.phase_reset()
            sl = S.slot()
            S.dma("sp", sl, d, src_ap)
            S.barrier()

        sl_misc = S.pslot()
        sl_x = S.pslot()
        sl_out = S.pslot()

        def mcol(l, c, j):
            o = (l * 48 + c) * 2 + j
            return modT[:, o:o + 1]

        S.dma("sp", sl_misc, cst[:], const_d, writes=[T_c])
        S.dma("sp", sl_misc, colv[:], colv_d.rearrange("l p n -> p l n"), writes=[T_colv])
        for g4 in range(NG):
            S.dma("sp", sl_x, xs[:, g4 * 4:(g4 + 1) * 4, :], x_d[g4 * 512:(g4 + 1) * 512, :].rearrange("(t p) d -> p t d", p=128),
                  writes=T_x[g4 * 4:(g4 + 1) * 4])
        S.dma("sp", sl_x, xs[:, NTT:NTT + 2, :], ctx_d.rearrange("(t p) d -> p t d", p=128), writes=T_x[NTT:NTT + 2])
        S.op("pool", lambda e: e.memset(misc[:, 0:1], -0.5), writes=[T_misc])
        for i in range(4, 20):
            S.op("pool", lambda e: e.memset(QT[i][:], 0.0), writes=[T_QT[i]])
        mhalf = misc[:, 0:1]

        T_wb = [dict((k_, Trk()) for k_ in wb[l]) for l in range(NL)]
        sl_wb = [dict((k_, S.pslot()) for k_ in wb[l]) for l in range(NL)]
        def convert(l, names):
            def cv(name, dst, src, extra=()):
                S.dma("pool", sl_wb[l][name], dst, src, reads=list(extra), writes=[T_wb[l][name]])
            for name in names:
                if name == "wA":
                    for k in range(8):
                        cv("wA", wb[l]["wA"].ap()[:, k * 1344:(k + 1) * 1344], wA_d[l, :, k, :])
                elif name == "wkvb":
                    cv("wkvb", wb[l]["wkvb"].ap(), wkvb_d[l])
                elif name == "wQ":
                    for k in range(8):
                        cv("wQ", wb[l]["wQ"].ap()[:, k * 1792:(k + 1) * 1792], wQ_d[l, :, k, :])
                elif name == "wqb":
                    for c in range(2):
                        cv("wqb", wb[l]["wqb"].ap()[:, c * 768:(c + 1) * 768], wqb_d[l, :, c, :])
                elif name == "wout":
                    for k in range(8):
                        cv("wout", wb[l]["wout"].ap()[:, k * 1024:(k + 1) * 1024], wout_d[l, :, k, :])
                elif name.startswith("ffn"):
                    j0, j1 = {"ffn": (0, 11), "ffn0": (0, 4), "ffn1": (4, 8), "ffn2": (8, 11)}[name]
                    for jj in range(j0, j1):
                        cv("wg", wb[l]["wg"].ap()[jj], wg_d[l, jj])
                        cv("wu", wb[l]["wu"].ap()[jj], wu_d[l, jj])
                    for jj in range(j0, j1):
                        cv("wd", wb[l]["wd"].ap()[jj], wd_d[l, jj])

        conv_queue = []

        def convert_lazy(l, names):
            real = S.dma

            def rec(*a, **k):
                conv_queue.append((a, k))
            S.dma = rec
            try:
                convert(l, names)
            finally:
                S.dma = real

        def conv_pop(n=1):
            for _ in range(n):
                if conv_queue:
                    a, k = conv_queue.pop(0)
                    S.dma(*a, **k)

        convert(0, ["wA", "wkvb"])

        def phase_M(l, after=None):
            S.phase_reset()
            with ExitStack() as ph:
                scv = sb("scv", [128, 16], F32, ph)
                T_scv = Trk()
                wad = [sb("wad%d" % i, [128, 8, 512], F32, ph) for i in range(3)]
                T_wad = [Trk(), Trk(), Trk()]
                sl_wad = [S.slot(), S.slot(), S.slot()]
                sl_c = S.slot()
                S.dma("sp", sl_c, scv[:], cvec_d, writes=[T_scv])
                S.op("act", lambda e: e.activation(out=scv[:], in_=scv[:], func=AF.Silu), reads=[T_scv], writes=[T_scv])
                modrow = sb("modrow", [2, 6144], F32, ph)
                T_mr = Trk()
                for cb in range(12):
                    bi = cb % 3
                    S.dma("sp", sl_wad[bi], wad[bi][:], wada_d[l, cb].rearrange("p (k n) -> p k n", k=8), writes=[T_wad[bi]])
                    if cb == 11 and after is not None:
                        after(T_wad[bi])
                    pr, tpr = nb()
                    for k in range(8):
                        S.op("pe", lambda e: e.matmul(pr[0:2, :], scv[:, 2 * k:2 * k + 2], wad[bi][:, k, :], start=(k == 0), stop=(k == 7)),
                             reads=[T_wad[bi], T_scv], writes=[tpr])
                    S.op("dve", lambda e: e.tensor_copy(out=modrow[0:2, cb * 512:(cb + 1) * 512], in_=pr[0:2, :]), reads=[tpr], writes=[T_mr])
                pm, tpm = nb()
                for c in range(48):
                    S.op("pe", lambda e: e.matmul(pm[:, 2 * c:2 * c + 2], modrow[0:2, c * 128:(c + 1) * 128], ident[0:2, 0:2], start=True, stop=True),
                         reads=[T_mr, T_c], writes=[tpm])
                for j in range(2):
                    S.op("dve", lambda e: e.tensor_tensor(
                        out=modT[:, l * 96:(l + 1) * 96].rearrange("p (c j) -> p c j", j=2)[:, :, j],
                        in0=pm[:, 0:96].rearrange("p (c j) -> p c j", j=2)[:, :, j],
                        in1=colv[:, l, 32:80], op=add), reads=[tpm, T_colv], writes=[T_mod])
                S.barrier()

        def rest_conversions(t_last):
            S._deps("pool", [t_last], [])
            convert(0, ["wQ", "wqb"])

        phase_M(0, after=rest_conversions)

        def layer_consts(l):
            lam_init = 0.8 - 0.6 * math.exp(-0.3 * l)
            for w in range(2):
                for j in range(2):
                    sc0 = (l * 48 + (1 + 3 * w) * 8) * 2
                    gt0 = (l * 48 + (2 + 3 * w) * 8) * 2
                    sc_v = modT[:, sc0:sc0 + 16].rearrange("p (k j) -> p k j", j=2)[:, :, j]
                    gt_v = modT[:, gt0:gt0 + 16].rearrange("p (k j) -> p k j", j=2)[:, :, j]
                    gpre = colv[:, l, 16 * w:16 * w + 8]
                    gpost = colv[:, l, 16 * w + 8:16 * w + 16]
                    S.op("dve", lambda e: e.scalar_tensor_tensor(out=A_t[:, w, j, :], in0=sc_v, scalar=1.0, in1=gpre,
                                                                  op0=add, op1=mult), reads=[T_mod, T_colv], writes=[T_A])
                    S.op("dve", lambda e: e.tensor_tensor(out=Gc_t[:, w, j, :], in0=gt_v, in1=gpost, op=mult),
                         reads=[T_mod, T_colv], writes=[T_A])
            S.op("dve", lambda e: e.tensor_tensor(out=misc[:, 8:40], in0=colv[:, l, 88:120], in1=colv[:, l, 120:152], op=mult),
                 reads=[T_colv, T_misc], writes=[T_misc])
            S.op("dve", lambda e: e.reduce_sum(out=misc[:, 4:5], in_=misc[:, 8:40], axis=mybir.AxisListType.X),
                 reads=[T_misc], writes=[T_misc])
            S.op("dve", lambda e: e.tensor_tensor(out=misc[:, 8:40], in0=colv[:, l, 152:184], in1=colv[:, l, 184:216], op=mult),
                 reads=[T_colv, T_misc], writes=[T_misc])
            S.op("dve", lambda e: e.reduce_sum(out=misc[:, 5:6], in_=misc[:, 8:40], axis=mybir.AxisListType.X),
                 reads=[T_misc], writes=[T_misc])
            S.op("act", lambda e: e.activation(out=misc[:, 4:6], in_=misc[:, 4:6], func=AF.Exp), reads=[T_misc], writes=[T_misc])
            S.op("dve", lambda e: e.scalar_tensor_tensor(out=misc[:, 1:2], in0=misc[:, 5:6], scalar=-lam_init, in1=misc[:, 4:5],
                                                          op0=add, op1=ALU.subtract), reads=[T_misc], writes=[T_misc])
            S.op("dve", lambda e: e.tensor_scalar(out=misc[:, 2:3], in0=colv[:, l, 87:88], scalar1=(1.0 - lam_init), scalar2=None,
                                                  op0=mult), reads=[T_colv, T_misc], writes=[T_misc])

        class G:
            pass

        groups = []
        for g in range(NG):
            gr = G()
            gr.tts = [g * 4 + i for i in range(4)]
            gr.T = 512
            gr.j = 0
            gr.tab0 = g * 512
            gr.g = g
            gr.ctx = False
            groups.append(gr)
        cg = G()
        cg.tts = [NTT, NTT + 1]
        cg.T = 256
        cg.j = 1
        cg.tab0 = NTOK
        cg.g = None
        cg.ctx = True

        def prep_hT(ph_t, l, gr, w):
            hT, T_hT, xn, T_xn, junk, T_junk, st, T_st = ph_t
            for i, tt in enumerate(gr.tts):
                b = i % 2
                s_ = st[i % 4]
                ts_ = T_st[i % 4]
                S.op("act", lambda e: e.activation(out=junk[:], in_=xs[:, tt, :], func=AF.Square, accum_out=s_[:, 0:1]),
                     reads=[T_x[tt]], writes=[T_junk, ts_])
                S.op("act", lambda e: e.activation(out=s_[:, 1:2], in_=s_[:, 0:1], func=AF.Ln, scale=1.0 / 1024, bias=EPS),
                     reads=[ts_], writes=[ts_])
                S.op("act", lambda e: e.activation(out=s_[:, 2:3], in_=s_[:, 1:2], func=AF.Exp, scale=-0.5),
                     reads=[ts_], writes=[ts_])
                S.op("dve", lambda e: e.tensor_scalar(out=xn[b][:], in0=xs[:, tt, :], scalar1=s_[:, 2:3], scalar2=None, op0=mult),
                     reads=[T_x[tt], ts_], writes=[T_xn[b]])
                for half in range(2):
                    pt, tpt = nb()
                    for q in range(4):
                        k = half * 4 + q
                        S.op("pe", lambda e: e.transpose(pt[:, q * 128:(q + 1) * 128], xn[b][:, k * 128:(k + 1) * 128], ident),
                             reads=[T_xn[b], T_c], writes=[tpt])
                    for q in range(4):
                        k = half * 4 + q
                        acol = A_t[:, w, gr.j, k:k + 1]
                        bcol = mcol(l, (3 * w) * 8 + k, gr.j)
                        o_ = hT[:, k, i * 128:(i + 1) * 128]
                        i_ = pt[:, q * 128:(q + 1) * 128]
                        if q % 2 == 0:
                            S.op("act", lambda e: e.activation(out=o_, in_=i_, func=AF.Identity, scale=acol, bias=bcol),
                                 reads=[tpt, T_A, T_mod], writes=[T_hT])
                        else:
                            S.op("dve", lambda e: e.tensor_scalar(out=o_, in0=i_, scalar1=acol, scalar2=bcol, op0=mult, op1=add),
                                 reads=[tpt, T_A, T_mod], writes=[T_hT])

        def alloc_prep(ph):
            hT = sb("hT", [128, 8, 512], BF16, ph)
            xn = [sb("xn%d" % i, [128, 1024], F32, ph) for i in range(2)]
            junk = sb("junk", [128, 1024], BF16, ph)
            st = [sb("st%d" % i, [128, 4], F32, ph) for i in range(4)]
            return (hT, Trk(), xn, [Trk(), Trk()], junk, Trk(), st, [Trk() for _ in range(4)])

        def proj_block(W, T_W, c0, wd, hT, T_hT, T):
            pb, tpb = nb()
            for k in range(8):
                S.op("pe", lambda e: e.matmul(pb[0:wd, 0:T], W[:, k, c0:c0 + wd], hT[:, k, 0:T], start=(k == 0), stop=(k == 7)),
                     reads=[T_W, T_hT], writes=[tpb])
            return pb, tpb

        class Tmp:
            def __init__(self, ph, n, name, dt=F32):
                self.t = [sb("%s%d" % (name, i), [128, 512], dt, ph) for i in range(n)]
                self.k = [Trk() for _ in range(n)]
                self.i = 0

            def get(self):
                i = self.i % len(self.t)
                self.i += 1
                return self.t[i], self.k[i]

        def rstd_bcast(tmp, src_list, lhs, n_feat, T, rows=128):
            pst, tpst = nb()
            for i, (p_, t_) in enumerate(src_list):
                sq, tsq = tmp.get()
                S.op("act", lambda e: e.activation(out=sq[0:rows, 0:T], in_=p_[0:rows, 0:T], func=AF.Square), reads=[t_], writes=[tsq])
                S.op("pe", lambda e: e.matmul(pst[0:rows, 0:T], lhs[0:rows, 0:rows], sq[0:rows, 0:T], start=(i == 0), stop=(i == len(src_list) - 1)),
                     reads=[tsq, T_c], writes=[tpst])
            r, tr = tmp.get()
            S.op("act", lambda e: e.activation(out=r[0:rows, 0:T], in_=pst[0:rows, 0:T], func=AF.Ln, scale=1.0 / n_feat, bias=EPS),
                 reads=[tpst], writes=[tr])
            S.op("act", lambda e: e.activation(out=r[0:rows, 0:T], in_=r[0:rows, 0:T], func=AF.Exp, scale=-0.5), reads=[tr], writes=[tr])
            return r, tr

        def rope_comb(tmp, out_ap, t_out, a, ta, b_, tb, Ct, St, T_tab, r0, r1, T):
            t1, tt1 = tmp.get()
            t2, tt2 = tmp.get()
            S.op("dve", lambda e: e.tensor_tensor(out=t1[r0:r1, 0:T], in0=a[r0:r1, 0:T], in1=Ct[r0:r1, 0:T], op=mult),
                 reads=[ta, T_tab], writes=[tt1])
            S.op("dve", lambda e: e.tensor_tensor(out=t2[r0:r1, 0:T], in0=b_[r0:r1, 0:T], in1=St[r0:r1, 0:T], op=mult),
                 reads=[tb, T_tab], writes=[tt2])
            S.op("dve", lambda e: e.tensor_tensor(out=out_ap, in0=t1[r0:r1, 0:T], in1=t2[r0:r1, 0:T], op=add),
                 reads=[tt1, tt2], writes=[t_out])

        def load_tabs(tabs, T_tab, sl_tab, gr):
            for ti in range(4):
                S.dma("sp", sl_tab, tabs[ti][:, 0:gr.T], rope_d[ti, :, gr.tab0:gr.tab0 + gr.T], writes=[T_tab])

        def gqa_block(tmp, pb, tpb, pbs, tpbs, gcol, gscol, tabs, T_tab, out_ap, t_out, T):
            r, tr = rstd_bcast(tmp, [(pb, tpb)], bd64, 64.0, T)
            xg, txg = tmp.get()
            xw, txw = tmp.get()
            S.op("dve", lambda e: e.scalar_tensor_tensor(out=xg[:, 0:T], in0=pb[:, 0:T], scalar=gcol, in1=r[:, 0:T], op0=mult, op1=mult),
                 reads=[tpb, tr, T_colv], writes=[txg])
            S.op("dve", lambda e: e.scalar_tensor_tensor(out=xw[:, 0:T], in0=pbs[:, 0:T], scalar=gscol, in1=r[:, 0:T], op0=mult, op1=mult),
                 reads=[tpbs, tr, T_colv], writes=[txw])
            rope_comb(tmp, out_ap, t_out, xg, txg, xw, txw, tabs[2], tabs[3], T_tab, 0, 128, T)

        def phase_A(l, glist):
            S.phase_reset()
            with ExitStack() as ph:
                prep = alloc_prep(ph)
                hT, T_hT = prep[0], prep[1]
                wA = sb("wA", [128, 8, 1344], BF16, ph)
                T_wA = Trk()
                wkvb = sb("wkvb", [128, 512], BF16, ph)
                T_wkvb = Trk()
                sl_w = S.slot()
                S.dma("sp", sl_w, wA[:], wb[l]["wA"].ap().rearrange("p (k n) -> p k n", k=8), reads=[T_wb[l]["wA"]], writes=[T_wA])
                S.dma("sp", sl_w, wkvb[:], wb[l]["wkvb"].ap(), reads=[T_wb[l]["wkvb"]], writes=[T_wkvb])
                tabs = [sb("tab%d" % i, [128, 512], F32, ph) for i in range(4)]
                T_tab = Trk()
                sl_tab = S.slot()
                tmp = Tmp(ph, 8, "tmpA")
                ckvn = sb("ckvn", [128, 512], BF16, ph)
                T_ckvn = Trk()
                kst = Tmp(ph, 3, "kst", BF16)
                sl_kst = [S.slot() for _ in range(3)]
                vst = sb("vst", [128, 10, 4, 66], BF16, ph)
                T_vst = Trk()
                sl_vst = S.slot()
                S.op("dve", lambda e: e.memset(vst[:, :, :, 64:65], 1.0), writes=[T_vst])
                S.op("dve", lambda e: e.memset(vst[:, :, :, 65:66], 0.0), writes=[T_vst])
                T_dram = Trk()

                def kout(tile_, ttile, idx, r0, nrows, gr, prow0=0):
                    if gr.ctx:
                        dst = kc_loc[l].ap()[r0:r0 + nrows, 0:256]
                    else:
                        s_, off, _ = kblk(r0)
                        dst = k_loc[l][s_].ap()[off:off + nrows, gr.g * 512:(gr.g + 1) * 512]
                    S.dma("sp", sl_kst[idx], dst, tile_[prow0:prow0 + nrows, 0:gr.T], reads=[ttile], writes=[])

                for gr in glist:
                    T = gr.T
                    nt = len(gr.tts)
                    prep_hT(prep, l, gr, 0)
                    load_tabs(tabs, T_tab, sl_tab, gr)
                    pb, tpb = proj_block(wA, T_wA, 0, 128, hT, T_hT, T)
                    r, tr = rstd_bcast(tmp, [(pb, tpb)], onesf, 128.0, T)
                    S.op("dve", lambda e: e.scalar_tensor_tensor(out=ckvn[:, 0:T], in0=pb[:, 0:T], scalar=colv[:, l, 82:83], in1=r[:, 0:T],
                                                                  op0=mult, op1=mult), reads=[tpb, tr, T_colv], writes=[T_ckvn])
                    for hp in range(2):
                        pk, tpk = nb()
                        S.op("pe", lambda e: e.matmul(pk[:, 0:T], wkvb[:, hp * 128:(hp + 1) * 128], ckvn[:, 0:T], start=True, stop=True),
                             reads=[T_wkvb, T_ckvn], writes=[tpk])
                        idx = kst.i % 3
                        ks, tks = kst.get()
                        S.op("act", lambda e: e.activation(out=ks[:, 0:T], in_=pk[:, 0:T], func=AF.Copy), reads=[tpk], writes=[tks])
                        kout(ks, tks, idx, hp * 128, 128, gr)
                    for i in range(nt):
                        pv, tpv = nb()
                        S.op("pe", lambda e: e.matmul(pv[:, 0:256], ckvn[:, i * 128:(i + 1) * 128], wkvb[:, 256:512], start=True, stop=True),
                             reads=[T_wkvb, T_ckvn], writes=[tpv])
                        S.op("dve", lambda e: e.tensor_copy(out=vst[:, 0:4, i, 0:64], in_=pv[:, 0:256].rearrange("p (u d) -> p u d", d=64)),
                             reads=[tpv], writes=[T_vst])
                    pb, tpb = proj_block(wA, T_wA, 128, 32, hT, T_hT, T)
                    pbs, tpbs = proj_block(wA, T_wA, 160, 32, hT, T_hT, T)
                    idx = kst.i % 3
                    ks, tks = kst.get()
                    rope_comb(tmp, ks[0:32, 0:T], tks, pb, tpb, pbs, tpbs, tabs[0], tabs[1], T_tab, 0, 32, T)
                    kout(ks, tks, idx, 256, 32, gr)
                    for b in range(2):
                        pb, tpb = proj_block(wA, T_wA, 192 + b * 256, 128, hT, T_hT, T)
                        pbs, tpbs = proj_block(wA, T_wA, 320 + b * 256, 128, hT, T_hT, T)
                        idx = kst.i % 3
                        ks, tks = kst.get()
                        rope_comb(tmp, ks[:, 0:T], tks, pb, tpb, pbs, tpbs, tabs[0], tabs[1], T_tab, 0, 128, T)
                        kout(ks, tks, idx, 288 + b * 128, 128, gr)
                    pb, tpb = proj_block(wA, T_wA, 704, 128, hT, T_hT, T)
                    pbs, tpbs = proj_block(wA, T_wA, 832, 128, hT, T_hT, T)
                    idx = kst.i % 3
                    ks, tks = kst.get()
                    gqa_block(tmp, pb, tpb, pbs, tpbs, colv[:, l, 85:86], colv[:, l, 86:87], tabs, T_tab, ks[:, 0:T], tks, T)
                    kout(ks, tks, idx, 544, 128, gr)
                    for i in range(nt):
                        pv, tpv = nb()
                        for k in range(8):
                            S.op("pe", lambda e: e.matmul(pv[:, 0:384], hT[:, k, i * 128:(i + 1) * 128], wA[:, k, 960:1344],
                                                         start=(k == 0), stop=(k == 7)), reads=[T_wA, T_hT], writes=[tpv])
                        S.op("dve", lambda e: e.tensor_copy(out=vst[:, 4:10, i, 0:64], in_=pv[:, 0:384].rearrange("p (u d) -> p u d", d=64)),
                             reads=[tpv], writes=[T_vst])
                    if gr.ctx:
                        dst = vc_loc[l].ap().rearrange("(u p) (t c) -> p u t c", p=128, c=66)
                        S.dma("sp", sl_vst, dst, vst[:, :, 0:2, :], reads=[T_vst])
                    else:
                        for s_ in range(5):
                            dst = v_loc[l][s_].ap().rearrange("(u p) (t c) -> p u t c", p=128, c=66)[:, :, gr.g * 4:(gr.g + 1) * 4, :]
                            S.dma("sp", sl_vst, dst, vst[:, 2 * s_:2 * s_ + 2, :, :], reads=[T_vst])
                S.barrier()

        cc_sem = es.enter_context(nc.semaphore("cc"))
        cc_cnt = [0]

        def gather(l):
            S.barrier()
            for (a, b_) in list(zip(k_loc[l], k_all[l])) + list(zip(v_loc[l], v_all[l])):
                nc.gpsimd.collective_compute("AllGather", ALU.bypass, replica_groups=[[0, 1, 2, 3], [4, 5, 6, 7]],
                                             ins=[a.ap().opt()], outs=[b_.ap().opt()]).then_inc(cc_sem, 1)
                cc_cnt[0] += 1

        def gather_wait():
            for q in S.eng:
                S.eng[q].wait_ge(cc_sem, cc_cnt[0])

        def phase_B1(l, gr):
            S.phase_reset()
            with ExitStack() as ph:
                prep = alloc_prep(ph)
                hT, T_hT = prep[0], prep[1]
                wQ = sb("wQ", [128, 8, 1792], BF16, ph)
                T_wQ = Trk()
                wqb = sb("wqb", [128, 2, 768], BF16, ph)
                T_wqb = Trk()
                sl_w = S.slot()
                S.dma("sp", sl_w, wQ[:], wb[l]["wQ"].ap().rearrange("p (k n) -> p k n", k=8), reads=[T_wb[l]["wQ"]], writes=[T_wQ])
                S.dma("sp", sl_w, wqb[:], wb[l]["wqb"].ap().rearrange("p (c n) -> p c n", c=2), reads=[T_wb[l]["wqb"]], writes=[T_wqb])
                tabs = [sb("tab%d" % i, [128, 512], F32, ph) for i in range(4)]
                T_tab = Trk()
                sl_tab = S.slot()
                tmp = Tmp(ph, 8, "tmpB")
                qan = sb("qan", [128, 2, 512], BF16, ph)
                T_qan = Trk()
                qfull = Tmp(ph, 2, "qfull", BF16)
                T = gr.T
                prep_hT(prep, l, gr, 0)
                load_tabs(tabs, T_tab, sl_tab, gr)
                p0, tp0 = proj_block(wQ, T_wQ, 0, 128, hT, T_hT, T)
                p1, tp1 = proj_block(wQ, T_wQ, 128, 128, hT, T_hT, T)
                r, tr = rstd_bcast(tmp, [(p0, tp0), (p1, tp1)], onesf, 256.0, T)
                for c, (p_, t_) in enumerate(((p0, tp0), (p1, tp1))):
                    S.op("dve", lambda e: e.scalar_tensor_tensor(out=qan[:, c, 0:T], in0=p_[:, 0:T], scalar=colv[:, l, 80 + c:81 + c], in1=r[:, 0:T],
                                                                  op0=mult, op1=mult), reads=[t_, tr, T_colv], writes=[T_qan])
                for h in range(4):
                    pb, tpb = nb()
                    pbs, tpbs = nb()
                    for c in range(2):
                        S.op("pe", lambda e: e.matmul(pb[0:96, 0:T], wqb[:, c, h * 192:h * 192 + 96], qan[:, c, 0:T], start=(c == 0), stop=(c == 1)),
                             reads=[T_wqb, T_qan], writes=[tpb])
                    for c in range(2):
                        S.op("pe", lambda e: e.matmul(pbs[0:96, 0:T], wqb[:, c, h * 192 + 96:h * 192 + 192], qan[:, c, 0:T], start=(c == 0), stop=(c == 1)),
                             reads=[T_wqb, T_qan], writes=[tpbs])
                    S.op("act", lambda e: e.activation(out=QT[h][0:64, 0:T], in_=pb[0:64, 0:T], func=AF.Copy), reads=[tpb], writes=[T_QT[h]])
                    rope_comb(tmp, QT[h][64:96, 0:T], T_QT[h], pb, tpb, pbs, tpbs, tabs[0], tabs[1], T_tab, 64, 96, T)
                for b in range(2):
                    pb, tpb = proj_block(wQ, T_wQ, 256 + b * 256, 128, hT, T_hT, T)
                    pbs, tpbs = proj_block(wQ, T_wQ, 384 + b * 256, 128, hT, T_hT, T)
                    qf, tqf = qfull.get()
                    rope_comb(tmp, qf[:, 0:T], tqf, pb, tpb, pbs, tpbs, tabs[0], tabs[1], T_tab, 0, 128, T)
                    for j in range(4):
                        m = b * 4 + j
                        S.op("pool", lambda e: e.tensor_copy(out=QT[4 + m][j * 32:(j + 1) * 32, 0:T], in_=qf[j * 32:(j + 1) * 32, 0:T]),
                             reads=[tqf], writes=[T_QT[4 + m]])
                for i in range(4):
                    pb, tpb = proj_block(wQ, T_wQ, 768 + i * 256, 128, hT, T_hT, T)
                    pbs, tpbs = proj_block(wQ, T_wQ, 896 + i * 256, 128, hT, T_hT, T)
                    qf, tqf = qfull.get()
                    gqa_block(tmp, pb, tpb, pbs, tpbs, colv[:, l, 83:84], colv[:, l, 84:85], tabs, T_tab, qf[:, 0:T], tqf, T)
                    for j in range(2):
                        hq = j * 4 + i
                        S.op("pool", lambda e: e.tensor_copy(out=QT[12 + hq][j * 64:(j + 1) * 64, 0:T], in_=qf[j * 64:(j + 1) * 64, 0:T]),
                             reads=[tqf], writes=[T_QT[12 + hq]])
                S.barrier()

        def phase_B2(l, gr):
            S.phase_reset()
            with ExitStack() as ph:
                NSL = 4
                kslot = [sb("kslot%d" % i, [128, NTOK], BF16, ph) for i in range(NSL)]
                vslot = [sb("vslot%d" % i, [128, NKR, 128], BF16, ph) for i in range(NSL)]
                T_ks = [Trk() for _ in range(NSL)]
                T_vs = [Trk() for _ in range(NSL)]
                sl_kv = [S.slot() for _ in range(NSL)]
                for i in range(NSL):
                    S.op("pool", lambda e: e.memset(vslot[i][:, :, 64:128], 0.0), writes=[T_vs[i]])
                Pt = [sb("Pt%d" % i, [128, 2, 512], BF16, ph) for i in range(3)]
                T_P = [Trk() for _ in range(3)]
                osb = Tmp(ph, 3, "osb")
                rec = Tmp(ph, 2, "rec")
                on_ = Tmp(ph, 3, "on")
                dtm = Tmp(ph, 3, "dtm")
                T = gr.T
                ring = [0]
                tcount = [0]
                obank = [0]
                pending = []
                chunks = [("ctx",)] if gr.ctx else [("r", 0), ("r", 1), ("r", 2), ("r", 3), ("ctx",)]

                units = []
                for h in range(4):
                    units.append(dict(q=h, pb=0, d=96, kr=[(h * 64, 64, 0), (256, 32, 64)], vu=h, sc=SC_MLA, kind="plain", u=h))
                for m in range(8):
                    units.append(dict(q=4 + m, pb=0, d=128, kr=[(288 + (m // 4) * 128, 128, 0)], vu=4 + m // 2, sc=SC_DIFF,
                                      kind="diff%d" % (m % 2), u=4 + m // 2))
                for hq in range(8):
                    kvh = hq // 4
                    units.append(dict(q=12 + hq, pb=0, d=128, kr=[(544, 128, 0)], vu=8 + kvh, sc=SC_GQA, kind="plain", u=8 + hq))

                def load_chunk(un, ch):
                    si = ring[0] % NSL
                    ring[0] += 1
                    if ch[0] == "r":
                        rk = ch[1]
                        for (r0, nr, p0) in un["kr"]:
                            s_, off, bn = kblk(r0)
                            S.dma("sp", sl_kv[si], kslot[si][p0:p0 + nr, 0:NTOK], k_all[l][s_].ap()[rk * bn + off:rk * bn + off + nr, :], writes=[T_ks[si]])
                        vu = un["vu"]
                        vsrc = v_all[l][vu // 2].ap()[rk * 256 + (vu % 2) * 128:rk * 256 + (vu % 2 + 1) * 128, :].rearrange("p (t c) -> p t c", c=66)
                        S.dma("sp", sl_kv[si], vslot[si][:, 0:NKR, 0:66], vsrc, writes=[T_vs[si]])
                        return si, NKR
                    else:
                        for (r0, nr, p0) in un["kr"]:
                            S.dma("sp", sl_kv[si], kslot[si][p0:p0 + nr, 0:256], kc_loc[l].ap()[r0:r0 + nr, :], writes=[T_ks[si]])
                        vsrc = vc_loc[l].ap()[un["vu"] * 128:(un["vu"] + 1) * 128, :].rearrange("p (t c) -> p t c", c=66)
                        S.dma("sp", sl_kv[si], vslot[si][:, 0:2, 0:66], vsrc, writes=[T_vs[si]])
                        return si, 2

                if l == 0 and gr.g is not None and NL > 1:
                    if gr.g == 0:
                        convert_lazy(0, ["wout", "ffn"])
                    if NG >= 4:
                        sched = {1: ["wA", "wkvb", "wQ", "wqb", "wout"], 2: ["ffn0", "ffn1"], 3: ["ffn2"]}.get(gr.g, [])
                    else:
                        sched = ["wA", "wkvb", "wQ", "wqb", "wout", "ffn"] if gr.g == 0 else []
                    convert_lazy(1, sched)
                conv_iv = max(1, (len(units_n := [0] * 20) * (len(chunks) * NKR // 2)) // (len(conv_queue) + 1)) if conv_queue else 0
                pair_ctr = [0]
                dstate = {}

                def finalize(un, po, tpo):
                    u = un["u"]
                    ob, tob = osb.get()
                    rc, trc = rec.get()
                    S.op("dve", lambda e: e.reciprocal(out=rc[64:65, 0:T], in_=po[64:65, 0:T]), reads=[tpo], writes=[trc])
                    S.op("act", lambda e: e.activation(out=ob[0:64, 0:T], in_=po[0:64, 0:T], func=AF.Copy), reads=[tpo], writes=[tob])
                    pbc, tpbc = po, tpo
                    S.op("pe", lambda e: e.matmul(pbc[0:64, 0:T], onesf[64:65, 0:64], rc[64:65, 0:T], start=True, stop=True),
                         reads=[trc, T_c], writes=[tpbc])
                    hp = (u % 2) * 64
                    if un["kind"] == "plain":
                        S.op("dve", lambda e: e.tensor_tensor(out=yT[hp:hp + 64, u // 2, 0:T], in0=ob[0:64, 0:T], in1=pbc[0:64, 0:T], op=mult),
                             reads=[tob, tpbc], writes=[T_yT[u]])
                        return
                    o_n, ton = on_.get()
                    S.op("dve", lambda e: e.tensor_tensor(out=o_n[0:64, 0:T], in0=ob[0:64, 0:T], in1=pbc[0:64, 0:T], op=mult),
                         reads=[tob, tpbc], writes=[ton])
                    if un["kind"] == "diff0":
                        dstate[u] = (o_n, ton)
                        return
                    o1, to1 = dstate.pop(u)
                    dd, tdd = dtm.get()
                    S.op("dve", lambda e: e.scalar_tensor_tensor(out=dd[0:64, 0:T], in0=o_n[0:64, 0:T], scalar=misc[0:64, 1:2], in1=o1[0:64, 0:T],
                                                                  op0=mult, op1=add), reads=[ton, to1, T_misc], writes=[tdd])
                    sq, tsq = dtm.get()
                    S.op("act", lambda e: e.activation(out=sq[0:64, 0:T], in_=dd[0:64, 0:T], func=AF.Square), reads=[tdd], writes=[tsq])
                    pss, tpss = po, tpo
                    S.op("pe", lambda e: e.matmul(pss[0:64, 0:T], onesf[0:64, 0:64], sq[0:64, 0:T], start=True, stop=True),
                         reads=[tsq, T_c], writes=[tpss])
                    rr, trr = dtm.get()
                    S.op("act", lambda e: e.activation(out=rr[0:64, 0:T], in_=pss[0:64, 0:T], func=AF.Ln, scale=1.0 / 64, bias=EPS),
                         reads=[tpss], writes=[trr])
                    S.op("act", lambda e: e.activation(out=rr[0:64, 0:T], in_=rr[0:64, 0:T], func=AF.Exp, scale=-0.5), reads=[trr], writes=[trr])
                    S.op("dve", lambda e: e.scalar_tensor_tensor(out=yT[hp:hp + 64, u // 2, 0:T], in0=dd[0:64, 0:T], scalar=misc[0:64, 2:3], in1=rr[0:64, 0:T],
                                                                  op0=mult, op1=mult), reads=[tdd, trr, T_misc], writes=[T_yT[u]])

                work = [(un, ch) for un in units for ch in chunks]
                loaded = {}
                nxt = [0]

                def ensure(idx):
                    idx = min(idx, len(work) - 1)
                    while nxt[0] <= idx:
                        loaded[nxt[0]] = load_chunk(*work[nxt[0]])
                        nxt[0] += 1

                for ui, un in enumerate(units):
                    tl = []
                    for ci, ch in enumerate(chunks):
                        wi = ui * len(chunks) + ci
                        for t in range(0, NKR if ch[0] == "r" else 2, 2):
                            tl.append((wi, t))
                    po, tpo = ps[6 + obank[0] % 2], T_ps[6 + obank[0] % 2]
                    obank[0] += 1
                    pb_, d_ = un["pb"], un["d"]
                    qt = QT[un["q"]]
                    tq = T_QT[un["q"]]
                    tp = (pb_, 0) if pb_ == 96 else None

                    def qk(n):
                        wi, t = tl[n]
                        ensure(wi + 2)
                        si = loaded[wi][0]
                        b2 = (tcount[0] + n) % 3
                        kw = {"tile_position": tp} if tp else {}
                        for h_ in range(2):
                            S.op("pe", lambda e: e.matmul(ps[2 * b2 + h_][:, 0:T], kslot[si][pb_:pb_ + d_, (t + h_) * 128:(t + h_ + 1) * 128],
                                                         qt[pb_:pb_ + d_, 0:T], start=True, stop=True, **kw),
                                 reads=[T_ks[si], tq], writes=[T_ps[2 * b2 + h_]], sig=(h_ == 1))

                    n_t = len(tl)
                    qk(0)
                    if n_t > 1:
                        qk(1)
                    for n in range(n_t):
                        if n + 2 < n_t:
                            qk(n + 2)
                        wi, t = tl[n]
                        si = loaded[wi][0]
                        b2 = (tcount[0] + n) % 3
                        b3 = (tcount[0] + n) % 3
                        S.op("act", lambda e: e.activation(out=Pt[b3][:, :, 0:T], in_=psd[b2][:, :].rearrange("p (h n) -> p h n", h=2)[:, :, 0:T],
                                                           func=AF.Exp, scale=un["sc"]),
                             reads=[T_ps[2 * b2], T_ps[2 * b2 + 1]], writes=[T_P[b3]])
                        for h_ in range(2):
                            S.op("pe", lambda e: e.matmul(po[:, 0:T], vslot[si][:, t + h_, :], Pt[b3][:, h_, 0:T],
                                                         start=(n == 0 and h_ == 0), stop=(n == n_t - 1 and h_ == 1)),
                                 reads=[T_vs[si], T_P[b3]], writes=[tpo])
                        if n == min(1, n_t - 1) and pending:
                            f = pending.pop(0)
                            f()
                        pair_ctr[0] += 1
                        if conv_iv and conv_queue and pair_ctr[0] % conv_iv == 0:
                            S._wait("pool", ("c", "act", S.n["act"] - 1))
                            conv_pop(1)
                    tcount[0] += n_t
                    pending.append(lambda un=un, po=po, tpo=tpo: finalize(un, po, tpo))
                while pending:
                    pending.pop(0)()
                conv_pop(len(conv_queue))
                S.barrier()

        def phase_B3(l, gr):
            S.phase_reset()
            with ExitStack() as ph:
                prep = alloc_prep(ph)
                hT, T_hT = prep[0], prep[1]
                wout = sb("wout", [128, 8, 1024], BF16, ph)
                T_wout = Trk()
                sl_w = S.slot()
                S.dma("sp", sl_w, wout[:], wb[l]["wout"].ap().rearrange("p (k n) -> p k n", k=8), reads=[T_wb[l]["wout"]], writes=[T_wout])
                NR = 5
                ringb = [sb("fring%d" % i, [128, 2048], BF16, ph) for i in range(NR)]
                T_ring = [Trk() for _ in range(NR)]
                sl_ring = [S.slot() for _ in range(NR)]
                rcount = [0]
                actT = sb("actT", [128, 22, 512], BF16, ph)
                T_act = [Trk() for _ in range(11)]
                Gbc = [sb("Gbc%d" % i, [128, 1024], F32, ph) for i in range(2)]
                T_G = [Trk(), Trk()]
                tmp = Tmp(ph, 4, "tmpF")
                dg = Tmp(ph, 2, "dg")
                st2 = [sb("st2_%d" % i, [128, 8], F32, ph) for i in range(2)]
                T_st2 = [Trk(), Trk()]
                T = gr.T
                nt = len(gr.tts)
                for w in range(2):
                    for half in range(2):
                        pg, tpg = nb()
                        for q in range(4):
                            k = half * 4 + q
                            d_, td_ = dg.get()
                            S.op("dve", lambda e: e.tensor_scalar(out=d_[:, 0:128], in0=ident, scalar1=Gc_t[:, w, gr.j, k:k + 1], scalar2=None, op0=mult),
                                 reads=[T_c, T_A], writes=[td_])
                            S.op("pe", lambda e: e.matmul(pg[:, q * 128:(q + 1) * 128], onesf, d_[:, 0:128], start=True, stop=True),
                                 reads=[td_, T_c], writes=[tpg])
                        S.op("act", lambda e: e.activation(out=Gbc[w][:, half * 512:(half + 1) * 512], in_=pg[:, :], func=AF.Copy),
                             reads=[tpg], writes=[T_G[w]])

                def post(w, i, tt, pa, tpa, pb2, tpb2):
                    s_ = st2[i % 2]
                    ts_ = T_st2[i % 2]
                    jk, tjk = tmp.get()
                    S.op("act", lambda e: e.activation(out=jk[:, :], in_=pa[:, :], func=AF.Square, accum_out=s_[:, 0:1]), reads=[tpa], writes=[tjk, ts_])
                    jk2, tjk2 = tmp.get()
                    S.op("act", lambda e: e.activation(out=jk2[:, :], in_=pb2[:, :], func=AF.Square, accum_out=s_[:, 1:2]), reads=[tpb2], writes=[tjk2, ts_])
                    S.op("dve", lambda e: e.tensor_tensor(out=s_[:, 2:3], in0=s_[:, 0:1], in1=s_[:, 1:2], op=add), reads=[ts_], writes=[ts_])
                    S.op("act", lambda e: e.activation(out=s_[:, 3:4], in_=s_[:, 2:3], func=AF.Ln, scale=1.0 / 1024, bias=EPS),
                         reads=[ts_], writes=[ts_])
                    S.op("act", lambda e: e.activation(out=s_[:, 4:5], in_=s_[:, 3:4], func=AF.Exp, scale=-0.5), reads=[ts_], writes=[ts_])
                    for half, (p_, t_) in enumerate(((pa, tpa), (pb2, tpb2))):
                        tm, ttm = tmp.get()
                        S.op("dve", lambda e: e.scalar_tensor_tensor(out=tm[:, :], in0=p_[:, :], scalar=s_[:, 4:5], in1=Gbc[w][:, half * 512:(half + 1) * 512],
                                                                      op0=mult, op1=mult), reads=[t_, ts_, T_G[w]], writes=[ttm])
                        S.op("pool", lambda e: e.tensor_tensor(out=xs[:, tt, half * 512:(half + 1) * 512], in0=xs[:, tt, half * 512:(half + 1) * 512],
                                                               in1=tm[:, :], op=add), reads=[ttm], writes=[T_x[tt]])

                for i, tt in enumerate(gr.tts):
                    pa, tpa = nb()
                    pb2, tpb2 = nb()
                    for k in range(8):
                        for (p_, t_, half) in ((pa, tpa, 0), (pb2, tpb2, 1)):
                            S.op("pe", lambda e: e.matmul(p_[:, :], yT[:, k, i * 128:(i + 1) * 128], wout[:, k, half * 512:(half + 1) * 512],
                                                         start=(k == 0), stop=(k == 7)), reads=[T_yT[2 * k], T_yT[2 * k + 1], T_wout], writes=[t_])
                    post(0, i, tt, pa, tpa, pb2, tpb2)
                if l == 0:
                    dump("xa", xs[:, 0:4, :], [128, 4, 1024], F32)
                prep_hT(prep, l, gr, 1)

                def ring_load(name, jj):
                    ri = rcount[0] % NR
                    rcount[0] += 1
                    S.dma("sp", sl_ring[ri], ringb[ri][:], wb[l][name].ap()[jj], reads=[T_wb[l][name]], writes=[T_ring[ri]])
                    return ri

                for jj in range(11):
                    rg = ring_load("wg", jj)
                    ru = ring_load("wu", jj)
                    wgv = ringb[rg][:].rearrange("p (k n) -> p k n", k=8)
                    wuv = ringb[ru][:].rearrange("p (k n) -> p k n", k=8)
                    for s_ in range(2):
                        j = jj * 2 + s_
                        pg, tpg = nb()
                        pu, tpu = nb()
                        for k in range(8):
                            S.op("pe", lambda e: e.matmul(pg[:, 0:T], wgv[:, k, s_ * 128:(s_ + 1) * 128], hT[:, k, 0:T], start=(k == 0), stop=(k == 7)),
                                 reads=[T_ring[rg], T_hT], writes=[tpg])
                        for k in range(8):
                            S.op("pe", lambda e: e.matmul(pu[:, 0:T], wuv[:, k, s_ * 128:(s_ + 1) * 128], hT[:, k, 0:T], start=(k == 0), stop=(k == 7)),
                                 reads=[T_ring[ru], T_hT], writes=[tpu])
                        sg, tsg = tmp.get()
                        S.op("act", lambda e: e.activation(out=sg[:, 0:T], in_=pg[:, 0:T], func=AF.Silu), reads=[tpg], writes=[tsg])
                        S.op("dve", lambda e: e.tensor_tensor(out=actT[:, j, 0:T], in0=sg[:, 0:T], in1=pu[:, 0:T], op=mult),
                             reads=[tsg, tpu], writes=[T_act[jj]])
                rds = []
                for jj in range(11):
                    rds.append(None)
                for i0 in range(0, nt, 4):
                    tis = list(range(i0, min(i0 + 4, nt)))
                    acc = {}
                    for i in tis:
                        acc[i] = ((ps[2 * i], T_ps[2 * i]), (ps[2 * i + 1], T_ps[2 * i + 1]))
                    for jj in range(11):
                        rd = ring_load("wd", jj)
                        wdv = ringb[rd][:].rearrange("p (s n) -> p s n", s=2)
                        for s_ in range(2):
                            j = jj * 2 + s_
                            for i in tis:
                                for half in range(2):
                                    p_, t_ = acc[i][half]
                                    S.op("pe", lambda e: e.matmul(p_[:, :], actT[:, j, i * 128:(i + 1) * 128], wdv[:, s_, half * 512:(half + 1) * 512],
                                                                 start=(j == 0), stop=(j == 21)), reads=[T_act[jj], T_ring[rd]], writes=[t_])
                    for i in tis:
                        (pa, tpa), (pb2, tpb2) = acc[i]
                        post(1, i, gr.tts[i], pa, tpa, pb2, tpb2)
                S.barrier()

        S.barrier()
        def ctx_B(l):
            if dbg and l == 0:
                return
            phase_B1(l, cg)
            phase_B2(l, cg)
            phase_B3(l, cg)

        for l in range(NL):
            last = l == NL - 1
            layer_consts(l)
            if l == 0:
                phase_A(l, [cg] + groups)
                gather(l)
            else:
                phase_A(l, groups)
                gather(l)
                layer_consts(l - 1)
                ctx_B(l - 1)
                layer_consts(l)
                phase_A(l, [cg])
            if l + 1 < NL:
                phase_M(l + 1)
            if l == 0:
                dump("kc", kc_loc[0].ap(), [672, 256], BF16)
                dump("vc", vc_loc[0].ap(), [1280, 132], BF16)
                dump("modT", modT[:], [128, NL * 96], F32)
                dump("misc", misc[:], [128, 64], F32)
            waited = False
            for gr in groups:
                phase_B1(l, gr)
                if not waited:
                    gather_wait()
                    waited = True
                if l == 0:
                    for i in range(20):
                        dump("QT%d" % i, QT[i][:], [128, 512], BF16)
                phase_B2(l, gr)
                if l == 0:
                    dump("yT", yT[:], [128, 8, 512], BF16)
                phase_B3(l, gr)
                if l == 0:
                    dump("x1", xs[:, 0:4, :], [128, 4, 1024], F32)
        if dbg:
            ctx_B(0)
        S.barrier()
        for g4 in range(NG):
            S.dma("sp", sl_out, out_d[g4 * 512:(g4 + 1) * 512, :].rearrange("(t p) d -> p t d", p=128), xs[:, g4 * 4:(g4 + 1) * 4, :],
                  reads=T_x[g4 * 4:(g4 + 1) * 4])
        S.barrier()
    return nc


def _swap(w, r):
    n = w.shape[-1]
    idx = np.arange(n).reshape(-1, r)
    idx = np.concatenate([idx[:, r // 2:], idx[:, :r // 2]], axis=1).reshape(-1)
    return w[..., idx]


def _rope_tabs(pos, n_ctx):
    row = (pos // 64).astype(np.float32)
    col = (pos % 64).astype(np.float32)

    def tab(rot):
        quarter = rot // 4
        inv = (np.float32(10000.0) ** (-np.arange(quarter, dtype=np.float32) / np.float32(quarter))).astype(np.float32)
        ang = np.concatenate([row[:, None] * inv, col[:, None] * inv], axis=-1).astype(np.float32)
        return np.cos(ang).astype(np.float32), np.sin(ang).astype(np.float32)

    out = []
    for rot in (32, 64):
        c, s = tab(rot)
        C = np.concatenate([c, c], 1).T
        Sg = np.concatenate([-s, s], 1).T
        C = np.tile(C, (128 // rot, 1))
        Sg = np.tile(Sg, (128 // rot, 1))
        C = np.concatenate([C, np.ones((128, n_ctx), np.float32)], 1)
        Sg = np.concatenate([Sg, np.zeros((128, n_ctx), np.float32)], 1)
        out += [C, Sg]
    return np.ascontiguousarray(np.stack(out, 0), dtype=np.float32)


def _chunk_rows(w):
    L, R, N = w.shape
    return np.ascontiguousarray(w.reshape(L, R // 128, 128, N).transpose(0, 2, 1, 3))


def prep_shared(inp):
    f = lambda a: np.asarray(a, dtype=np.float32)
    L = f(inp["w_ada"]).shape[0]
    w_in = f(inp["w_in"])
    qa, ckv, kpe = w_in[..., 0:256], w_in[..., 256:384], w_in[..., 384:416]
    dq, dk, dv = w_in[..., 416:672], w_in[..., 672:928], w_in[..., 928:1184]
    gq, gk, gv = w_in[..., 1184:1696], w_in[..., 1696:1824], w_in[..., 1824:1952]
    A = [ckv, kpe, _swap(kpe, 32)]
    for b in range(2):
        blk = dk[..., b * 128:(b + 1) * 128]
        A += [blk, _swap(blk, 32)]
    A += [gk, _swap(gk, 64), dv, gv]
    wA = _chunk_rows(np.concatenate(A, -1))
    Q = [qa]
    for b in range(2):
        blk = dq[..., b * 128:(b + 1) * 128]
        Q += [blk, _swap(blk, 32)]
    for i in range(4):
        blk = np.concatenate([gq[..., i * 64:(i + 1) * 64], gq[..., (4 + i) * 64:(5 + i) * 64]], -1)
        Q += [blk, _swap(blk, 64)]
    wQ = _chunk_rows(np.concatenate(Q, -1))
    wqb = f(inp["w_mla_qb"])
    qb = []
    for h in range(4):
        blk = wqb[..., h * 96:(h + 1) * 96]
        qb += [blk, np.concatenate([blk[..., 0:64], _swap(blk[..., 64:96], 32)], -1)]
    wqb_r = _chunk_rows(np.concatenate(qb, -1))
    wkvb = f(inp["w_mla_kvb"]).reshape(L, 128, 4, 128)
    wkvb_r = np.ascontiguousarray(np.concatenate([wkvb[..., 0:64].reshape(L, 128, 256), wkvb[..., 64:128].reshape(L, 128, 256)], -1))
    wout_r = _chunk_rows(f(inp["w_out"]))
    w_ada = f(inp["w_ada"])
    wada_r = np.ascontiguousarray(w_ada.reshape(L, 8, 128, 12, 512).transpose(0, 3, 2, 1, 4).reshape(L, 12, 128, 4096))
    wg = f(inp["w_ffn_gate"]).reshape(L, 8, 128, 11, 256).transpose(0, 3, 2, 1, 4).reshape(L, 11, 128, 2048)
    wu = f(inp["w_ffn_up"]).reshape(L, 8, 128, 11, 256).transpose(0, 3, 2, 1, 4).reshape(L, 11, 128, 2048)
    wd = f(inp["w_ffn_down"]).reshape(L, 11, 2, 128, 1024).transpose(0, 1, 3, 2, 4).reshape(L, 11, 128, 2048)
    colv = np.zeros((L, 128, NV), np.float32)

    def cols(v):
        return v.reshape(L, -1, 128).transpose(0, 2, 1)

    colv[:, :, 0:8] = cols(f(inp["g_attn_pre"]))
    colv[:, :, 8:16] = cols(f(inp["g_attn_post"]))
    colv[:, :, 16:24] = cols(f(inp["g_ffn_pre"]))
    colv[:, :, 24:32] = cols(f(inp["g_ffn_post"]))
    colv[:, :, 32:80] = cols(f(inp["b_ada"]))
    colv[:, :, 80:82] = cols(f(inp["g_mla_q"]))
    colv[:, :, 82] = f(inp["g_mla_kv"])
    ggq, ggk = f(inp["g_gqa_q"]), f(inp["g_gqa_k"])
    colv[:, :, 83] = np.tile(ggq, (1, 2))
    colv[:, :, 84] = np.tile(_swap(ggq, 64), (1, 2))
    colv[:, :, 85] = np.tile(ggk, (1, 2))
    colv[:, :, 86] = np.tile(_swap(ggk, 64), (1, 2))
    colv[:, :, 87] = np.tile(f(inp["g_diff_sub"]), (1, 2))
    for i, nm in enumerate(("lambda_q1", "lambda_k1", "lambda_q2", "lambda_k2")):
        colv[:, :, 88 + 32 * i:120 + 32 * i] = f(inp[nm])[:, None, :]
    consts = np.zeros((128, 384), np.float32)
    consts[:, 0:128] = np.eye(128, dtype=np.float32)
    consts[:, 128:256] = 1.0
    consts[0:64, 256:320] = 1.0
    consts[64:128, 320:384] = 1.0
    return dict(w_ada_r=wada_r, colv=colv, w_inA=wA, w_inQ=wQ, w_qb_r=wqb_r, w_kvb_r=wkvb_r, w_out_r=wout_r,
                wg_r=np.ascontiguousarray(wg), wu_r=np.ascontiguousarray(wu), wd_r=np.ascontiguousarray(wd), consts=consts)


def make_in_maps(inp, NG):
    shared = prep_shared(inp)
    x = np.asarray(inp["x"], np.float32)
    c = np.asarray(inp["c"], np.float32)
    ctx = np.asarray(inp["ctx"], np.float32)
    c_ctx = np.asarray(inp["c_ctx"], np.float32)
    NTOK = NG * 512
    maps = []
    for core in range(8):
        b, q = core // 4, core % 4
        m = dict(shared)
        m["x_own"] = np.ascontiguousarray(x[b, q * NTOK:(q + 1) * NTOK, :])
        m["ctx_b"] = np.ascontiguousarray(ctx[b])
        cv = np.stack([c[b].reshape(8, 128).T, c_ctx.reshape(8, 128).T], -1)
        m["cvec"] = np.ascontiguousarray(cv.reshape(128, 16))
        m["rope"] = _rope_tabs(np.arange(q * NTOK, (q + 1) * NTOK), 256)
        maps.append(m)
    return maps


_NC_CACHE = {}


def run(inp, NG, ret_res=False):
    if NG not in _NC_CACHE:
        _NC_CACHE[NG] = build(NG)
    nc = _NC_CACHE[NG]
    maps = make_in_maps(inp, NG)
    res = run_bass_kernel_spmd(nc, maps, core_ids=list(range(8)))
    if ret_res:
        return res
    NTOK = NG * 512
    out = np.zeros((2, 4 * NTOK, 1024), np.float32)
    for core in range(8):
        b, q = core // 4, core % 4
        out[b, q * NTOK:(q + 1) * NTOK, :] = res.results[core]["out"]
    return out


def kernel(**inputs):
    return run(inputs, 4)
```

```python
import bisect
import math
from contextlib import ExitStack

import numpy as np
import concourse.bass as bass
import concourse.mybir as mybir
from concourse.bass_utils import run_bass_kernel_spmd

F32 = mybir.dt.float32
BF16 = mybir.dt.bfloat16
AF = mybir.ActivationFunctionType
ALU = mybir.AluOpType
EPS = 1e-6
NV = 216
SC_MLA = 1.0 / math.sqrt(96.0)
SC_DIFF = 1.0 / math.sqrt(32.0)
SC_GQA = 1.0 / 8.0


class Trk:
    __slots__ = ("w", "r")

    def __init__(self):
        self.w = None
        self.r = []


class Slot:
    pass


class Sched:
    COMPUTE = ("pe", "act", "dve", "pool")

    def __init__(self, nc, es):
        self.nc = nc
        self.es = es
        self.eng = {"pe": nc.tensor, "act": nc.scalar, "dve": nc.vector, "pool": nc.gpsimd, "sp": nc.sync}
        self.sem = {e: es.enter_context(nc.semaphore("s_" + e)) for e in self.COMPUTE}
        self.n = {e: 0 for e in self.eng}
        self.last = {e: None for e in self.eng}
        self.sig = {e: [] for e in self.COMPUTE}
        self.known = {e: {} for e in self.eng}
        self.slots = []
        self.pool = []
        self.pidx = 0
        self.nwait = 0

    def _sigval(self, e, idx):
        s = self.sig[e]
        p = bisect.bisect_left(s, idx)
        if p < len(s):
            return p + 1
        li = self.n[e] - 1
        assert li >= idx
        self.last[e].then_inc(self.sem[e], 1)
        s.append(li)
        return len(s)

    def _wait(self, q, dep):
        if dep[0] == "c":
            _, e, idx = dep
            if e == q and q == "pe":
                return
            key = ("c", e)
            kv = self.known[q].get(key, 0)
            s = self.sig[e]
            p = bisect.bisect_left(s, idx)
            if p < len(s) and p + 1 <= kv:
                return
            v = self._sigval(e, idx)
            if v <= kv:
                return
            self.eng[q].wait_ge(self.sem[e], v)
            self.nwait += 1
            self.known[q][key] = v
        else:
            _, sl, val = dep
            val = sl.cnt
            key = ("d", id(sl))
            if self.known[q].get(key, 0) >= val:
                return
            self.eng[q].wait_ge(sl.sem, val)
            self.nwait += 1
            self.known[q][key] = val

    def _deps(self, q, reads, writes):
        for t in reads:
            if t.w is not None:
                self._wait(q, t.w)
        for t in writes:
            if t.w is not None:
                self._wait(q, t.w)
            for d in t.r:
                self._wait(q, d)

    def op(self, q, fn, reads=(), writes=(), sig=False):
        self._deps(q, reads, writes)
        ins = fn(self.eng[q])
        idx = self.n[q]
        self.n[q] += 1
        self.last[q] = ins
        if sig:
            ins.then_inc(self.sem[q], 1)
            self.sig[q].append(idx)
        me = ("c", q, idx)
        for t in reads:
            t.r.append(me)
        for t in writes:
            t.w = me
            t.r = []
        return ins

    def pslot(self):
        sl = Slot()
        sl.sem = self.es.enter_context(self.nc.semaphore("d%d" % len(self.slots)))
        sl.cnt = 0
        self.slots.append(sl)
        return sl

    def slot(self):
        if self.pidx >= len(self.pool):
            self.pool.append(self.pslot())
        sl = self.pool[self.pidx]
        self.pidx += 1
        return sl

    def phase_reset(self):
        self.barrier()
        self.pidx = 0

    def dma(self, q, sl, out, in_, reads=(), writes=()):
        self._deps(q, reads, writes)
        ins = self.eng[q].dma_start(out=out, in_=in_)
        ins.then_inc(sl.sem, 16)
        sl.cnt += 16
        me = ("d", sl, sl.cnt)
        for t in reads:
            t.r.append(me)
        for t in writes:
            t.w = me
            t.r = []
        return ins

    def barrier(self):
        for q in self.eng:
            for e in self.COMPUTE:
                if e != q and self.n[e] > 0:
                    self._wait(q, ("c", e, self.n[e] - 1))
            for sl in self.slots:
                if sl.cnt > 0:
                    self._wait(q, ("d", sl, sl.cnt))


def build(NG=4, NL=2, dbg=False):
    nc = bass.Bass("TRN2", target_bir_lowering=False)
    NTOK = NG * 512
    NTT = NG * 4
    NKR = NG * 4
    mult, add = ALU.mult, ALU.add

    def din(name, shape, dt=F32):
        return nc.dram_tensor(name, shape, dt, kind="ExternalInput").ap()

    x_d = din("x_own", [NTOK, 1024])
    ctx_d = din("ctx_b", [256, 1024])
    cvec_d = din("cvec", [128, 16])
    wada_d = din("w_ada_r", [NL, 12, 128, 8 * 512])
    colv_d = din("colv", [NL, 128, NV])
    wA_d = din("w_inA", [NL, 128, 8, 1344])
    wQ_d = din("w_inQ", [NL, 128, 8, 1792])
    wqb_d = din("w_qb_r", [NL, 128, 2, 768])
    wkvb_d = din("w_kvb_r", [NL, 128, 512])
    wout_d = din("w_out_r", [NL, 128, 8, 1024])
    wg_d = din("wg_r", [NL, 11, 128, 2048])
    wu_d = din("wu_r", [NL, 11, 128, 2048])
    wd_d = din("wd_r", [NL, 11, 128, 2048])
    rope_d = din("rope", [4, 128, NTOK + 256])
    const_d = din("consts", [128, 384])
    out_d = nc.dram_tensor("out", [NTOK, 1024], F32, kind="ExternalOutput").ap()

    KB = [0, 128, 288, 416, 544, 672]
    k_loc = [[nc.dram_tensor("k_loc%d_%d" % (l, s_), [KB[s_ + 1] - KB[s_], NTOK], BF16) for s_ in range(5)] for l in range(NL)]
    k_all = [[nc.dram_tensor("k_all%d_%d" % (l, s_), [4 * (KB[s_ + 1] - KB[s_]), NTOK], BF16) for s_ in range(5)] for l in range(NL)]
    v_loc = [[nc.dram_tensor("v_loc%d_%d" % (l, s_), [256, NKR * 66], BF16) for s_ in range(5)] for l in range(NL)]
    v_all = [[nc.dram_tensor("v_all%d_%d" % (l, s_), [4 * 256, NKR * 66], BF16) for s_ in range(5)] for l in range(NL)]

    wb = []
    for l in range(NL):
        wb.append(dict(
            wA=nc.dram_tensor("wA_b%d" % l, [128, 8 * 1344], BF16), wkvb=nc.dram_tensor("wkvb_b%d" % l, [128, 512], BF16),
            wQ=nc.dram_tensor("wQ_b%d" % l, [128, 8 * 1792], BF16), wqb=nc.dram_tensor("wqb_b%d" % l, [128, 2 * 768], BF16),
            wout=nc.dram_tensor("wout_b%d" % l, [128, 8 * 1024], BF16),
            wg=nc.dram_tensor("wg_b%d" % l, [11, 128, 2048], BF16), wu=nc.dram_tensor("wu_b%d" % l, [11, 128, 2048], BF16),
            wd=nc.dram_tensor("wd_b%d" % l, [11, 128, 2048], BF16)))

    def kblk(r0):
        for s_ in range(5):
            if KB[s_] <= r0 < KB[s_ + 1]:
                return s_, r0 - KB[s_], KB[s_ + 1] - KB[s_]
        raise ValueError
    kc_loc = [nc.dram_tensor("kc_loc%d" % l, [672, 256], BF16) for l in range(NL)]
    vc_loc = [nc.dram_tensor("vc_loc%d" % l, [1280, 2 * 66], BF16) for l in range(NL)]

    with ExitStack() as es:
        S = Sched(nc, es)

        uid = [0]

        def sb(name, shape, dt, stack=None):
            uid[0] += 1
            return (stack or es).enter_context(nc.sbuf_tensor("sb%d_%s" % (uid[0], name), shape, dt))

        xs = sb("xs", [128, NTT + 2, 1024], F32)
        T_x = [Trk() for _ in range(NTT + 2)]
        cst = sb("cst", [128, 384], F32)
        T_c = Trk()
        ident = cst[:, 0:128]
        onesf = cst[:, 128:256]
        bd64 = cst[:, 256:384]
        colv = sb("colv", [128, NL, NV], F32)
        T_colv = Trk()
        modT = sb("modT", [128, NL * 96], F32)
        T_mod = Trk()
        misc = sb("misc", [128, 64], F32)
        T_misc = Trk()
        A_t = sb("A_t", [128, 2, 2, 8], F32)
        Gc_t = sb("Gc_t", [128, 2, 2, 8], F32)
        T_A = Trk()
        QT = [sb("QT%d" % i, [128, 512], BF16) for i in range(20)]
        T_QT = [Trk() for _ in range(20)]
        yT = sb("yT", [128, 8, 512], BF16)
        T_yT = [Trk() for _ in range(16)]

        psd = [es.enter_context(nc.psum_tensor("psd%d" % i, [128, 1024], F32)) for i in range(4)]
        ps = [psd[i // 2][:, (i % 2) * 512:(i % 2 + 1) * 512] for i in range(8)]
        T_ps = [Trk() for _ in range(8)]
        bank_rr = [0]

        def nb(lo=0, hi=8):
            i = lo + (bank_rr[0] % (hi - lo))
            bank_rr[0] += 1
            return ps[i], T_ps[i]

        dbg_done = set()

        def dump(name, src_ap, shape, dt):
            if not dbg or name in dbg_done:
                return
            dbg_done.add(name)
            d = nc.dram_tensor("dbg_" + name, shape, dt, kind="ExternalOutput").ap()
            S.phase_reset()
            sl = S.slot()
            S.dma("sp", sl, d, src_ap)
            S.barrier()

        sl_misc = S.pslot()
        sl_x = S.pslot()
        sl_out = S.pslot()

        def mcol(l, c, j):
            o = (l * 48 + c) * 2 + j
            return modT[:, o:o + 1]

        S.dma("sp", sl_misc, cst[:], const_d, writes=[T_c])
        S.dma("sp", sl_misc, colv[:], colv_d.rearrange("l p n -> p l n"), writes=[T_colv])
        for g4 in range(NG):
            S.dma("sp", sl_x, xs[:, g4 * 4:(g4 + 1) * 4, :], x_d[g4 * 512:(g4 + 1) * 512, :].rearrange("(t p) d -> p t d", p=128),
                  writes=T_x[g4 * 4:(g4 + 1) * 4])
        S.dma("sp", sl_x, xs[:, NTT:NTT + 2, :], ctx_d.rearrange("(t p) d -> p t d", p=128), writes=T_x[NTT:NTT + 2])
        S.op("pool", lambda e: e.memset(misc[:, 0:1], -0.5), writes=[T_misc])
        for i in range(4, 20):
            S.op("pool", lambda e: e.memset(QT[i][:], 0.0), writes=[T_QT[i]])
        mhalf = misc[:, 0:1]

        T_wb = [dict((k_, Trk()) for k_ in wb[l]) for l in range(NL)]
        sl_wb = [dict((k_, S.pslot()) for k_ in wb[l]) for l in range(NL)]
        def convert(l, names):
            def cv(name, dst, src, extra=()):
                S.dma("pool", sl_wb[l][name], dst, src, reads=list(extra), writes=[T_wb[l][name]])
            for name in names:
                if name == "wA":
                    for k in range(8):
                        cv("wA", wb[l]["wA"].ap()[:, k * 1344:(k + 1) * 1344], wA_d[l, :, k, :])
                elif name == "wkvb":
                    cv("wkvb", wb[l]["wkvb"].ap(), wkvb_d[l])
                elif name == "wQ":
                    for k in range(8):
                        cv("wQ", wb[l]["wQ"].ap()[:, k * 1792:(k + 1) * 1792], wQ_d[l, :, k, :])
                elif name == "wqb":
                    for c in range(2):
                        cv("wqb", wb[l]["wqb"].ap()[:, c * 768:(c + 1) * 768], wqb_d[l, :, c, :])
                elif name == "wout":
                    for k in range(8):
                        cv("wout", wb[l]["wout"].ap()[:, k * 1024:(k + 1) * 1024], wout_d[l, :, k, :])
                elif name.startswith("ffn"):
                    j0, j1 = {"ffn": (0, 11), "ffn0": (0, 4), "ffn1": (4, 8), "ffn2": (8, 11)}[name]
                    for jj in range(j0, j1):
                        cv("wg", wb[l]["wg"].ap()[jj], wg_d[l, jj])
                        cv("wu", wb[l]["wu"].ap()[jj], wu_d[l, jj])
                    for jj in range(j0, j1):
                        cv("wd", wb[l]["wd"].ap()[jj], wd_d[l, jj])

        conv_queue = []

        def convert_lazy(l, names):
            real = S.dma

            def rec(*a, **k):
                conv_queue.append((a, k))
            S.dma = rec
            try:
                convert(l, names)
            finally:
                S.dma = real

        def conv_pop(n=1):
            for _ in range(n):
                if conv_queue:
                    a, k = conv_queue.pop(0)
                    S.dma(*a, **k)

        convert(0, ["wA", "wkvb"])

        def phase_M(l, after=None):
            S.phase_reset()
            with ExitStack() as ph:
                scv = sb("scv", [128, 16], F32, ph)
                T_scv = Trk()
                wad = [sb("wad%d" % i, [128, 8, 512], F32, ph) for i in range(3)]
                T_wad = [Trk(), Trk(), Trk()]
                sl_wad = [S.slot(), S.slot(), S.slot()]
                sl_c = S.slot()
                S.dma("sp", sl_c, scv[:], cvec_d, writes=[T_scv])
                S.op("act", lambda e: e.activation(out=scv[:], in_=scv[:], func=AF.Silu), reads=[T_scv], writes=[T_scv])
                modrow = sb("modrow", [2, 6144], F32, ph)
                T_mr = Trk()
                for cb in range(12):
                    bi = cb % 3
                    S.dma("sp", sl_wad[bi], wad[bi][:], wada_d[l, cb].rearrange("p (k n) -> p k n", k=8), writes=[T_wad[bi]])
                    if cb == 11 and after is not None:
                        after(T_wad[bi])
                    pr, tpr = nb()
                    for k in range(8):
                        S.op("pe", lambda e: e.matmul(pr[0:2, :], scv[:, 2 * k:2 * k + 2], wad[bi][:, k, :], start=(k == 0), stop=(k == 7)),
                             reads=[T_wad[bi], T_scv], writes=[tpr])
                    S.op("dve", lambda e: e.tensor_copy(out=modrow[0:2, cb * 512:(cb + 1) * 512], in_=pr[0:2, :]), reads=[tpr], writes=[T_mr])
                pm, tpm = nb()
                for c in range(48):
                    S.op("pe", lambda e: e.matmul(pm[:, 2 * c:2 * c + 2], modrow[0:2, c * 128:(c + 1) * 128], ident[0:2, 0:2], start=True, stop=True),
                         reads=[T_mr, T_c], writes=[tpm])
                for j in range(2):
                    S.op("dve", lambda e: e.tensor_tensor(
                        out=modT[:, l * 96:(l + 1) * 96].rearrange("p (c j) -> p c j", j=2)[:, :, j],
                        in0=pm[:, 0:96].rearrange("p (c j) -> p c j", j=2)[:, :, j],
                        in1=colv[:, l, 32:80], op=add), reads=[tpm, T_colv], writes=[T_mod])
                S.barrier()

        def rest_conversions(t_last):
            S._deps("pool", [t_last], [])
            convert(0, ["wQ", "wqb"])

        phase_M(0, after=rest_conversions)

        def layer_consts(l):
            lam_init = 0.8 - 0.6 * math.exp(-0.3 * l)
            for w in range(2):
                for j in range(2):
                    sc0 = (l * 48 + (1 + 3 * w) * 8) * 2
                    gt0 = (l * 48 + (2 + 3 * w) * 8) * 2
                    sc_v = modT[:, sc0:sc0 + 16].rearrange("p (k j) -> p k j", j=2)[:, :, j]
                    gt_v = modT[:, gt0:gt0 + 16].rearrange("p (k j) -> p k j", j=2)[:, :, j]
                    gpre = colv[:, l, 16 * w:16 * w + 8]
                    gpost = colv[:, l, 16 * w + 8:16 * w + 16]
                    S.op("dve", lambda e: e.scalar_tensor_tensor(out=A_t[:, w, j, :], in0=sc_v, scalar=1.0, in1=gpre,
                                                                  op0=add, op1=mult), reads=[T_mod, T_colv], writes=[T_A])
                    S.op("dve", lambda e: e.tensor_tensor(out=Gc_t[:, w, j, :], in0=gt_v, in1=gpost, op=mult),
                         reads=[T_mod, T_colv], writes=[T_A])
            S.op("dve", lambda e: e.tensor_tensor(out=misc[:, 8:40], in0=colv[:, l, 88:120], in1=colv[:, l, 120:152], op=mult),
                 reads=[T_colv, T_misc], writes=[T_misc])
            S.op("dve", lambda e: e.reduce_sum(out=misc[:, 4:5], in_=misc[:, 8:40], axis=mybir.AxisListType.X),
                 reads=[T_misc], writes=[T_misc])
            S.op("dve", lambda e: e.tensor_tensor(out=misc[:, 8:40], in0=colv[:, l, 152:184], in1=colv[:, l, 184:216], op=mult),
                 reads=[T_colv, T_misc], writes=[T_misc])
            S.op("dve", lambda e: e.reduce_sum(out=misc[:, 5:6], in_=misc[:, 8:40], axis=mybir.AxisListType.X),
                 reads=[T_misc], writes=[T_misc])
            S.op("act", lambda e: e.activation(out=misc[:, 4:6], in_=misc[:, 4:6], func=AF.Exp), reads=[T_misc], writes=[T_misc])
            S.op("dve", lambda e: e.scalar_tensor_tensor(out=misc[:, 1:2], in0=misc[:, 5:6], scalar=-lam_init, in1=misc[:, 4:5],
                                                          op0=add, op1=ALU.subtract), reads=[T_misc], writes=[T_misc])
            S.op("dve", lambda e: e.tensor_scalar(out=misc[:, 2:3], in0=colv[:, l, 87:88], scalar1=(1.0 - lam_init), scalar2=None,
                                                  op0=mult), reads=[T_colv, T_misc], writes=[T_misc])

        class G:
            pass

        groups = []
        for g in range(NG):
            gr = G()
            gr.tts = [g * 4 + i for i in range(4)]
            gr.T = 512
            gr.j = 0
            gr.tab0 = g * 512
            gr.g = g
            gr.ctx = False
            groups.append(gr)
        cg = G()
        cg.tts = [NTT, NTT + 1]
        cg.T = 256
        cg.j = 1
        cg.tab0 = NTOK
        cg.g = None
        cg.ctx = True

        def prep_hT(ph_t, l, gr, w):
            hT, T_hT, xn, T_xn, junk, T_junk, st, T_st = ph_t
            for i, tt in enumerate(gr.tts):
                b = i % 2
                s_ = st[i % 4]
                ts_ = T_st[i % 4]
                S.op("act", lambda e: e.activation(out=junk[:], in_=xs[:, tt, :], func=AF.Square, accum_out=s_[:, 0:1]),
                     reads=[T_x[tt]], writes=[T_junk, ts_])
                S.op("act", lambda e: e.activation(out=s_[:, 1:2], in_=s_[:, 0:1], func=AF.Ln, scale=1.0 / 1024, bias=EPS),
                     reads=[ts_], writes=[ts_])
                S.op("act", lambda e: e.activation(out=s_[:, 2:3], in_=s_[:, 1:2], func=AF.Exp, scale=-0.5),
                     reads=[ts_], writes=[ts_])
                S.op("dve", lambda e: e.tensor_scalar(out=xn[b][:], in0=xs[:, tt, :], scalar1=s_[:, 2:3], scalar2=None, op0=mult),
                     reads=[T_x[tt], ts_], writes=[T_xn[b]])
                for half in range(2):
                    pt, tpt = nb()
                    for q in range(4):
                        k = half * 4 + q
                        S.op("pe", lambda e: e.transpose(pt[:, q * 128:(q + 1) * 128], xn[b][:, k * 128:(k + 1) * 128], ident),
                             reads=[T_xn[b], T_c], writes=[tpt])
                    for q in range(4):
                        k = half * 4 + q
                        acol = A_t[:, w, gr.j, k:k + 1]
                        bcol = mcol(l, (3 * w) * 8 + k, gr.j)
                        o_ = hT[:, k, i * 128:(i + 1) * 128]
                        i_ = pt[:, q * 128:(q + 1) * 128]
                        if q % 2 == 0:
                            S.op("act", lambda e: e.activation(out=o_, in_=i_, func=AF.Identity, scale=acol, bias=bcol),
                                 reads=[tpt, T_A, T_mod], writes=[T_hT])
                        else:
                            S.op("dve", lambda e: e.tensor_scalar(out=o_, in0=i_, scalar1=acol, scalar2=bcol, op0=mult, op1=add),
                                 reads=[tpt, T_A, T_mod], writes=[T_hT])

        def alloc_prep(ph):
            hT = sb("hT", [128, 8, 512], BF16, ph)
            xn = [sb("xn%d" % i, [128, 1024], F32, ph) for i in range(2)]
            junk = sb("junk", [128, 1024], BF16, ph)
            st = [sb("st%d" % i, [128, 4], F32, ph) for i in range(4)]
            return (hT, Trk(), xn, [Trk(), Trk()], junk, Trk(), st, [Trk() for _ in range(4)])

        def proj_block(W, T_W, c0, wd, hT, T_hT, T):
            pb, tpb = nb()
            for k in range(8):
                S.op("pe", lambda e: e.matmul(pb[0:wd, 0:T], W[:, k, c0:c0 + wd], hT[:, k, 0:T], start=(k == 0), stop=(k == 7)),
                     reads=[T_W, T_hT], writes=[tpb])
            return pb, tpb

        class Tmp:
            def __init__(self, ph, n, name, dt=F32):
                self.t = [sb("%s%d" % (name, i), [128, 512], dt, ph) for i in range(n)]
                self.k = [Trk() for _ in range(n)]
                self.i = 0

            def get(self):
                i = self.i % len(self.t)
                self.i += 1
                return self.t[i], self.k[i]

        def rstd_bcast(tmp, src_list, lhs, n_feat, T, rows=128):
            pst, tpst = nb()
            for i, (p_, t_) in enumerate(src_list):
                sq, tsq = tmp.get()
                S.op("act", lambda e: e.activation(out=sq[0:rows, 0:T], in_=p_[0:rows, 0:T], func=AF.Square), reads=[t_], writes=[tsq])
                S.op("pe", lambda e: e.matmul(pst[0:rows, 0:T], lhs[0:rows, 0:rows], sq[0:rows, 0:T], start=(i == 0), stop=(i == len(src_list) - 1)),
                     reads=[tsq, T_c], writes=[tpst])
            r, tr = tmp.get()
            S.op("act", lambda e: e.activation(out=r[0:rows, 0:T], in_=pst[0:rows, 0:T], func=AF.Ln, scale=1.0 / n_feat, bias=EPS),
                 reads=[tpst], writes=[tr])
            S.op("act", lambda e: e.activation(out=r[0:rows, 0:T], in_=r[0:rows, 0:T], func=AF.Exp, scale=-0.5), reads=[tr], writes=[tr])
            return r, tr

        def rope_comb(tmp, out_ap, t_out, a, ta, b_, tb, Ct, St, T_tab, r0, r1, T, outs=None):
            t1, tt1 = tmp.get()
            t2, tt2 = tmp.get()
            S.op("dve", lambda e: e.tensor_tensor(out=t1[r0:r1, 0:T], in0=a[r0:r1, 0:T], in1=Ct[r0:r1, 0:T], op=mult),
                 reads=[ta, T_tab], writes=[tt1])
            S.op("dve", lambda e: e.tensor_tensor(out=t2[r0:r1, 0:T], in0=b_[r0:r1, 0:T], in1=St[r0:r1, 0:T], op=mult),
                 reads=[tb, T_tab], writes=[tt2])
            if outs is not None:
                for (oap, otr, ra, rb) in outs:
                    S.op("dve", lambda e: e.tensor_tensor(out=oap, in0=t1[ra:rb, 0:T], in1=t2[ra:rb, 0:T], op=add),
                         reads=[tt1, tt2], writes=[otr])
                return
            S.op("dve", lambda e: e.tensor_tensor(out=out_ap, in0=t1[r0:r1, 0:T], in1=t2[r0:r1, 0:T], op=add),
                 reads=[tt1, tt2], writes=[t_out])

        def load_tabs(tabs, T_tab, sl_tab, gr):
            for ti in range(4):
                S.dma("sp", sl_tab, tabs[ti][:, 0:gr.T], rope_d[ti, :, gr.tab0:gr.tab0 + gr.T], writes=[T_tab])

        def gqa_block(tmp, pb, tpb, pbs, tpbs, gcol, gscol, tabs, T_tab, out_ap, t_out, T, outs=None):
            r, tr = rstd_bcast(tmp, [(pb, tpb)], bd64, 64.0, T)
            xg, txg = tmp.get()
            xw, txw = tmp.get()
            S.op("dve", lambda e: e.scalar_tensor_tensor(out=xg[:, 0:T], in0=pb[:, 0:T], scalar=gcol, in1=r[:, 0:T], op0=mult, op1=mult),
                 reads=[tpb, tr, T_colv], writes=[txg])
            S.op("dve", lambda e: e.scalar_tensor_tensor(out=xw[:, 0:T], in0=pbs[:, 0:T], scalar=gscol, in1=r[:, 0:T], op0=mult, op1=mult),
                 reads=[tpbs, tr, T_colv], writes=[txw])
            rope_comb(tmp, out_ap, t_out, xg, txg, xw, txw, tabs[2], tabs[3], T_tab, 0, 128, T, outs=outs)

        def phase_A(l, glist):
            S.phase_reset()
            with ExitStack() as ph:
                preps = [alloc_prep(ph), alloc_prep(ph)]
                wA = sb("wA", [128, 8, 1344], BF16, ph)
                T_wA = Trk()
                wkvb = sb("wkvb", [128, 512], BF16, ph)
                T_wkvb = Trk()
                sl_w = S.slot()
                S.dma("sp", sl_w, wA[:], wb[l]["wA"].ap().rearrange("p (k n) -> p k n", k=8), reads=[T_wb[l]["wA"]], writes=[T_wA])
                S.dma("sp", sl_w, wkvb[:], wb[l]["wkvb"].ap(), reads=[T_wb[l]["wkvb"]], writes=[T_wkvb])
                tabs2 = [[sb("tab%d" % i, [128, 512], F32, ph) for i in range(4)] for _ in range(2)]
                T_tab2 = [Trk(), Trk()]
                sl_tab2 = [S.slot(), S.slot()]
                tmp = Tmp(ph, 8, "tmpA")
                ckvn = sb("ckvn", [128, 512], BF16, ph)
                T_ckvn = Trk()
                kst = Tmp(ph, 3, "kst", BF16)
                sl_kst = [S.slot() for _ in range(3)]
                vst = sb("vst", [128, 10, 4, 66], BF16, ph)
                T_vst = Trk()
                sl_vst = S.slot()
                S.op("dve", lambda e: e.memset(vst[:, :, :, 64:65], 1.0), writes=[T_vst])
                S.op("dve", lambda e: e.memset(vst[:, :, :, 65:66], 0.0), writes=[T_vst])
                T_dram = Trk()

                def kout(tile_, ttile, idx, r0, nrows, gr, prow0=0):
                    if gr.ctx:
                        dst = kc_loc[l].ap()[r0:r0 + nrows, 0:256]
                    else:
                        s_, off, _ = kblk(r0)
                        dst = k_loc[l][s_].ap()[off:off + nrows, gr.g * 512:(gr.g + 1) * 512]
                    S.dma("sp", sl_kst[idx], dst, tile_[prow0:prow0 + nrows, 0:gr.T], reads=[ttile], writes=[])

                for gi, gr in enumerate(glist):
                    T = gr.T
                    nt = len(gr.tts)
                    prep = preps[gi % 2]
                    hT, T_hT = prep[0], prep[1]
                    tabs, T_tab, sl_tab = tabs2[gi % 2], T_tab2[gi % 2], sl_tab2[gi % 2]
                    prep_hT(prep, l, gr, 0)
                    load_tabs(tabs, T_tab, sl_tab, gr)
                    pb, tpb = proj_block(wA, T_wA, 0, 128, hT, T_hT, T)
                    r, tr = rstd_bcast(tmp, [(pb, tpb)], onesf, 128.0, T)
                    S.op("dve", lambda e: e.scalar_tensor_tensor(out=ckvn[:, 0:T], in0=pb[:, 0:T], scalar=colv[:, l, 82:83], in1=r[:, 0:T],
                                                                  op0=mult, op1=mult), reads=[tpb, tr, T_colv], writes=[T_ckvn])
                    for hp in range(2):
                        pk, tpk = nb()
                        S.op("pe", lambda e: e.matmul(pk[:, 0:T], wkvb[:, hp * 128:(hp + 1) * 128], ckvn[:, 0:T], start=True, stop=True),
                             reads=[T_wkvb, T_ckvn], writes=[tpk])
                        idx = kst.i % 3
                        ks, tks = kst.get()
                        S.op("act", lambda e: e.activation(out=ks[:, 0:T], in_=pk[:, 0:T], func=AF.Copy), reads=[tpk], writes=[tks])
                        kout(ks, tks, idx, hp * 128, 128, gr)
                    for i in range(nt):
                        pv, tpv = nb()
                        S.op("pe", lambda e: e.matmul(pv[:, 0:256], ckvn[:, i * 128:(i + 1) * 128], wkvb[:, 256:512], start=True, stop=True),
                             reads=[T_wkvb, T_ckvn], writes=[tpv])
                        S.op("dve", lambda e: e.tensor_copy(out=vst[:, 0:4, i, 0:64], in_=pv[:, 0:256].rearrange("p (u d) -> p u d", d=64)),
                             reads=[tpv], writes=[T_vst])
                    pb, tpb = proj_block(wA, T_wA, 128, 32, hT, T_hT, T)
                    pbs, tpbs = proj_block(wA, T_wA, 160, 32, hT, T_hT, T)
                    idx = kst.i % 3
                    ks, tks = kst.get()
                    rope_comb(tmp, ks[0:32, 0:T], tks, pb, tpb, pbs, tpbs, tabs[0], tabs[1], T_tab, 0, 32, T)
                    kout(ks, tks, idx, 256, 32, gr)
                    for b in range(2):
                        pb, tpb = proj_block(wA, T_wA, 192 + b * 256, 128, hT, T_hT, T)
                        pbs, tpbs = proj_block(wA, T_wA, 320 + b * 256, 128, hT, T_hT, T)
                        idx = kst.i % 3
                        ks, tks = kst.get()
                        rope_comb(tmp, ks[:, 0:T], tks, pb, tpb, pbs, tpbs, tabs[0], tabs[1], T_tab, 0, 128, T)
                        kout(ks, tks, idx, 288 + b * 128, 128, gr)
                    pb, tpb = proj_block(wA, T_wA, 704, 128, hT, T_hT, T)
                    pbs, tpbs = proj_block(wA, T_wA, 832, 128, hT, T_hT, T)
                    idx = kst.i % 3
                    ks, tks = kst.get()
                    gqa_block(tmp, pb, tpb, pbs, tpbs, colv[:, l, 85:86], colv[:, l, 86:87], tabs, T_tab, ks[:, 0:T], tks, T)
                    kout(ks, tks, idx, 544, 128, gr)
                    for i in range(nt):
                        pv, tpv = nb()
                        for k in range(8):
                            S.op("pe", lambda e: e.matmul(pv[:, 0:384], hT[:, k, i * 128:(i + 1) * 128], wA[:, k, 960:1344],
                                                         start=(k == 0), stop=(k == 7)), reads=[T_wA, T_hT], writes=[tpv])
                        S.op("dve", lambda e: e.tensor_copy(out=vst[:, 4:10, i, 0:64], in_=pv[:, 0:384].rearrange("p (u d) -> p u d", d=64)),
                             reads=[tpv], writes=[T_vst])
                    if gr.ctx:
                        dst = vc_loc[l].ap().rearrange("(u p) (t c) -> p u t c", p=128, c=66)
                        S.dma("sp", sl_vst, dst, vst[:, :, 0:2, :], reads=[T_vst])
                    else:
                        for s_ in range(5):
                            dst = v_loc[l][s_].ap().rearrange("(u p) (t c) -> p u t c", p=128, c=66)[:, :, gr.g * 4:(gr.g + 1) * 4, :]
                            S.dma("sp", sl_vst, dst, vst[:, 2 * s_:2 * s_ + 2, :, :], reads=[T_vst])
                S.barrier()

        cc_sem = es.enter_context(nc.semaphore("cc"))
        cc_cnt = [0]

        def gather(l):
            S.barrier()
            for (a, b_) in list(zip(k_loc[l], k_all[l])) + list(zip(v_loc[l], v_all[l])):
                nc.gpsimd.collective_compute("AllGather", ALU.bypass, replica_groups=[[0, 1, 2, 3], [4, 5, 6, 7]],
                                             ins=[a.ap().opt()], outs=[b_.ap().opt()]).then_inc(cc_sem, 1)
                cc_cnt[0] += 1

        def gather_wait():
            for q in S.eng:
                S.eng[q].wait_ge(cc_sem, cc_cnt[0])

        def phase_B1(l, gr):
            S.phase_reset()
            with ExitStack() as ph:
                prep = alloc_prep(ph)
                hT, T_hT = prep[0], prep[1]
                wQ = sb("wQ", [128, 8, 1792], BF16, ph)
                T_wQ = Trk()
                wqb = sb("wqb", [128, 2, 768], BF16, ph)
                T_wqb = Trk()
                sl_w = S.slot()
                S.dma("sp", sl_w, wQ[:], wb[l]["wQ"].ap().rearrange("p (k n) -> p k n", k=8), reads=[T_wb[l]["wQ"]], writes=[T_wQ])
                S.dma("sp", sl_w, wqb[:], wb[l]["wqb"].ap().rearrange("p (c n) -> p c n", c=2), reads=[T_wb[l]["wqb"]], writes=[T_wqb])
                tabs = [sb("tab%d" % i, [128, 512], F32, ph) for i in range(4)]
                T_tab = Trk()
                sl_tab = S.slot()
                tmp = Tmp(ph, 8, "tmpB")
                qan = sb("qan", [128, 2, 512], BF16, ph)
                T_qan = Trk()
                qfull = Tmp(ph, 2, "qfull", BF16)
                T = gr.T
                prep_hT(prep, l, gr, 0)
                load_tabs(tabs, T_tab, sl_tab, gr)
                p0, tp0 = proj_block(wQ, T_wQ, 0, 128, hT, T_hT, T)
                p1, tp1 = proj_block(wQ, T_wQ, 128, 128, hT, T_hT, T)
                r, tr = rstd_bcast(tmp, [(p0, tp0), (p1, tp1)], onesf, 256.0, T)
                for c, (p_, t_) in enumerate(((p0, tp0), (p1, tp1))):
                    S.op("dve", lambda e: e.scalar_tensor_tensor(out=qan[:, c, 0:T], in0=p_[:, 0:T], scalar=colv[:, l, 80 + c:81 + c], in1=r[:, 0:T],
                                                                  op0=mult, op1=mult), reads=[t_, tr, T_colv], writes=[T_qan])
                for h in range(4):
                    pb, tpb = nb()
                    pbs, tpbs = nb()
                    for c in range(2):
                        S.op("pe", lambda e: e.matmul(pb[0:96, 0:T], wqb[:, c, h * 192:h * 192 + 96], qan[:, c, 0:T], start=(c == 0), stop=(c == 1)),
                             reads=[T_wqb, T_qan], writes=[tpb])
                    for c in range(2):
                        S.op("pe", lambda e: e.matmul(pbs[0:96, 0:T], wqb[:, c, h * 192 + 96:h * 192 + 192], qan[:, c, 0:T], start=(c == 0), stop=(c == 1)),
                             reads=[T_wqb, T_qan], writes=[tpbs])
                    S.op("act", lambda e: e.activation(out=QT[h][0:64, 0:T], in_=pb[0:64, 0:T], func=AF.Copy), reads=[tpb], writes=[T_QT[h]])
                    rope_comb(tmp, QT[h][64:96, 0:T], T_QT[h], pb, tpb, pbs, tpbs, tabs[0], tabs[1], T_tab, 64, 96, T)
                for b in range(2):
                    pb, tpb = proj_block(wQ, T_wQ, 256 + b * 256, 128, hT, T_hT, T)
                    pbs, tpbs = proj_block(wQ, T_wQ, 384 + b * 256, 128, hT, T_hT, T)
                    outs = [(QT[4 + b * 4 + j][j * 32:(j + 1) * 32, 0:T], T_QT[4 + b * 4 + j], j * 32, (j + 1) * 32) for j in range(4)]
                    rope_comb(tmp, None, None, pb, tpb, pbs, tpbs, tabs[0], tabs[1], T_tab, 0, 128, T, outs=outs)
                for i in range(4):
                    pb, tpb = proj_block(wQ, T_wQ, 768 + i * 256, 128, hT, T_hT, T)
                    pbs, tpbs = proj_block(wQ, T_wQ, 896 + i * 256, 128, hT, T_hT, T)
                    outs = [(QT[12 + j * 4 + i][j * 64:(j + 1) * 64, 0:T], T_QT[12 + j * 4 + i], j * 64, (j + 1) * 64) for j in range(2)]
                    gqa_block(tmp, pb, tpb, pbs, tpbs, colv[:, l, 83:84], colv[:, l, 84:85], tabs, T_tab, None, None, T, outs=outs)
                S.barrier()

        def phase_B2(l, gr):
            S.phase_reset()
            with ExitStack() as ph:
                NSL = 4
                kslot = [sb("kslot%d" % i, [128, NTOK], BF16, ph) for i in range(NSL)]
                vslot = [sb("vslot%d" % i, [128, NKR, 128], BF16, ph) for i in range(NSL)]
                T_ks = [Trk() for _ in range(NSL)]
                T_vs = [Trk() for _ in range(NSL)]
                sl_kv = [S.slot() for _ in range(NSL)]
                for i in range(NSL):
                    S.op("pool", lambda e: e.memset(vslot[i][:, :, 64:128], 0.0), writes=[T_vs[i]])
                Pt = [sb("Pt%d" % i, [128, 2, 512], BF16, ph) for i in range(3)]
                T_P = [Trk() for _ in range(3)]
                osb = Tmp(ph, 3, "osb")
                rec = Tmp(ph, 2, "rec")
                on_ = Tmp(ph, 3, "on")
                dtm = Tmp(ph, 3, "dtm")
                T = gr.T
                ring = [0]
                tcount = [0]
                obank = [0]
                pending = []
                chunks = [("ctx",)] if gr.ctx else [("r", 0), ("r", 1), ("r", 2), ("r", 3), ("ctx",)]

                units = []
                for h in range(4):
                    units.append(dict(q=h, pb=0, d=96, kr=[(h * 64, 64, 0), (256, 32, 64)], vu=h, sc=SC_MLA, kind="plain", u=h))
                for m in range(8):
                    units.append(dict(q=4 + m, pb=0, d=128, kr=[(288 + (m // 4) * 128, 128, 0)], vu=4 + m // 2, sc=SC_DIFF,
                                      kind="diff%d" % (m % 2), u=4 + m // 2))
                for hq in range(8):
                    kvh = hq // 4
                    units.append(dict(q=12 + hq, pb=0, d=128, kr=[(544, 128, 0)], vu=8 + kvh, sc=SC_GQA, kind="plain", u=8 + hq))

                def load_chunk(un, ch):
                    si = ring[0] % NSL
                    ring[0] += 1
                    if ch[0] == "r":
                        rk = ch[1]
                        for (r0, nr, p0) in un["kr"]:
                            s_, off, bn = kblk(r0)
                            S.dma("sp", sl_kv[si], kslot[si][p0:p0 + nr, 0:NTOK], k_all[l][s_].ap()[rk * bn + off:rk * bn + off + nr, :], writes=[T_ks[si]])
                        vu = un["vu"]
                        vsrc = v_all[l][vu // 2].ap()[rk * 256 + (vu % 2) * 128:rk * 256 + (vu % 2 + 1) * 128, :].rearrange("p (t c) -> p t c", c=66)
                        S.dma("sp", sl_kv[si], vslot[si][:, 0:NKR, 0:66], vsrc, writes=[T_vs[si]])
                        return si, NKR
                    else:
                        for (r0, nr, p0) in un["kr"]:
                            S.dma("sp", sl_kv[si], kslot[si][p0:p0 + nr, 0:256], kc_loc[l].ap()[r0:r0 + nr, :], writes=[T_ks[si]])
                        vsrc = vc_loc[l].ap()[un["vu"] * 128:(un["vu"] + 1) * 128, :].rearrange("p (t c) -> p t c", c=66)
                        S.dma("sp", sl_kv[si], vslot[si][:, 0:2, 0:66], vsrc, writes=[T_vs[si]])
                        return si, 2

                if l == 0 and gr.g is not None and NL > 1:
                    if gr.g == 0:
                        convert_lazy(0, ["wout", "ffn"])
                    if NG >= 4:
                        sched = {1: ["wA", "wkvb", "wQ", "wqb", "wout"], 2: ["ffn0", "ffn1"], 3: ["ffn2"]}.get(gr.g, [])
                    else:
                        sched = ["wA", "wkvb", "wQ", "wqb", "wout", "ffn"] if gr.g == 0 else []
                    convert_lazy(1, sched)
                conv_iv = max(1, (len(units_n := [0] * 20) * (len(chunks) * NKR // 2)) // (len(conv_queue) + 1)) if conv_queue else 0
                pair_ctr = [0]
                dstate = {}

                def finalize(un, po, tpo):
                    u = un["u"]
                    ob, tob = osb.get()
                    rc, trc = rec.get()
                    S.op("dve", lambda e: e.reciprocal(out=rc[64:65, 0:T], in_=po[64:65, 0:T]), reads=[tpo], writes=[trc])
                    S.op("act", lambda e: e.activation(out=ob[0:64, 0:T], in_=po[0:64, 0:T], func=AF.Copy), reads=[tpo], writes=[tob])
                    pbc, tpbc = po, tpo
                    S.op("pe", lambda e: e.matmul(pbc[0:64, 0:T], onesf[64:65, 0:64], rc[64:65, 0:T], start=True, stop=True),
                         reads=[trc, T_c], writes=[tpbc])
                    hp = (u % 2) * 64
                    if un["kind"] == "plain":
                        S.op("dve", lambda e: e.tensor_tensor(out=yT[hp:hp + 64, u // 2, 0:T], in0=ob[0:64, 0:T], in1=pbc[0:64, 0:T], op=mult),
                             reads=[tob, tpbc], writes=[T_yT[u]])
                        return
                    o_n, ton = on_.get()
                    S.op("dve", lambda e: e.tensor_tensor(out=o_n[0:64, 0:T], in0=ob[0:64, 0:T], in1=pbc[0:64, 0:T], op=mult),
                         reads=[tob, tpbc], writes=[ton])
                    if un["kind"] == "diff0":
                        dstate[u] = (o_n, ton)
                        return
                    o1, to1 = dstate.pop(u)
                    dd, tdd = dtm.get()
                    S.op("dve", lambda e: e.scalar_tensor_tensor(out=dd[0:64, 0:T], in0=o_n[0:64, 0:T], scalar=misc[0:64, 1:2], in1=o1[0:64, 0:T],
                                                                  op0=mult, op1=add), reads=[ton, to1, T_misc], writes=[tdd])
                    sq, tsq = dtm.get()
                    S.op("act", lambda e: e.activation(out=sq[0:64, 0:T], in_=dd[0:64, 0:T], func=AF.Square), reads=[tdd], writes=[tsq])
                    pss, tpss = po, tpo
                    S.op("pe", lambda e: e.matmul(pss[0:64, 0:T], onesf[0:64, 0:64], sq[0:64, 0:T], start=True, stop=True),
                         reads=[tsq, T_c], writes=[tpss])
                    rr, trr = dtm.get()
                    S.op("act", lambda e: e.activation(out=rr[0:64, 0:T], in_=pss[0:64, 0:T], func=AF.Ln, scale=1.0 / 64, bias=EPS),
                         reads=[tpss], writes=[trr])
                    S.op("act", lambda e: e.activation(out=rr[0:64, 0:T], in_=rr[0:64, 0:T], func=AF.Exp, scale=-0.5), reads=[trr], writes=[trr])
                    S.op("dve", lambda e: e.scalar_tensor_tensor(out=yT[hp:hp + 64, u // 2, 0:T], in0=dd[0:64, 0:T], scalar=misc[0:64, 2:3], in1=rr[0:64, 0:T],
                                                                  op0=mult, op1=mult), reads=[tdd, trr, T_misc], writes=[T_yT[u]])

                work = [(un, ch) for un in units for ch in chunks]
                loaded = {}
                nxt = [0]

                def ensure(idx):
                    idx = min(idx, len(work) - 1)
                    while nxt[0] <= idx:
                        loaded[nxt[0]] = load_chunk(*work[nxt[0]])
                        nxt[0] += 1

                for ui, un in enumerate(units):
                    tl = []
                    for ci, ch in enumerate(chunks):
                        wi = ui * len(chunks) + ci
                        for t in range(0, NKR if ch[0] == "r" else 2, 2):
                            tl.append((wi, t))
                    po, tpo = ps[6 + obank[0] % 2], T_ps[6 + obank[0] % 2]
                    obank[0] += 1
                    pb_, d_ = un["pb"], un["d"]
                    qt = QT[un["q"]]
                    tq = T_QT[un["q"]]
                    tp = (pb_, 0) if pb_ == 96 else None

                    def qk(n):
                        wi, t = tl[n]
                        ensure(wi + 2)
                        si = loaded[wi][0]
                        b2 = (tcount[0] + n) % 3
                        kw = {"tile_position": tp} if tp else {}
                        for h_ in range(2):
                            S.op("pe", lambda e: e.matmul(ps[2 * b2 + h_][:, 0:T], kslot[si][pb_:pb_ + d_, (t + h_) * 128:(t + h_ + 1) * 128],
                                                         qt[pb_:pb_ + d_, 0:T], start=True, stop=True, **kw),
                                 reads=[T_ks[si], tq], writes=[T_ps[2 * b2 + h_]], sig=(h_ == 1))

                    n_t = len(tl)
                    qk(0)
                    if n_t > 1:
                        qk(1)
                    for n in range(n_t):
                        if n + 2 < n_t:
                            qk(n + 2)
                        wi, t = tl[n]
                        si = loaded[wi][0]
                        b2 = (tcount[0] + n) % 3
                        b3 = (tcount[0] + n) % 3
                        S.op("act", lambda e: e.activation(out=Pt[b3][:, :, 0:T], in_=psd[b2][:, :].rearrange("p (h n) -> p h n", h=2)[:, :, 0:T],
                                                           func=AF.Exp, scale=un["sc"]),
                             reads=[T_ps[2 * b2], T_ps[2 * b2 + 1]], writes=[T_P[b3]])
                        for h_ in range(2):
                            S.op("pe", lambda e: e.matmul(po[:, 0:T], vslot[si][:, t + h_, :], Pt[b3][:, h_, 0:T],
                                                         start=(n == 0 and h_ == 0), stop=(n == n_t - 1 and h_ == 1)),
                                 reads=[T_vs[si], T_P[b3]], writes=[tpo])
                        if n == min(1, n_t - 1) and pending:
                            f = pending.pop(0)
                            f()
                        pair_ctr[0] += 1
                        if conv_iv and conv_queue and pair_ctr[0] % conv_iv == 0:
                            S._wait("pool", ("c", "act", S.n["act"] - 1))
                            conv_pop(1)
                    tcount[0] += n_t
                    pending.append(lambda un=un, po=po, tpo=tpo: finalize(un, po, tpo))
                while pending:
                    pending.pop(0)()
                conv_pop(len(conv_queue))
                S.barrier()

        def phase_B3(l, gr):
            S.phase_reset()
            with ExitStack() as ph:
                prep = alloc_prep(ph)
                hT, T_hT = prep[0], prep[1]
                wout = sb("wout", [128, 8, 1024], BF16, ph)
                T_wout = Trk()
                sl_w = S.slot()
                S.dma("sp", sl_w, wout[:], wb[l]["wout"].ap().rearrange("p (k n) -> p k n", k=8), reads=[T_wb[l]["wout"]], writes=[T_wout])
                NR = 5
                ringb = [sb("fring%d" % i, [128, 2048], BF16, ph) for i in range(NR)]
                T_ring = [Trk() for _ in range(NR)]
                sl_ring = [S.slot() for _ in range(NR)]
                rcount = [0]
                actT = sb("actT", [128, 22, 512], BF16, ph)
                T_act = [Trk() for _ in range(11)]
                Gbc = [sb("Gbc%d" % i, [128, 1024], F32, ph) for i in range(2)]
                T_G = [Trk(), Trk()]
                tmp = Tmp(ph, 4, "tmpF")
                dg = Tmp(ph, 2, "dg")
                st2 = [sb("st2_%d" % i, [128, 8], F32, ph) for i in range(2)]
                T_st2 = [Trk(), Trk()]
                T = gr.T
                nt = len(gr.tts)
                for w in range(2):
                    for half in range(2):
                        pg, tpg = nb()
                        for q in range(4):
                            k = half * 4 + q
                            d_, td_ = dg.get()
                            S.op("dve", lambda e: e.tensor_scalar(out=d_[:, 0:128], in0=ident, scalar1=Gc_t[:, w, gr.j, k:k + 1], scalar2=None, op0=mult),
                                 reads=[T_c, T_A], writes=[td_])
                            S.op("pe", lambda e: e.matmul(pg[:, q * 128:(q + 1) * 128], onesf, d_[:, 0:128], start=True, stop=True),
                                 reads=[td_, T_c], writes=[tpg])
                        S.op("act", lambda e: e.activation(out=Gbc[w][:, half * 512:(half + 1) * 512], in_=pg[:, :], func=AF.Copy),
                             reads=[tpg], writes=[T_G[w]])

                def post(w, i, tt, pa, tpa, pb2, tpb2):
                    s_ = st2[i % 2]
                    ts_ = T_st2[i % 2]
                    jk, tjk = tmp.get()
                    S.op("act", lambda e: e.activation(out=jk[:, :], in_=pa[:, :], func=AF.Square, accum_out=s_[:, 0:1]), reads=[tpa], writes=[tjk, ts_])
                    jk2, tjk2 = tmp.get()
                    S.op("act", lambda e: e.activation(out=jk2[:, :], in_=pb2[:, :], func=AF.Square, accum_out=s_[:, 1:2]), reads=[tpb2], writes=[tjk2, ts_])
                    S.op("dve", lambda e: e.tensor_tensor(out=s_[:, 2:3], in0=s_[:, 0:1], in1=s_[:, 1:2], op=add), reads=[ts_], writes=[ts_])
                    S.op("act", lambda e: e.activation(out=s_[:, 3:4], in_=s_[:, 2:3], func=AF.Ln, scale=1.0 / 1024, bias=EPS),
                         reads=[ts_], writes=[ts_])
                    S.op("act", lambda e: e.activation(out=s_[:, 4:5], in_=s_[:, 3:4], func=AF.Exp, scale=-0.5), reads=[ts_], writes=[ts_])
                    for half, (p_, t_) in enumerate(((pa, tpa), (pb2, tpb2))):
                        tm, ttm = tmp.get()
                        S.op("dve", lambda e: e.scalar_tensor_tensor(out=tm[:, :], in0=p_[:, :], scalar=s_[:, 4:5], in1=Gbc[w][:, half * 512:(half + 1) * 512],
                                                                      op0=mult, op1=mult), reads=[t_, ts_, T_G[w]], writes=[ttm])
                        S.op("pool", lambda e: e.tensor_tensor(out=xs[:, tt, half * 512:(half + 1) * 512], in0=xs[:, tt, half * 512:(half + 1) * 512],
                                                               in1=tm[:, :], op=add), reads=[ttm], writes=[T_x[tt]])

                for i, tt in enumerate(gr.tts):
                    pa, tpa = nb()
                    pb2, tpb2 = nb()
                    for k in range(8):
                        for (p_, t_, half) in ((pa, tpa, 0), (pb2, tpb2, 1)):
                            S.op("pe", lambda e: e.matmul(p_[:, :], yT[:, k, i * 128:(i + 1) * 128], wout[:, k, half * 512:(half + 1) * 512],
                                                         start=(k == 0), stop=(k == 7)), reads=[T_yT[2 * k], T_yT[2 * k + 1], T_wout], writes=[t_])
                    post(0, i, tt, pa, tpa, pb2, tpb2)
                if l == 0:
                    dump("xa", xs[:, 0:4, :], [128, 4, 1024], F32)
                prep_hT(prep, l, gr, 1)

                def ring_load(name, jj):
                    ri = rcount[0] % NR
                    rcount[0] += 1
                    S.dma("sp", sl_ring[ri], ringb[ri][:], wb[l][name].ap()[jj], reads=[T_wb[l][name]], writes=[T_ring[ri]])
                    return ri

                for jj in range(11):
                    rg = ring_load("wg", jj)
                    ru = ring_load("wu", jj)
                    wgv = ringb[rg][:].rearrange("p (k n) -> p k n", k=8)
                    wuv = ringb[ru][:].rearrange("p (k n) -> p k n", k=8)
                    for s_ in range(2):
                        j = jj * 2 + s_
                        pg, tpg = nb()
                        pu, tpu = nb()
                        for k in range(8):
                            S.op("pe", lambda e: e.matmul(pg[:, 0:T], wgv[:, k, s_ * 128:(s_ + 1) * 128], hT[:, k, 0:T], start=(k == 0), stop=(k == 7)),
                                 reads=[T_ring[rg], T_hT], writes=[tpg])
                        for k in range(8):
                            S.op("pe", lambda e: e.matmul(pu[:, 0:T], wuv[:, k, s_ * 128:(s_ + 1) * 128], hT[:, k, 0:T], start=(k == 0), stop=(k == 7)),
                                 reads=[T_ring[ru], T_hT], writes=[tpu])
                        sg, tsg = tmp.get()
                        S.op("act", lambda e: e.activation(out=sg[:, 0:T], in_=pg[:, 0:T], func=AF.Silu), reads=[tpg], writes=[tsg])
                        S.op("dve", lambda e: e.tensor_tensor(out=actT[:, j, 0:T], in0=sg[:, 0:T], in1=pu[:, 0:T], op=mult),
                             reads=[tsg, tpu], writes=[T_act[jj]])
                rds = []
                for jj in range(11):
                    rds.append(None)
                for i0 in range(0, nt, 4):
                    tis = list(range(i0, min(i0 + 4, nt)))
                    acc = {}
                    for i in tis:
                        acc[i] = ((ps[2 * i], T_ps[2 * i]), (ps[2 * i + 1], T_ps[2 * i + 1]))
                    for jj in range(11):
                        rd = ring_load("wd", jj)
                        wdv = ringb[rd][:].rearrange("p (s n) -> p s n", s=2)
                        for s_ in range(2):
                            j = jj * 2 + s_
                            for i in tis:
                                for half in range(2):
                                    p_, t_ = acc[i][half]
                                    S.op("pe", lambda e: e.matmul(p_[:, :], actT[:, j, i * 128:(i + 1) * 128], wdv[:, s_, half * 512:(half + 1) * 512],
                                                                 start=(j == 0), stop=(j == 21)), reads=[T_act[jj], T_ring[rd]], writes=[t_])
                    for i in tis:
                        (pa, tpa), (pb2, tpb2) = acc[i]
                        post(1, i, gr.tts[i], pa, tpa, pb2, tpb2)
                S.barrier()

        S.barrier()
        def ctx_B(l):
            if dbg and l == 0:
                return
            phase_B1(l, cg)
            phase_B2(l, cg)
            phase_B3(l, cg)

        for l in range(NL):
            last = l == NL - 1
            layer_consts(l)
            if l == 0:
                phase_A(l, [cg] + groups)
                gather(l)
            else:
                phase_A(l, groups)
                gather(l)
                layer_consts(l - 1)
                ctx_B(l - 1)
                layer_consts(l)
                phase_A(l, [cg])
            if l + 1 < NL:
                phase_M(l + 1)
            if l == 0:
                dump("kc", kc_loc[0].ap(), [672, 256], BF16)
                dump("vc", vc_loc[0].ap(), [1280, 132], BF16)
                dump("modT", modT[:], [128, NL * 96], F32)
                dump("misc", misc[:], [128, 64], F32)
            waited = False
            for gr in groups:
                phase_B1(l, gr)
                if not waited:
                    gather_wait()
                    waited = True
                if l == 0:
                    for i in range(20):
                        dump("QT%d" % i, QT[i][:], [128, 512], BF16)
                phase_B2(l, gr)
                if l == 0:
                    dump("yT", yT[:], [128, 8, 512], BF16)
                phase_B3(l, gr)
                if l == 0:
                    dump("x1", xs[:, 0:4, :], [128, 4, 1024], F32)
        if dbg:
            ctx_B(0)
        S.barrier()
        for g4 in range(NG):
            S.dma("sp", sl_out, out_d[g4 * 512:(g4 + 1) * 512, :].rearrange("(t p) d -> p t d", p=128), xs[:, g4 * 4:(g4 + 1) * 4, :],
                  reads=T_x[g4 * 4:(g4 + 1) * 4])
        S.barrier()
    return nc


def _swap(w, r):
    n = w.shape[-1]
    idx = np.arange(n).reshape(-1, r)
    idx = np.concatenate([idx[:, r // 2:], idx[:, :r // 2]], axis=1).reshape(-1)
    return w[..., idx]


def _rope_tabs(pos, n_ctx):
    row = (pos // 64).astype(np.float32)
    col = (pos % 64).astype(np.float32)

    def tab(rot):
        quarter = rot // 4
        inv = (np.float32(10000.0) ** (-np.arange(quarter, dtype=np.float32) / np.float32(quarter))).astype(np.float32)
        ang = np.concatenate([row[:, None] * inv, col[:, None] * inv], axis=-1).astype(np.float32)
        return np.cos(ang).astype(np.float32), np.sin(ang).astype(np.float32)

    out = []
    for rot in (32, 64):
        c, s = tab(rot)
        C = np.concatenate([c, c], 1).T
        Sg = np.concatenate([-s, s], 1).T
        C = np.tile(C, (128 // rot, 1))
        Sg = np.tile(Sg, (128 // rot, 1))
        C = np.concatenate([C, np.ones((128, n_ctx), np.float32)], 1)
        Sg = np.concatenate([Sg, np.zeros((128, n_ctx), np.float32)], 1)
        out += [C, Sg]
    return np.ascontiguousarray(np.stack(out, 0), dtype=np.float32)


def _chunk_rows(w):
    L, R, N = w.shape
    return np.ascontiguousarray(w.reshape(L, R // 128, 128, N).transpose(0, 2, 1, 3))


def prep_shared(inp):
    f = lambda a: np.asarray(a, dtype=np.float32)
    L = f(inp["w_ada"]).shape[0]
    w_in = f(inp["w_in"])
    qa, ckv, kpe = w_in[..., 0:256], w_in[..., 256:384], w_in[..., 384:416]
    dq, dk, dv = w_in[..., 416:672], w_in[..., 672:928], w_in[..., 928:1184]
    gq, gk, gv = w_in[..., 1184:1696], w_in[..., 1696:1824], w_in[..., 1824:1952]
    A = [ckv, kpe, _swap(kpe, 32)]
    for b in range(2):
        blk = dk[..., b * 128:(b + 1) * 128]
        A += [blk, _swap(blk, 32)]
    A += [gk, _swap(gk, 64), dv, gv]
    wA = _chunk_rows(np.concatenate(A, -1))
    Q = [qa]
    for b in range(2):
        blk = dq[..., b * 128:(b + 1) * 128]
        Q += [blk, _swap(blk, 32)]
    for i in range(4):
        blk = np.concatenate([gq[..., i * 64:(i + 1) * 64], gq[..., (4 + i) * 64:(5 + i) * 64]], -1)
        Q += [blk, _swap(blk, 64)]
    wQ = _chunk_rows(np.concatenate(Q, -1))
    wqb = f(inp["w_mla_qb"])
    qb = []
    for h in range(4):
        blk = wqb[..., h * 96:(h + 1) * 96]
        qb += [blk, np.concatenate([blk[..., 0:64], _swap(blk[..., 64:96], 32)], -1)]
    wqb_r = _chunk_rows(np.concatenate(qb, -1))
    wkvb = f(inp["w_mla_kvb"]).reshape(L, 128, 4, 128)
    wkvb_r = np.ascontiguousarray(np.concatenate([wkvb[..., 0:64].reshape(L, 128, 256), wkvb[..., 64:128].reshape(L, 128, 256)], -1))
    wout_r = _chunk_rows(f(inp["w_out"]))
    w_ada = f(inp["w_ada"])
    wada_r = np.ascontiguousarray(w_ada.reshape(L, 8, 128, 12, 512).transpose(0, 3, 2, 1, 4).reshape(L, 12, 128, 4096))
    wg = f(inp["w_ffn_gate"]).reshape(L, 8, 128, 11, 256).transpose(0, 3, 2, 1, 4).reshape(L, 11, 128, 2048)
    wu = f(inp["w_ffn_up"]).reshape(L, 8, 128, 11, 256).transpose(0, 3, 2, 1, 4).reshape(L, 11, 128, 2048)
    wd = f(inp["w_ffn_down"]).reshape(L, 11, 2, 128, 1024).transpose(0, 1, 3, 2, 4).reshape(L, 11, 128, 2048)
    colv = np.zeros((L, 128, NV), np.float32)

    def cols(v):
        return v.reshape(L, -1, 128).transpose(0, 2, 1)

    colv[:, :, 0:8] = cols(f(inp["g_attn_pre"]))
    colv[:, :, 8:16] = cols(f(inp["g_attn_post"]))
    colv[:, :, 16:24] = cols(f(inp["g_ffn_pre"]))
    colv[:, :, 24:32] = cols(f(inp["g_ffn_post"]))
    colv[:, :, 32:80] = cols(f(inp["b_ada"]))
    colv[:, :, 80:82] = cols(f(inp["g_mla_q"]))
    colv[:, :, 82] = f(inp["g_mla_kv"])
    ggq, ggk = f(inp["g_gqa_q"]), f(inp["g_gqa_k"])
    colv[:, :, 83] = np.tile(ggq, (1, 2))
    colv[:, :, 84] = np.tile(_swap(ggq, 64), (1, 2))
    colv[:, :, 85] = np.tile(ggk, (1, 2))
    colv[:, :, 86] = np.tile(_swap(ggk, 64), (1, 2))
    colv[:, :, 87] = np.tile(f(inp["g_diff_sub"]), (1, 2))
    for i, nm in enumerate(("lambda_q1", "lambda_k1", "lambda_q2", "lambda_k2")):
        colv[:, :, 88 + 32 * i:120 + 32 * i] = f(inp[nm])[:, None, :]
    consts = np.zeros((128, 384), np.float32)
    consts[:, 0:128] = np.eye(128, dtype=np.float32)
    consts[:, 128:256] = 1.0
    consts[0:64, 256:320] = 1.0
    consts[64:128, 320:384] = 1.0
    return dict(w_ada_r=wada_r, colv=colv, w_inA=wA, w_inQ=wQ, w_qb_r=wqb_r, w_kvb_r=wkvb_r, w_out_r=wout_r,
                wg_r=np.ascontiguousarray(wg), wu_r=np.ascontiguousarray(wu), wd_r=np.ascontiguousarray(wd), consts=consts)


def make_in_maps(inp, NG):
    shared = prep_shared(inp)
    x = np.asarray(inp["x"], np.float32)
    c = np.asarray(inp["c"], np.float32)
    ctx = np.asarray(inp["ctx"], np.float32)
    c_ctx = np.asarray(inp["c_ctx"], np.float32)
    NTOK = NG * 512
    maps = []
    for core in range(8):
        b, q = core // 4, core % 4
        m = dict(shared)
        m["x_own"] = np.ascontiguousarray(x[b, q * NTOK:(q + 1) * NTOK, :])
        m["ctx_b"] = np.ascontiguousarray(ctx[b])
        cv = np.stack([c[b].reshape(8, 128).T, c_ctx.reshape(8, 128).T], -1)
        m["cvec"] = np.ascontiguousarray(cv.reshape(128, 16))
        m["rope"] = _rope_tabs(np.arange(q * NTOK, (q + 1) * NTOK), 256)
        maps.append(m)
    return maps


_NC_CACHE = {}


def run(inp, NG, ret_res=False):
    if NG not in _NC_CACHE:
        _NC_CACHE[NG] = build(NG)
    nc = _NC_CACHE[NG]
    maps = make_in_maps(inp, NG)
    res = run_bass_kernel_spmd(nc, maps, core_ids=list(range(8)))
    if ret_res:
        return res
    NTOK = NG * 512
    out = np.zeros((2, 4 * NTOK, 1024), np.float32)
    for core in range(8):
        b, q = core // 4, core % 4
        out[b, q * NTOK:(q + 1) * NTOK, :] = res.results[core]["out"]
    return out


def kernel(**inputs):
    return run(inputs, 4)
```

```python
import bisect
import math
from contextlib import ExitStack

import numpy as np
import concourse.bass as bass
import concourse.mybir as mybir
from concourse.bass_utils import run_bass_kernel_spmd

F32 = mybir.dt.float32
BF16 = mybir.dt.bfloat16
AF = mybir.ActivationFunctionType
ALU = mybir.AluOpType
EPS = 1e-6
NV = 216
SC_MLA = 1.0 / math.sqrt(96.0)
SC_DIFF = 1.0 / math.sqrt(32.0)
SC_GQA = 1.0 / 8.0


class Trk:
    __slots__ = ("w", "r")

    def __init__(self):
        self.w = None
        self.r = []


class Slot:
    pass


class Sched:
    COMPUTE = ("pe", "act", "dve", "pool")

    def __init__(self, nc, es):
        self.nc = nc
        self.es = es
        self.eng = {"pe": nc.tensor, "act": nc.scalar, "dve": nc.vector, "pool": nc.gpsimd, "sp": nc.sync}
        self.sem = {e: es.enter_context(nc.semaphore("s_" + e)) for e in self.COMPUTE}
        self.n = {e: 0 for e in self.eng}
        self.last = {e: None for e in self.eng}
        self.sig = {e: [] for e in self.COMPUTE}
        self.known = {e: {} for e in self.eng}
        self.slots = []
        self.pool = []
        self.pidx = 0
        self.nwait = 0

    def _sigval(self, e, idx):
        s = self.sig[e]
        p = bisect.bisect_left(s, idx)
        if p < len(s):
            return p + 1
        li = self.n[e] - 1
        assert li >= idx
        self.last[e].then_inc(self.sem[e], 1)
        s.append(li)
        return len(s)

    def _wait(self, q, dep):
        if dep[0] == "c":
            _, e, idx = dep
            if e == q and q == "pe":
                return
            key = ("c", e)
            kv = self.known[q].get(key, 0)
            s = self.sig[e]
            p = bisect.bisect_left(s, idx)
            if p < len(s) and p + 1 <= kv:
                return
            v = self._sigval(e, idx)
            if v <= kv:
                return
            self.eng[q].wait_ge(self.sem[e], v)
            self.nwait += 1
            self.known[q][key] = v
        else:
            _, sl, val = dep
            val = sl.cnt
            key = ("d", id(sl))
            if self.known[q].get(key, 0) >= val:
                return
            self.eng[q].wait_ge(sl.sem, val)
            self.nwait += 1
            self.known[q][key] = val

    def _deps(self, q, reads, writes):
        for t in reads:
            if t.w is not None:
                self._wait(q, t.w)
        for t in writes:
            if t.w is not None:
                self._wait(q, t.w)
            for d in t.r:
                self._wait(q, d)

    def op(self, q, fn, reads=(), writes=(), sig=False):
        self._deps(q, reads, writes)
        ins = fn(self.eng[q])
        idx = self.n[q]
        self.n[q] += 1
        self.last[q] = ins
        if sig:
            ins.then_inc(self.sem[q], 1)
            self.sig[q].append(idx)
        me = ("c", q, idx)
        for t in reads:
            t.r.append(me)
        for t in writes:
            t.w = me
            t.r = []
        return ins

    def pslot(self):
        sl = Slot()
        sl.sem = self.es.enter_context(self.nc.semaphore("d%d" % len(self.slots)))
        sl.cnt = 0
        self.slots.append(sl)
        return sl

    def slot(self):
        if self.pidx >= len(self.pool):
            self.pool.append(self.pslot())
        sl = self.pool[self.pidx]
        self.pidx += 1
        return sl

    def phase_reset(self):
        self.barrier()
        self.pidx = 0

    def dma(self, q, sl, out, in_, reads=(), writes=()):
        self._deps(q, reads, writes)
        ins = self.eng[q].dma_start(out=out, in_=in_)
        ins.then_inc(sl.sem, 16)
        sl.cnt += 16
        me = ("d", sl, sl.cnt)
        for t in reads:
            t.r.append(me)
        for t in writes:
            t.w = me
            t.r = []
        return ins

    def barrier(self):
        for q in self.eng:
            for e in self.COMPUTE:
                if e != q and self.n[e] > 0:
                    self._wait(q, ("c", e, self.n[e] - 1))
            for sl in self.slots:
                if sl.cnt > 0:
                    self._wait(q, ("d", sl, sl.cnt))


def build(NG=4, NL=2, dbg=False):
    nc = bass.Bass("TRN2", target_bir_lowering=False)
    NTOK = NG * 512
    NTT = NG * 4
    NKR = NG * 4
    mult, add = ALU.mult, ALU.add

    def din(name, shape, dt=F32):
        return nc.dram_tensor(name, shape, dt, kind="ExternalInput").ap()

    x_d = din("x_own", [NTOK, 1024])
    ctx_d = din("ctx_b", [256, 1024])
    cvec_d = din("cvec", [128, 16])
    wada_d = din("w_ada_r", [NL, 12, 128, 8 * 512])
    colv_d = din("colv", [NL, 128, NV])
    wA_d = din("w_inA", [NL, 128, 8, 1344])
    wQ_d = din("w_inQ", [NL, 128, 8, 1792])
    wqb_d = din("w_qb_r", [NL, 128, 2, 768])
    wkvb_d = din("w_kvb_r", [NL, 128, 512])
    wout_d = din("w_out_r", [NL, 128, 8, 1024])
    wg_d = din("wg_r", [NL, 11, 128, 2048])
    wu_d = din("wu_r", [NL, 11, 128, 2048])
    wd_d = din("wd_r", [NL, 11, 128, 2048])
    rope_d = din("rope", [4, 128, NTOK + 256])
    const_d = din("consts", [128, 384])
    out_d = nc.dram_tensor("out", [NTOK, 1024], F32, kind="ExternalOutput").ap()

    KB = [0, 128, 288, 416, 544, 672]
    k_loc = [[nc.dram_tensor("k_loc%d_%d" % (l, s_), [KB[s_ + 1] - KB[s_], NTOK], BF16) for s_ in range(5)] for l in range(NL)]
    k_all = [[nc.dram_tensor("k_all%d_%d" % (l, s_), [4 * (KB[s_ + 1] - KB[s_]), NTOK], BF16) for s_ in range(5)] for l in range(NL)]
    v_loc = [[nc.dram_tensor("v_loc%d_%d" % (l, s_), [256, NKR * 66], BF16) for s_ in range(5)] for l in range(NL)]
    v_all = [[nc.dram_tensor("v_all%d_%d" % (l, s_), [4 * 256, NKR * 66], BF16) for s_ in range(5)] for l in range(NL)]

    wb = []
    for l in range(NL):
        wb.append(dict(
            wA=nc.dram_tensor("wA_b%d" % l, [128, 8 * 1344], BF16), wkvb=nc.dram_tensor("wkvb_b%d" % l, [128, 512], BF16),
            wQ=nc.dram_tensor("wQ_b%d" % l, [128, 8 * 1792], BF16), wqb=nc.dram_tensor("wqb_b%d" % l, [128, 2 * 768], BF16),
            wout=nc.dram_tensor("wout_b%d" % l, [128, 8 * 1024], BF16),
            wg=nc.dram_tensor("wg_b%d" % l, [11, 128, 2048], BF16), wu=nc.dram_tensor("wu_b%d" % l, [11, 128, 2048], BF16),
            wd=nc.dram_tensor("wd_b%d" % l, [11, 128, 2048], BF16)))

    def kblk(r0):
        for s_ in range(5):
            if KB[s_] <= r0 < KB[s_ + 1]:
                return s_, r0 - KB[s_], KB[s_ + 1] - KB[s_]
        raise ValueError
    kc_loc = [nc.dram_tensor("kc_loc%d" % l, [672, 256], BF16) for l in range(NL)]
    vc_loc = [nc.dram_tensor("vc_loc%d" % l, [1280, 2 * 66], BF16) for l in range(NL)]

    with ExitStack() as es:
        S = Sched(nc, es)

        uid = [0]

        def sb(name, shape, dt, stack=None):
            uid[0] += 1
            return (stack or es).enter_context(nc.sbuf_tensor("sb%d_%s" % (uid[0], name), shape, dt))

        xs = sb("xs", [128, NTT + 2, 1024], F32)
        T_x = [Trk() for _ in range(NTT + 2)]
        cst = sb("cst", [128, 384], F32)
        T_c = Trk()
        ident = cst[:, 0:128]
        onesf = cst[:, 128:256]
        bd64 = cst[:, 256:384]
        colv = sb("colv", [128, NL, NV], F32)
        T_colv = Trk()
        modT = sb("modT", [128, NL * 96], F32)
        T_mod = Trk()
        misc = sb("misc", [128, 64], F32)
        T_misc = Trk()
        A_t = sb("A_t", [128, 2, 2, 8], F32)
        Gc_t = sb("Gc_t", [128, 2, 2, 8], F32)
        T_A = Trk()
        QT = [sb("QT%d" % i, [128, 512], BF16) for i in range(20)]
        T_QT = [Trk() for _ in range(20)]
        Gbc = [sb("Gbc%d" % i, [128, 1024], F32) for i in range(2)]
        T_G = [Trk(), Trk()]
        wout = sb("wout", [128, 8, 1024], BF16)
        T_wout = Trk()
        cur_wout = [-1]
        yT = sb("yT", [128, 8, 512], BF16)
        T_yT = [Trk() for _ in range(16)]

        psd = [es.enter_context(nc.psum_tensor("psd%d" % i, [128, 1024], F32)) for i in range(4)]
        ps = [psd[i // 2][:, (i % 2) * 512:(i % 2 + 1) * 512] for i in range(8)]
        T_ps = [Trk() for _ in range(8)]
        bank_rr = [0]

        def nb(lo=0, hi=8):
            i = lo + (bank_rr[0] % (hi - lo))
            bank_rr[0] += 1
            return ps[i], T_ps[i]

        dbg_done = set()

        def dump(name, src_ap, shape, dt):
            if not dbg or name in dbg_done:
                return
            dbg_done.add(name)
            d = nc.dram_tensor("dbg_" + name, shape, dt, kind="ExternalOutput").ap()
            S.phase_reset()
            sl = S.slot()
            S.dma("sp", sl, d, src_ap)
            S.barrier()

        sl_wout = S.pslot()
        sl_misc = S.pslot()
        sl_x = S.pslot()
        sl_out = S.pslot()

        def mcol(l, c, j):
            o = (l * 48 + c) * 2 + j
            return modT[:, o:o + 1]

        S.dma("sp", sl_misc, cst[:], const_d, writes=[T_c])
        S.dma("sp", sl_misc, colv[:], colv_d.rearrange("l p n -> p l n"), writes=[T_colv])
        for g4 in range(NG):
            S.dma("sp", sl_x, xs[:, g4 * 4:(g4 + 1) * 4, :], x_d[g4 * 512:(g4 + 1) * 512, :].rearrange("(t p) d -> p t d", p=128),
                  writes=T_x[g4 * 4:(g4 + 1) * 4])
        S.dma("sp", sl_x, xs[:, NTT:NTT + 2, :], ctx_d.rearrange("(t p) d -> p t d", p=128), writes=T_x[NTT:NTT + 2])
        S.op("pool", lambda e: e.memset(misc[:, 0:1], -0.5), writes=[T_misc])
        for i in range(4, 20):
            S.op("pool", lambda e: e.memset(QT[i][:], 0.0), writes=[T_QT[i]])
        mhalf = misc[:, 0:1]

        T_wb = [dict((k_, Trk()) for k_ in wb[l]) for l in range(NL)]
        sl_wb = [dict((k_, S.pslot()) for k_ in wb[l]) for l in range(NL)]
        def convert(l, names):
            def cv(name, dst, src, extra=()):
                S.dma("pool", sl_wb[l][name], dst, src, reads=list(extra), writes=[T_wb[l][name]])
            for name in names:
                if name == "wA":
                    for k in range(8):
                        cv("wA", wb[l]["wA"].ap()[:, k * 1344:(k + 1) * 1344], wA_d[l, :, k, :])
                elif name == "wkvb":
                    cv("wkvb", wb[l]["wkvb"].ap(), wkvb_d[l])
                elif name == "wQ":
                    for k in range(8):
                        cv("wQ", wb[l]["wQ"].ap()[:, k * 1792:(k + 1) * 1792], wQ_d[l, :, k, :])
                elif name == "wqb":
                    for c in range(2):
                        cv("wqb", wb[l]["wqb"].ap()[:, c * 768:(c + 1) * 768], wqb_d[l, :, c, :])
                elif name == "wout":
                    for k in range(8):
                        cv("wout", wb[l]["wout"].ap()[:, k * 1024:(k + 1) * 1024], wout_d[l, :, k, :])
                elif name.startswith("ffn"):
                    j0, j1 = {"ffn": (0, 11), "ffn0": (0, 4), "ffn1": (4, 8), "ffn2": (8, 11)}[name]
                    for jj in range(j0, j1):
                        cv("wg", wb[l]["wg"].ap()[jj], wg_d[l, jj])
                        cv("wu", wb[l]["wu"].ap()[jj], wu_d[l, jj])
                    for jj in range(j0, j1):
                        cv("wd", wb[l]["wd"].ap()[jj], wd_d[l, jj])

        conv_queue = []

        def convert_lazy(l, names):
            real = S.dma

            def rec(*a, **k):
                conv_queue.append((a, k))
            S.dma = rec
            try:
                convert(l, names)
            finally:
                S.dma = real

        def conv_pop(n=1):
            for _ in range(n):
                if conv_queue:
                    a, k = conv_queue.pop(0)
                    S.dma(*a, **k)

        convert(0, ["wA", "wkvb"])

        def phase_M(l, after=None):
            S.phase_reset()
            with ExitStack() as ph:
                scv = sb("scv", [128, 16], F32, ph)
                T_scv = Trk()
                wad = [sb("wad%d" % i, [128, 8, 512], F32, ph) for i in range(3)]
                T_wad = [Trk(), Trk(), Trk()]
                sl_wad = [S.slot(), S.slot(), S.slot()]
                sl_c = S.slot()
                S.dma("sp", sl_c, scv[:], cvec_d, writes=[T_scv])
                S.op("act", lambda e: e.activation(out=scv[:], in_=scv[:], func=AF.Silu), reads=[T_scv], writes=[T_scv])
                modrow = sb("modrow", [2, 6144], F32, ph)
                T_mr = Trk()
                for cb in range(12):
                    bi = cb % 3
                    S.dma("sp", sl_wad[bi], wad[bi][:], wada_d[l, cb].rearrange("p (k n) -> p k n", k=8), writes=[T_wad[bi]])
                    if cb == 11 and after is not None:
                        after(T_wad[bi])
                    pr, tpr = nb()
                    for k in range(8):
                        S.op("pe", lambda e: e.matmul(pr[0:2, :], scv[:, 2 * k:2 * k + 2], wad[bi][:, k, :], start=(k == 0), stop=(k == 7)),
                             reads=[T_wad[bi], T_scv], writes=[tpr])
                    S.op("dve", lambda e: e.tensor_copy(out=modrow[0:2, cb * 512:(cb + 1) * 512], in_=pr[0:2, :]), reads=[tpr], writes=[T_mr])
                pm, tpm = nb()
                for c in range(48):
                    S.op("pe", lambda e: e.matmul(pm[:, 2 * c:2 * c + 2], modrow[0:2, c * 128:(c + 1) * 128], ident[0:2, 0:2], start=True, stop=True),
                         reads=[T_mr, T_c], writes=[tpm])
                for j in range(2):
                    S.op("dve", lambda e: e.tensor_tensor(
                        out=modT[:, l * 96:(l + 1) * 96].rearrange("p (c j) -> p c j", j=2)[:, :, j],
                        in0=pm[:, 0:96].rearrange("p (c j) -> p c j", j=2)[:, :, j],
                        in1=colv[:, l, 32:80], op=add), reads=[tpm, T_colv], writes=[T_mod])
                S.barrier()

        def rest_conversions(t_last):
            S._deps("pool", [t_last], [])
            convert(0, ["wQ", "wqb", "wout"])

        phase_M(0, after=rest_conversions)

        def layer_consts(l):
            lam_init = 0.8 - 0.6 * math.exp(-0.3 * l)
            for w in range(2):
                for j in range(2):
                    sc0 = (l * 48 + (1 + 3 * w) * 8) * 2
                    gt0 = (l * 48 + (2 + 3 * w) * 8) * 2
                    sc_v = modT[:, sc0:sc0 + 16].rearrange("p (k j) -> p k j", j=2)[:, :, j]
                    gt_v = modT[:, gt0:gt0 + 16].rearrange("p (k j) -> p k j", j=2)[:, :, j]
                    gpre = colv[:, l, 16 * w:16 * w + 8]
                    gpost = colv[:, l, 16 * w + 8:16 * w + 16]
                    S.op("dve", lambda e: e.scalar_tensor_tensor(out=A_t[:, w, j, :], in0=sc_v, scalar=1.0, in1=gpre,
                                                                  op0=add, op1=mult), reads=[T_mod, T_colv], writes=[T_A])
                    S.op("dve", lambda e: e.tensor_tensor(out=Gc_t[:, w, j, :], in0=gt_v, in1=gpost, op=mult),
                         reads=[T_mod, T_colv], writes=[T_A])
            S.op("dve", lambda e: e.tensor_tensor(out=misc[:, 8:40], in0=colv[:, l, 88:120], in1=colv[:, l, 120:152], op=mult),
                 reads=[T_colv, T_misc], writes=[T_misc])
            S.op("dve", lambda e: e.reduce_sum(out=misc[:, 4:5], in_=misc[:, 8:40], axis=mybir.AxisListType.X),
                 reads=[T_misc], writes=[T_misc])
            S.op("dve", lambda e: e.tensor_tensor(out=misc[:, 8:40], in0=colv[:, l, 152:184], in1=colv[:, l, 184:216], op=mult),
                 reads=[T_colv, T_misc], writes=[T_misc])
            S.op("dve", lambda e: e.reduce_sum(out=misc[:, 5:6], in_=misc[:, 8:40], axis=mybir.AxisListType.X),
                 reads=[T_misc], writes=[T_misc])
            S.op("act", lambda e: e.activation(out=misc[:, 4:6], in_=misc[:, 4:6], func=AF.Exp), reads=[T_misc], writes=[T_misc])
            S.op("dve", lambda e: e.scalar_tensor_tensor(out=misc[:, 1:2], in0=misc[:, 5:6], scalar=-lam_init, in1=misc[:, 4:5],
                                                          op0=add, op1=ALU.subtract), reads=[T_misc], writes=[T_misc])
            S.op("dve", lambda e: e.tensor_scalar(out=misc[:, 2:3], in0=colv[:, l, 87:88], scalar1=(1.0 - lam_init), scalar2=None,
                                                  op0=mult), reads=[T_colv, T_misc], writes=[T_misc])

        class G:
            pass

        groups = []
        for g in range(NG):
            gr = G()
            gr.tts = [g * 4 + i for i in range(4)]
            gr.T = 512
            gr.j = 0
            gr.tab0 = g * 512
            gr.g = g
            gr.ctx = False
            groups.append(gr)
        cg = G()
        cg.tts = [NTT, NTT + 1]
        cg.T = 256
        cg.j = 1
        cg.tab0 = NTOK
        cg.g = None
        cg.ctx = True

        def prep_hT(ph_t, l, gr, w):
            hT, T_hT, xn, T_xn, junk, T_junk, st, T_st = ph_t
            for i, tt in enumerate(gr.tts):
                b = i % 2
                s_ = st[i % 4]
                ts_ = T_st[i % 4]
                S.op("act", lambda e: e.activation(out=junk[:], in_=xs[:, tt, :], func=AF.Square, accum_out=s_[:, 0:1]),
                     reads=[T_x[tt]], writes=[T_junk, ts_])
                S.op("act", lambda e: e.activation(out=s_[:, 1:2], in_=s_[:, 0:1], func=AF.Ln, scale=1.0 / 1024, bias=EPS),
                     reads=[ts_], writes=[ts_])
                S.op("act", lambda e: e.activation(out=s_[:, 2:3], in_=s_[:, 1:2], func=AF.Exp, scale=-0.5),
                     reads=[ts_], writes=[ts_])
                S.op("dve", lambda e: e.tensor_scalar(out=xn[b][:], in0=xs[:, tt, :], scalar1=s_[:, 2:3], scalar2=None, op0=mult),
                     reads=[T_x[tt], ts_], writes=[T_xn[b]])
                for half in range(2):
                    pt, tpt = nb()
                    for q in range(4):
                        k = half * 4 + q
                        S.op("pe", lambda e: e.transpose(pt[:, q * 128:(q + 1) * 128], xn[b][:, k * 128:(k + 1) * 128], ident),
                             reads=[T_xn[b], T_c], writes=[tpt])
                    for q in range(4):
                        k = half * 4 + q
                        acol = A_t[:, w, gr.j, k:k + 1]
                        bcol = mcol(l, (3 * w) * 8 + k, gr.j)
                        o_ = hT[:, k, i * 128:(i + 1) * 128]
                        i_ = pt[:, q * 128:(q + 1) * 128]
                        if q % 2 == 0:
                            S.op("act", lambda e: e.activation(out=o_, in_=i_, func=AF.Identity, scale=acol, bias=bcol),
                                 reads=[tpt, T_A, T_mod], writes=[T_hT])
                        else:
                            S.op("dve", lambda e: e.tensor_scalar(out=o_, in0=i_, scalar1=acol, scalar2=bcol, op0=mult, op1=add),
                                 reads=[tpt, T_A, T_mod], writes=[T_hT])

        def alloc_prep(ph):
            hT = sb("hT", [128, 8, 512], BF16, ph)
            xn = [sb("xn%d" % i, [128, 1024], F32, ph) for i in range(2)]
            junk = sb("junk", [128, 1024], BF16, ph)
            st = [sb("st%d" % i, [128, 4], F32, ph) for i in range(4)]
            return (hT, Trk(), xn, [Trk(), Trk()], junk, Trk(), st, [Trk() for _ in range(4)])

        def proj_block(W, T_W, c0, wd, hT, T_hT, T):
            pb, tpb = nb()
            for k in range(8):
                S.op("pe", lambda e: e.matmul(pb[0:wd, 0:T], W[:, k, c0:c0 + wd], hT[:, k, 0:T], start=(k == 0), stop=(k == 7)),
                     reads=[T_W, T_hT], writes=[tpb])
            return pb, tpb

        class Tmp:
            def __init__(self, ph, n, name, dt=F32):
                self.t = [sb("%s%d" % (name, i), [128, 512], dt, ph) for i in range(n)]
                self.k = [Trk() for _ in range(n)]
                self.i = 0

            def get(self):
                i = self.i % len(self.t)
                self.i += 1
                return self.t[i], self.k[i]

        def rstd_bcast(tmp, src_list, lhs, n_feat, T, rows=128):
            pst, tpst = nb()
            for i, (p_, t_) in enumerate(src_list):
                sq, tsq = tmp.get()
                S.op("act", lambda e: e.activation(out=sq[0:rows, 0:T], in_=p_[0:rows, 0:T], func=AF.Square), reads=[t_], writes=[tsq])
                S.op("pe", lambda e: e.matmul(pst[0:rows, 0:T], lhs[0:rows, 0:rows], sq[0:rows, 0:T], start=(i == 0), stop=(i == len(src_list) - 1)),
                     reads=[tsq, T_c], writes=[tpst])
            r, tr = tmp.get()
            S.op("act", lambda e: e.activation(out=r[0:rows, 0:T], in_=pst[0:rows, 0:T], func=AF.Ln, scale=1.0 / n_feat, bias=EPS),
                 reads=[tpst], writes=[tr])
            S.op("act", lambda e: e.activation(out=r[0:rows, 0:T], in_=r[0:rows, 0:T], func=AF.Exp, scale=-0.5), reads=[tr], writes=[tr])
            return r, tr

        def rope_comb(tmp, out_ap, t_out, a, ta, b_, tb, Ct, St, T_tab, r0, r1, T, outs=None):
            t1, tt1 = tmp.get()
            t2, tt2 = tmp.get()
            S.op("dve", lambda e: e.tensor_tensor(out=t1[r0:r1, 0:T], in0=a[r0:r1, 0:T], in1=Ct[r0:r1, 0:T], op=mult),
                 reads=[ta, T_tab], writes=[tt1])
            S.op("dve", lambda e: e.tensor_tensor(out=t2[r0:r1, 0:T], in0=b_[r0:r1, 0:T], in1=St[r0:r1, 0:T], op=mult),
                 reads=[tb, T_tab], writes=[tt2])
            if outs is not None:
                for (oap, otr, ra, rb) in outs:
                    S.op("dve", lambda e: e.tensor_tensor(out=oap, in0=t1[ra:rb, 0:T], in1=t2[ra:rb, 0:T], op=add),
                         reads=[tt1, tt2], writes=[otr])
                return
            S.op("dve", lambda e: e.tensor_tensor(out=out_ap, in0=t1[r0:r1, 0:T], in1=t2[r0:r1, 0:T], op=add),
                 reads=[tt1, tt2], writes=[t_out])

        def load_tabs(tabs, T_tab, sl_tab, gr):
            for ti in range(4):
                S.dma("sp", sl_tab, tabs[ti][:, 0:gr.T], rope_d[ti, :, gr.tab0:gr.tab0 + gr.T], writes=[T_tab])

        def gqa_block(tmp, pb, tpb, pbs, tpbs, gcol, gscol, tabs, T_tab, out_ap, t_out, T, outs=None):
            r, tr = rstd_bcast(tmp, [(pb, tpb)], bd64, 64.0, T)
            xg, txg = tmp.get()
            xw, txw = tmp.get()
            S.op("dve", lambda e: e.scalar_tensor_tensor(out=xg[:, 0:T], in0=pb[:, 0:T], scalar=gcol, in1=r[:, 0:T], op0=mult, op1=mult),
                 reads=[tpb, tr, T_colv], writes=[txg])
            S.op("dve", lambda e: e.scalar_tensor_tensor(out=xw[:, 0:T], in0=pbs[:, 0:T], scalar=gscol, in1=r[:, 0:T], op0=mult, op1=mult),
                 reads=[tpbs, tr, T_colv], writes=[txw])
            rope_comb(tmp, out_ap, t_out, xg, txg, xw, txw, tabs[2], tabs[3], T_tab, 0, 128, T, outs=outs)

        def phase_A(l, glist):
            S.phase_reset()
            with ExitStack() as ph:
                _p = alloc_prep(ph)
                preps = [_p, _p]
                wA = sb("wA", [128, 8, 1344], BF16, ph)
                T_wA = Trk()
                wkvb = sb("wkvb", [128, 512], BF16, ph)
                T_wkvb = Trk()
                sl_w = S.slot()
                S.dma("sp", sl_w, wA[:], wb[l]["wA"].ap().rearrange("p (k n) -> p k n", k=8), reads=[T_wb[l]["wA"]], writes=[T_wA])
                S.dma("sp", sl_w, wkvb[:], wb[l]["wkvb"].ap(), reads=[T_wb[l]["wkvb"]], writes=[T_wkvb])
                _tb = [sb("tab%d" % i, [128, 512], F32, ph) for i in range(4)]
                _tt = Trk()
                _ts = S.slot()
                tabs2 = [_tb, _tb]
                T_tab2 = [_tt, _tt]
                sl_tab2 = [_ts, _ts]
                tmp = Tmp(ph, 8, "tmpA")
                ckvn = sb("ckvn", [128, 512], BF16, ph)
                T_ckvn = Trk()
                kst = Tmp(ph, 3, "kst", BF16)
                sl_kst = [S.slot() for _ in range(3)]
                vst = sb("vst", [128, 10, 4, 66], BF16, ph)
                T_vst = Trk()
                sl_vst = S.slot()
                S.op("dve", lambda e: e.memset(vst[:, :, :, 64:65], 1.0), writes=[T_vst])
                S.op("dve", lambda e: e.memset(vst[:, :, :, 65:66], 0.0), writes=[T_vst])
                T_dram = Trk()

                def kout(tile_, ttile, idx, r0, nrows, gr, prow0=0):
                    if gr.ctx:
                        dst = kc_loc[l].ap()[r0:r0 + nrows, 0:256]
                    else:
                        s_, off, _ = kblk(r0)
                        dst = k_loc[l][s_].ap()[off:off + nrows, gr.g * 512:(gr.g + 1) * 512]
                    S.dma("sp", sl_kst[idx], dst, tile_[prow0:prow0 + nrows, 0:gr.T], reads=[ttile], writes=[])

                for gi, gr in enumerate(glist):
                    T = gr.T
                    nt = len(gr.tts)
                    prep = preps[gi % 2]
                    hT, T_hT = prep[0], prep[1]
                    tabs, T_tab, sl_tab = tabs2[gi % 2], T_tab2[gi % 2], sl_tab2[gi % 2]
                    prep_hT(prep, l, gr, 0)
                    load_tabs(tabs, T_tab, sl_tab, gr)
                    pb, tpb = proj_block(wA, T_wA, 0, 128, hT, T_hT, T)
                    r, tr = rstd_bcast(tmp, [(pb, tpb)], onesf, 128.0, T)
                    S.op("dve", lambda e: e.scalar_tensor_tensor(out=ckvn[:, 0:T], in0=pb[:, 0:T], scalar=colv[:, l, 82:83], in1=r[:, 0:T],
                                                                  op0=mult, op1=mult), reads=[tpb, tr, T_colv], writes=[T_ckvn])
                    for hp in range(2):
                        pk, tpk = nb()
                        S.op("pe", lambda e: e.matmul(pk[:, 0:T], wkvb[:, hp * 128:(hp + 1) * 128], ckvn[:, 0:T], start=True, stop=True),
                             reads=[T_wkvb, T_ckvn], writes=[tpk])
                        idx = kst.i % 3
                        ks, tks = kst.get()
                        S.op("act", lambda e: e.activation(out=ks[:, 0:T], in_=pk[:, 0:T], func=AF.Copy), reads=[tpk], writes=[tks])
                        kout(ks, tks, idx, hp * 128, 128, gr)
                    for i in range(nt):
                        pv, tpv = nb()
                        S.op("pe", lambda e: e.matmul(pv[:, 0:256], ckvn[:, i * 128:(i + 1) * 128], wkvb[:, 256:512], start=True, stop=True),
                             reads=[T_wkvb, T_ckvn], writes=[tpv])
                        S.op("dve", lambda e: e.tensor_copy(out=vst[:, 0:4, i, 0:64], in_=pv[:, 0:256].rearrange("p (u d) -> p u d", d=64)),
                             reads=[tpv], writes=[T_vst])
                    pb, tpb = proj_block(wA, T_wA, 128, 32, hT, T_hT, T)
                    pbs, tpbs = proj_block(wA, T_wA, 160, 32, hT, T_hT, T)
                    idx = kst.i % 3
                    ks, tks = kst.get()
                    rope_comb(tmp, ks[0:32, 0:T], tks, pb, tpb, pbs, tpbs, tabs[0], tabs[1], T_tab, 0, 32, T)
                    kout(ks, tks, idx, 256, 32, gr)
                    for b in range(2):
                        pb, tpb = proj_block(wA, T_wA, 192 + b * 256, 128, hT, T_hT, T)
                        pbs, tpbs = proj_block(wA, T_wA, 320 + b * 256, 128, hT, T_hT, T)
                        idx = kst.i % 3
                        ks, tks = kst.get()
                        rope_comb(tmp, ks[:, 0:T], tks, pb, tpb, pbs, tpbs, tabs[0], tabs[1], T_tab, 0, 128, T)
                        kout(ks, tks, idx, 288 + b * 128, 128, gr)
                    pb, tpb = proj_block(wA, T_wA, 704, 128, hT, T_hT, T)
                    pbs, tpbs = proj_block(wA, T_wA, 832, 128, hT, T_hT, T)
                    idx = kst.i % 3
                    ks, tks = kst.get()
                    gqa_block(tmp, pb, tpb, pbs, tpbs, colv[:, l, 85:86], colv[:, l, 86:87], tabs, T_tab, ks[:, 0:T], tks, T)
                    kout(ks, tks, idx, 544, 128, gr)
                    for i in range(nt):
                        pv, tpv = nb()
                        for k in range(8):
                            S.op("pe", lambda e: e.matmul(pv[:, 0:384], hT[:, k, i * 128:(i + 1) * 128], wA[:, k, 960:1344],
                                                         start=(k == 0), stop=(k == 7)), reads=[T_wA, T_hT], writes=[tpv])
                        S.op("dve", lambda e: e.tensor_copy(out=vst[:, 4:10, i, 0:64], in_=pv[:, 0:384].rearrange("p (u d) -> p u d", d=64)),
                             reads=[tpv], writes=[T_vst])
                    if gr.ctx:
                        dst = vc_loc[l].ap().rearrange("(u p) (t c) -> p u t c", p=128, c=66)
                        S.dma("sp", sl_vst, dst, vst[:, :, 0:2, :], reads=[T_vst])
                    else:
                        for s_ in range(5):
                            dst = v_loc[l][s_].ap().rearrange("(u p) (t c) -> p u t c", p=128, c=66)[:, :, gr.g * 4:(gr.g + 1) * 4, :]
                            S.dma("sp", sl_vst, dst, vst[:, 2 * s_:2 * s_ + 2, :, :], reads=[T_vst])
                S.barrier()

        cc_sem = es.enter_context(nc.semaphore("cc"))
        cc_cnt = [0]

        def gather(l):
            S.barrier()
            for (a, b_) in list(zip(k_loc[l], k_all[l])) + list(zip(v_loc[l], v_all[l])):
                nc.gpsimd.collective_compute("AllGather", ALU.bypass, replica_groups=[[0, 1, 2, 3], [4, 5, 6, 7]],
                                             ins=[a.ap().opt()], outs=[b_.ap().opt()]).then_inc(cc_sem, 1)
                cc_cnt[0] += 1

        def gather_wait():
            for q in S.eng:
                S.eng[q].wait_ge(cc_sem, cc_cnt[0])

        def phase_B1(l, gr):
            S.phase_reset()
            with ExitStack() as ph:
                prep = alloc_prep(ph)
                hT, T_hT = prep[0], prep[1]
                wQ = sb("wQ", [128, 8, 1792], BF16, ph)
                T_wQ = Trk()
                wqb = sb("wqb", [128, 2, 768], BF16, ph)
                T_wqb = Trk()
                sl_w = S.slot()
                S.dma("sp", sl_w, wQ[:], wb[l]["wQ"].ap().rearrange("p (k n) -> p k n", k=8), reads=[T_wb[l]["wQ"]], writes=[T_wQ])
                S.dma("sp", sl_w, wqb[:], wb[l]["wqb"].ap().rearrange("p (c n) -> p c n", c=2), reads=[T_wb[l]["wqb"]], writes=[T_wqb])
                tabs = [sb("tab%d" % i, [128, 512], F32, ph) for i in range(4)]
                T_tab = Trk()
                sl_tab = S.slot()
                tmp = Tmp(ph, 8, "tmpB")
                qan = sb("qan", [128, 2, 512], BF16, ph)
                T_qan = Trk()
                qfull = Tmp(ph, 2, "qfull", BF16)
                T = gr.T
                prep_hT(prep, l, gr, 0)
                load_tabs(tabs, T_tab, sl_tab, gr)
                p0, tp0 = proj_block(wQ, T_wQ, 0, 128, hT, T_hT, T)
                p1, tp1 = proj_block(wQ, T_wQ, 128, 128, hT, T_hT, T)
                r, tr = rstd_bcast(tmp, [(p0, tp0), (p1, tp1)], onesf, 256.0, T)
                for c, (p_, t_) in enumerate(((p0, tp0), (p1, tp1))):
                    S.op("dve", lambda e: e.scalar_tensor_tensor(out=qan[:, c, 0:T], in0=p_[:, 0:T], scalar=colv[:, l, 80 + c:81 + c], in1=r[:, 0:T],
                                                                  op0=mult, op1=mult), reads=[t_, tr, T_colv], writes=[T_qan])
                for h in range(4):
                    pb, tpb = nb()
                    pbs, tpbs = nb()
                    for c in range(2):
                        S.op("pe", lambda e: e.matmul(pb[0:96, 0:T], wqb[:, c, h * 192:h * 192 + 96], qan[:, c, 0:T], start=(c == 0), stop=(c == 1)),
                             reads=[T_wqb, T_qan], writes=[tpb])
                    for c in range(2):
                        S.op("pe", lambda e: e.matmul(pbs[0:96, 0:T], wqb[:, c, h * 192 + 96:h * 192 + 192], qan[:, c, 0:T], start=(c == 0), stop=(c == 1)),
                             reads=[T_wqb, T_qan], writes=[tpbs])
                    S.op("act", lambda e: e.activation(out=QT[h][0:64, 0:T], in_=pb[0:64, 0:T], func=AF.Copy), reads=[tpb], writes=[T_QT[h]])
                    rope_comb(tmp, QT[h][64:96, 0:T], T_QT[h], pb, tpb, pbs, tpbs, tabs[0], tabs[1], T_tab, 64, 96, T)
                for b in range(2):
                    pb, tpb = proj_block(wQ, T_wQ, 256 + b * 256, 128, hT, T_hT, T)
                    pbs, tpbs = proj_block(wQ, T_wQ, 384 + b * 256, 128, hT, T_hT, T)
                    outs = [(QT[4 + b * 4 + j][j * 32:(j + 1) * 32, 0:T], T_QT[4 + b * 4 + j], j * 32, (j + 1) * 32) for j in range(4)]
                    rope_comb(tmp, None, None, pb, tpb, pbs, tpbs, tabs[0], tabs[1], T_tab, 0, 128, T, outs=outs)
                for i in range(4):
                    pb, tpb = proj_block(wQ, T_wQ, 768 + i * 256, 128, hT, T_hT, T)
                    pbs, tpbs = proj_block(wQ, T_wQ, 896 + i * 256, 128, hT, T_hT, T)
                    outs = [(QT[12 + j * 4 + i][j * 64:(j + 1) * 64, 0:T], T_QT[12 + j * 4 + i], j * 64, (j + 1) * 64) for j in range(2)]
                    gqa_block(tmp, pb, tpb, pbs, tpbs, colv[:, l, 83:84], colv[:, l, 84:85], tabs, T_tab, None, None, T, outs=outs)
                S.barrier()

        def phase_B2(l, gr):
            S.phase_reset()
            with ExitStack() as ph:
                NSL = 4
                kslot = [sb("kslot%d" % i, [128, NTOK], BF16, ph) for i in range(NSL)]
                vslot = [sb("vslot%d" % i, [128, NKR, 128], BF16, ph) for i in range(NSL)]
                T_ks = [Trk() for _ in range(NSL)]
                T_vs = [Trk() for _ in range(NSL)]
                sl_kv = [S.slot() for _ in range(NSL)]
                for i in range(NSL):
                    S.op("pool", lambda e: e.memset(vslot[i][:, :, 64:128], 0.0), writes=[T_vs[i]])
                Pt = [sb("Pt%d" % i, [128, 2, 512], BF16, ph) for i in range(3)]
                T_P = [Trk() for _ in range(3)]
                osb = Tmp(ph, 3, "osb")
                rec = Tmp(ph, 2, "rec")
                on_ = Tmp(ph, 3, "on")
                dtm = Tmp(ph, 3, "dtm")
                T = gr.T
                dg = Tmp(ph, 2, "dg")
                if cur_wout[0] != l:
                    S.dma("sp", sl_wout, wout[:], wb[l]["wout"].ap().rearrange("p (k n) -> p k n", k=8), reads=[T_wb[l]["wout"]], writes=[T_wout])
                    cur_wout[0] = l
                for w in range(2):
                    for half in range(2):
                        pg, tpg = nb()
                        for q in range(4):
                            k = half * 4 + q
                            d_, td_ = dg.get()
                            S.op("dve", lambda e: e.tensor_scalar(out=d_[:, 0:128], in0=ident, scalar1=Gc_t[:, w, gr.j, k:k + 1], scalar2=None, op0=mult),
                                 reads=[T_c, T_A], writes=[td_])
                            S.op("pe", lambda e: e.matmul(pg[:, q * 128:(q + 1) * 128], onesf, d_[:, 0:128], start=True, stop=True),
                                 reads=[td_, T_c], writes=[tpg])
                        S.op("act", lambda e: e.activation(out=Gbc[w][:, half * 512:(half + 1) * 512], in_=pg[:, :], func=AF.Copy),
                             reads=[tpg], writes=[T_G[w]])

                ring = [0]
                tcount = [0]
                obank = [0]
                pending = []
                chunks = [("ctx",)] if gr.ctx else [("r", 0), ("r", 1), ("r", 2), ("r", 3), ("ctx",)]

                units = []
                for h in range(4):
                    units.append(dict(q=h, pb=0, d=96, kr=[(h * 64, 64, 0), (256, 32, 64)], vu=h, sc=SC_MLA, kind="plain", u=h))
                for m in range(8):
                    units.append(dict(q=4 + m, pb=0, d=128, kr=[(288 + (m // 4) * 128, 128, 0)], vu=4 + m // 2, sc=SC_DIFF,
                                      kind="diff%d" % (m % 2), u=4 + m // 2))
                for hq in range(8):
                    kvh = hq // 4
                    units.append(dict(q=12 + hq, pb=0, d=128, kr=[(544, 128, 0)], vu=8 + kvh, sc=SC_GQA, kind="plain", u=8 + hq))

                def load_chunk(un, ch):
                    si = ring[0] % NSL
                    ring[0] += 1
                    if ch[0] == "r":
                        rk = ch[1]
                        for (r0, nr, p0) in un["kr"]:
                            s_, off, bn = kblk(r0)
                            S.dma("sp", sl_kv[si], kslot[si][p0:p0 + nr, 0:NTOK], k_all[l][s_].ap()[rk * bn + off:rk * bn + off + nr, :], writes=[T_ks[si]])
                        vu = un["vu"]
                        vsrc = v_all[l][vu // 2].ap()[rk * 256 + (vu % 2) * 128:rk * 256 + (vu % 2 + 1) * 128, :].rearrange("p (t c) -> p t c", c=66)
                        S.dma("sp", sl_kv[si], vslot[si][:, 0:NKR, 0:66], vsrc, writes=[T_vs[si]])
                        return si, NKR
                    else:
                        for (r0, nr, p0) in un["kr"]:
                            S.dma("sp", sl_kv[si], kslot[si][p0:p0 + nr, 0:256], kc_loc[l].ap()[r0:r0 + nr, :], writes=[T_ks[si]])
                        vsrc = vc_loc[l].ap()[un["vu"] * 128:(un["vu"] + 1) * 128, :].rearrange("p (t c) -> p t c", c=66)
                        S.dma("sp", sl_kv[si], vslot[si][:, 0:2, 0:66], vsrc, writes=[T_vs[si]])
                        return si, 2

                if l == 0 and gr.g is not None and NL > 1:
                    if gr.g == 0:
                        convert_lazy(0, ["ffn"])
                    if NG >= 4:
                        sched = {1: ["wA", "wkvb", "wQ", "wqb", "wout"], 2: ["ffn0", "ffn1"], 3: ["ffn2"]}.get(gr.g, [])
                    else:
                        sched = ["wA", "wkvb", "wQ", "wqb", "wout", "ffn"] if gr.g == 0 else []
                    convert_lazy(1, sched)
                conv_iv = max(1, (len(units_n := [0] * 20) * (len(chunks) * NKR // 2)) // (len(conv_queue) + 1)) if conv_queue else 0
                pair_ctr = [0]
                dstate = {}

                def finalize(un, po, tpo):
                    u = un["u"]
                    ob, tob = osb.get()
                    rc, trc = rec.get()
                    S.op("dve", lambda e: e.reciprocal(out=rc[64:65, 0:T], in_=po[64:65, 0:T]), reads=[tpo], writes=[trc])
                    S.op("act", lambda e: e.activation(out=ob[0:64, 0:T], in_=po[0:64, 0:T], func=AF.Copy), reads=[tpo], writes=[tob])
                    pbc, tpbc = po, tpo
                    S.op("pe", lambda e: e.matmul(pbc[0:64, 0:T], onesf[64:65, 0:64], rc[64:65, 0:T], start=True, stop=True),
                         reads=[trc, T_c], writes=[tpbc])
                    hp = (u % 2) * 64
                    if un["kind"] == "plain":
                        S.op("dve", lambda e: e.tensor_tensor(out=yT[hp:hp + 64, u // 2, 0:T], in0=ob[0:64, 0:T], in1=pbc[0:64, 0:T], op=mult),
                             reads=[tob, tpbc], writes=[T_yT[u]])
                        return
                    o_n, ton = on_.get()
                    S.op("dve", lambda e: e.tensor_tensor(out=o_n[0:64, 0:T], in0=ob[0:64, 0:T], in1=pbc[0:64, 0:T], op=mult),
                         reads=[tob, tpbc], writes=[ton])
                    if un["kind"] == "diff0":
                        dstate[u] = (o_n, ton)
                        return
                    o1, to1 = dstate.pop(u)
                    dd, tdd = dtm.get()
                    S.op("dve", lambda e: e.scalar_tensor_tensor(out=dd[0:64, 0:T], in0=o_n[0:64, 0:T], scalar=misc[0:64, 1:2], in1=o1[0:64, 0:T],
                                                                  op0=mult, op1=add), reads=[ton, to1, T_misc], writes=[tdd])
                    sq, tsq = dtm.get()
                    S.op("act", lambda e: e.activation(out=sq[0:64, 0:T], in_=dd[0:64, 0:T], func=AF.Square), reads=[tdd], writes=[tsq])
                    pss, tpss = po, tpo
                    S.op("pe", lambda e: e.matmul(pss[0:64, 0:T], onesf[0:64, 0:64], sq[0:64, 0:T], start=True, stop=True),
                         reads=[tsq, T_c], writes=[tpss])
                    rr, trr = dtm.get()
                    S.op("act", lambda e: e.activation(out=rr[0:64, 0:T], in_=pss[0:64, 0:T], func=AF.Ln, scale=1.0 / 64, bias=EPS),
                         reads=[tpss], writes=[trr])
                    S.op("act", lambda e: e.activation(out=rr[0:64, 0:T], in_=rr[0:64, 0:T], func=AF.Exp, scale=-0.5), reads=[trr], writes=[trr])
                    S.op("dve", lambda e: e.scalar_tensor_tensor(out=yT[hp:hp + 64, u // 2, 0:T], in0=dd[0:64, 0:T], scalar=misc[0:64, 2:3], in1=rr[0:64, 0:T],
                                                                  op0=mult, op1=mult), reads=[tdd, trr, T_misc], writes=[T_yT[u]])

                work = [(un, ch) for un in units for ch in chunks]
                loaded = {}
                nxt = [0]

                def ensure(idx):
                    idx = min(idx, len(work) - 1)
                    while nxt[0] <= idx:
                        loaded[nxt[0]] = load_chunk(*work[nxt[0]])
                        nxt[0] += 1

                for ui, un in enumerate(units):
                    tl = []
                    for ci, ch in enumerate(chunks):
                        wi = ui * len(chunks) + ci
                        for t in range(0, NKR if ch[0] == "r" else 2, 2):
                            tl.append((wi, t))
                    po, tpo = ps[6 + obank[0] % 2], T_ps[6 + obank[0] % 2]
                    obank[0] += 1
                    pb_, d_ = un["pb"], un["d"]
                    qt = QT[un["q"]]
                    tq = T_QT[un["q"]]
                    tp = (pb_, 0) if pb_ == 96 else None

                    def qk(n):
                        wi, t = tl[n]
                        ensure(wi + 2)
                        si = loaded[wi][0]
                        b2 = (tcount[0] + n) % 3
                        kw = {"tile_position": tp} if tp else {}
                        for h_ in range(2):
                            S.op("pe", lambda e: e.matmul(ps[2 * b2 + h_][:, 0:T], kslot[si][pb_:pb_ + d_, (t + h_) * 128:(t + h_ + 1) * 128],
                                                         qt[pb_:pb_ + d_, 0:T], start=True, stop=True, **kw),
                                 reads=[T_ks[si], tq], writes=[T_ps[2 * b2 + h_]], sig=(h_ == 1))

                    n_t = len(tl)
                    qk(0)
                    if n_t > 1:
                        qk(1)
                    for n in range(n_t):
                        if n + 2 < n_t:
                            qk(n + 2)
                        wi, t = tl[n]
                        si = loaded[wi][0]
                        b2 = (tcount[0] + n) % 3
                        b3 = (tcount[0] + n) % 3
                        S.op("act", lambda e: e.activation(out=Pt[b3][:, :, 0:T], in_=psd[b2][:, :].rearrange("p (h n) -> p h n", h=2)[:, :, 0:T],
                                                           func=AF.Exp, scale=un["sc"]),
                             reads=[T_ps[2 * b2], T_ps[2 * b2 + 1]], writes=[T_P[b3]])
                        for h_ in range(2):
                            S.op("pe", lambda e: e.matmul(po[:, 0:T], vslot[si][:, t + h_, :], Pt[b3][:, h_, 0:T],
                                                         start=(n == 0 and h_ == 0), stop=(n == n_t - 1 and h_ == 1)),
                                 reads=[T_vs[si], T_P[b3]], writes=[tpo])
                        if n == min(1, n_t - 1) and pending:
                            f = pending.pop(0)
                            f()
                        pair_ctr[0] += 1
                        if conv_iv and conv_queue and pair_ctr[0] % conv_iv == 0:
                            S._wait("pool", ("c", "act", S.n["act"] - 1))
                            conv_pop(1)
                    tcount[0] += n_t
                    pending.append(lambda un=un, po=po, tpo=tpo: finalize(un, po, tpo))
                while pending:
                    pending.pop(0)()
                conv_pop(len(conv_queue))
                S.barrier()

        def phase_B3(l, gr):
            S.phase_reset()
            with ExitStack() as ph:
                prep = alloc_prep(ph)
                hT, T_hT = prep[0], prep[1]
                NR = 5
                ringb = [sb("fring%d" % i, [128, 2048], BF16, ph) for i in range(NR)]
                T_ring = [Trk() for _ in range(NR)]
                sl_ring = [S.slot() for _ in range(NR)]
                rcount = [0]
                actT = sb("actT", [128, 22, 512], BF16, ph)
                T_act = [Trk() for _ in range(11)]
                tmp = Tmp(ph, 4, "tmpF")
                st2 = [sb("st2_%d" % i, [128, 8], F32, ph) for i in range(2)]
                T_st2 = [Trk(), Trk()]
                T = gr.T
                nt = len(gr.tts)
                def post(w, i, tt, pa, tpa, pb2, tpb2):
                    s_ = st2[i % 2]
                    ts_ = T_st2[i % 2]
                    jk, tjk = tmp.get()
                    S.op("act", lambda e: e.activation(out=jk[:, :], in_=pa[:, :], func=AF.Square, accum_out=s_[:, 0:1]), reads=[tpa], writes=[tjk, ts_])
                    jk2, tjk2 = tmp.get()
                    S.op("act", lambda e: e.activation(out=jk2[:, :], in_=pb2[:, :], func=AF.Square, accum_out=s_[:, 1:2]), reads=[tpb2], writes=[tjk2, ts_])
                    S.op("dve", lambda e: e.tensor_tensor(out=s_[:, 2:3], in0=s_[:, 0:1], in1=s_[:, 1:2], op=add), reads=[ts_], writes=[ts_])
                    S.op("act", lambda e: e.activation(out=s_[:, 3:4], in_=s_[:, 2:3], func=AF.Ln, scale=1.0 / 1024, bias=EPS),
                         reads=[ts_], writes=[ts_])
                    S.op("act", lambda e: e.activation(out=s_[:, 4:5], in_=s_[:, 3:4], func=AF.Exp, scale=-0.5), reads=[ts_], writes=[ts_])
                    for half, (p_, t_) in enumerate(((pa, tpa), (pb2, tpb2))):
                        tm, ttm = tmp.get()
                        S.op("dve", lambda e: e.scalar_tensor_tensor(out=tm[:, :], in0=p_[:, :], scalar=s_[:, 4:5], in1=Gbc[w][:, half * 512:(half + 1) * 512],
                                                                      op0=mult, op1=mult), reads=[t_, ts_, T_G[w]], writes=[ttm])
                        S.op("pool", lambda e: e.tensor_tensor(out=xs[:, tt, half * 512:(half + 1) * 512], in0=xs[:, tt, half * 512:(half + 1) * 512],
                                                               in1=tm[:, :], op=add), reads=[ttm], writes=[T_x[tt]])

                for i, tt in enumerate(gr.tts):
                    pa, tpa = nb()
                    pb2, tpb2 = nb()
                    for k in range(8):
                        for (p_, t_, half) in ((pa, tpa, 0), (pb2, tpb2, 1)):
                            S.op("pe", lambda e: e.matmul(p_[:, :], yT[:, k, i * 128:(i + 1) * 128], wout[:, k, half * 512:(half + 1) * 512],
                                                         start=(k == 0), stop=(k == 7)), reads=[T_yT[2 * k], T_yT[2 * k + 1], T_wout], writes=[t_])
                    post(0, i, tt, pa, tpa, pb2, tpb2)
                if l == 0:
                    dump("xa", xs[:, 0:4, :], [128, 4, 1024], F32)
                prep_hT(prep, l, gr, 1)

                def ring_load(name, jj):
                    ri = rcount[0] % NR
                    rcount[0] += 1
                    S.dma("sp", sl_ring[ri], ringb[ri][:], wb[l][name].ap()[jj], reads=[T_wb[l][name]], writes=[T_ring[ri]])
                    return ri

                for jj in range(11):
                    rg = ring_load("wg", jj)
                    ru = ring_load("wu", jj)
                    wgv = ringb[rg][:].rearrange("p (k n) -> p k n", k=8)
                    wuv = ringb[ru][:].rearrange("p (k n) -> p k n", k=8)
                    for s_ in range(2):
                        j = jj * 2 + s_
                        pg, tpg = nb()
                        pu, tpu = nb()
                        for k in range(8):
                            S.op("pe", lambda e: e.matmul(pg[:, 0:T], wgv[:, k, s_ * 128:(s_ + 1) * 128], hT[:, k, 0:T], start=(k == 0), stop=(k == 7)),
                                 reads=[T_ring[rg], T_hT], writes=[tpg])
                        for k in range(8):
                            S.op("pe", lambda e: e.matmul(pu[:, 0:T], wuv[:, k, s_ * 128:(s_ + 1) * 128], hT[:, k, 0:T], start=(k == 0), stop=(k == 7)),
                                 reads=[T_ring[ru], T_hT], writes=[tpu])
                        sg, tsg = tmp.get()
                        S.op("act", lambda e: e.activation(out=sg[:, 0:T], in_=pg[:, 0:T], func=AF.Silu), reads=[tpg], writes=[tsg])
                        S.op("dve", lambda e: e.tensor_tensor(out=actT[:, j, 0:T], in0=sg[:, 0:T], in1=pu[:, 0:T], op=mult),
                             reads=[tsg, tpu], writes=[T_act[jj]])
                rds = []
                for jj in range(11):
                    rds.append(None)
                for i0 in range(0, nt, 4):
                    tis = list(range(i0, min(i0 + 4, nt)))
                    acc = {}
                    for i in tis:
                        acc[i] = ((ps[2 * i], T_ps[2 * i]), (ps[2 * i + 1], T_ps[2 * i + 1]))
                    for jj in range(11):
                        rd = ring_load("wd", jj)
                        wdv = ringb[rd][:].rearrange("p (s n) -> p s n", s=2)
                        for s_ in range(2):
                            j = jj * 2 + s_
                            for i in tis:
                                for half in range(2):
                                    p_, t_ = acc[i][half]
                                    S.op("pe", lambda e: e.matmul(p_[:, :], actT[:, j, i * 128:(i + 1) * 128], wdv[:, s_, half * 512:(half + 1) * 512],
                                                                 start=(j == 0), stop=(j == 21)), reads=[T_act[jj], T_ring[rd]], writes=[t_])
                    for i in tis:
                        (pa, tpa), (pb2, tpb2) = acc[i]
                        post(1, i, gr.tts[i], pa, tpa, pb2, tpb2)
                S.barrier()

        S.barrier()
        def ctx_B(l):
            if dbg and l == 0:
                return
            phase_B1(l, cg)
            phase_B2(l, cg)
            phase_B3(l, cg)

        for l in range(NL):
            last = l == NL - 1
            layer_consts(l)
            if l == 0:
                phase_A(l, [cg] + groups)
                gather(l)
            else:
                phase_A(l, groups)
                gather(l)
                layer_consts(l - 1)
                ctx_B(l - 1)
                layer_consts(l)
                phase_A(l, [cg])
            if l + 1 < NL:
                phase_M(l + 1)
            if l == 0:
                dump("kc", kc_loc[0].ap(), [672, 256], BF16)
                dump("vc", vc_loc[0].ap(), [1280, 132], BF16)
                dump("modT", modT[:], [128, NL * 96], F32)
                dump("misc", misc[:], [128, 64], F32)
            waited = False
            for gr in groups:
                phase_B1(l, gr)
                if not waited:
                    gather_wait()
                    waited = True
                if l == 0:
                    for i in range(20):
                        dump("QT%d" % i, QT[i][:], [128, 512], BF16)
                phase_B2(l, gr)
                if l == 0:
                    dump("yT", yT[:], [128, 8, 512], BF16)
                phase_B3(l, gr)
                if l == 0:
                    dump("x1", xs[:, 0:4, :], [128, 4, 1024], F32)
        if dbg:
            ctx_B(0)
        S.barrier()
        for g4 in range(NG):
            S.dma("sp", sl_out, out_d[g4 * 512:(g4 + 1) * 512, :].rearrange("(t p) d -> p t d", p=128), xs[:, g4 * 4:(g4 + 1) * 4, :],
                  reads=T_x[g4 * 4:(g4 + 1) * 4])
        S.barrier()
    return nc


def _swap(w, r):
    n = w.shape[-1]
    idx = np.arange(n).reshape(-1, r)
    idx = np.concatenate([idx[:, r // 2:], idx[:, :r // 2]], axis=1).reshape(-1)
    return w[..., idx]


def _rope_tabs(pos, n_ctx):
    row = (pos // 64).astype(np.float32)
    col = (pos % 64).astype(np.float32)

    def tab(rot):
        quarter = rot // 4
        inv = (np.float32(10000.0) ** (-np.arange(quarter, dtype=np.float32) / np.float32(quarter))).astype(np.float32)
        ang = np.concatenate([row[:, None] * inv, col[:, None] * inv], axis=-1).astype(np.float32)
        return np.cos(ang).astype(np.float32), np.sin(ang).astype(np.float32)

    out = []
    for rot in (32, 64):
        c, s = tab(rot)
        C = np.concatenate([c, c], 1).T
        Sg = np.concatenate([-s, s], 1).T
        C = np.tile(C, (128 // rot, 1))
        Sg = np.tile(Sg, (128 // rot, 1))
        C = np.concatenate([C, np.ones((128, n_ctx), np.float32)], 1)
        Sg = np.concatenate([Sg, np.zeros((128, n_ctx), np.float32)], 1)
        out += [C, Sg]
    return np.ascontiguousarray(np.stack(out, 0), dtype=np.float32)


def _chunk_rows(w):
    L, R, N = w.shape
    return np.ascontiguousarray(w.reshape(L, R // 128, 128, N).transpose(0, 2, 1, 3))


def prep_shared(inp):
    f = lambda a: np.asarray(a, dtype=np.float32)
    L = f(inp["w_ada"]).shape[0]
    w_in = f(inp["w_in"])
    qa, ckv, kpe = w_in[..., 0:256], w_in[..., 256:384], w_in[..., 384:416]
    dq, dk, dv = w_in[..., 416:672], w_in[..., 672:928], w_in[..., 928:1184]
    gq, gk, gv = w_in[..., 1184:1696], w_in[..., 1696:1824], w_in[..., 1824:1952]
    A = [ckv, kpe, _swap(kpe, 32)]
    for b in range(2):
        blk = dk[..., b * 128:(b + 1) * 128]
        A += [blk, _swap(blk, 32)]
    A += [gk, _swap(gk, 64), dv, gv]
    wA = _chunk_rows(np.concatenate(A, -1))
    Q = [qa]
    for b in range(2):
        blk = dq[..., b * 128:(b + 1) * 128]
        Q += [blk, _swap(blk, 32)]
    for i in range(4):
        blk = np.concatenate([gq[..., i * 64:(i + 1) * 64], gq[..., (4 + i) * 64:(5 + i) * 64]], -1)
        Q += [blk, _swap(blk, 64)]
    wQ = _chunk_rows(np.concatenate(Q, -1))
    wqb = f(inp["w_mla_qb"])
    qb = []
    for h in range(4):
        blk = wqb[..., h * 96:(h + 1) * 96]
        qb += [blk, np.concatenate([blk[..., 0:64], _swap(blk[..., 64:96], 32)], -1)]
    wqb_r = _chunk_rows(np.concatenate(qb, -1))
    wkvb = f(inp["w_mla_kvb"]).reshape(L, 128, 4, 128)
    wkvb_r = np.ascontiguousarray(np.concatenate([wkvb[..., 0:64].reshape(L, 128, 256), wkvb[..., 64:128].reshape(L, 128, 256)], -1))
    wout_r = _chunk_rows(f(inp["w_out"]))
    w_ada = f(inp["w_ada"])
    wada_r = np.ascontiguousarray(w_ada.reshape(L, 8, 128, 12, 512).transpose(0, 3, 2, 1, 4).reshape(L, 12, 128, 4096))
    wg = f(inp["w_ffn_gate"]).reshape(L, 8, 128, 11, 256).transpose(0, 3, 2, 1, 4).reshape(L, 11, 128, 2048)
    wu = f(inp["w_ffn_up"]).reshape(L, 8, 128, 11, 256).transpose(0, 3, 2, 1, 4).reshape(L, 11, 128, 2048)
    wd = f(inp["w_ffn_down"]).reshape(L, 11, 2, 128, 1024).transpose(0, 1, 3, 2, 4).reshape(L, 11, 128, 2048)
    colv = np.zeros((L, 128, NV), np.float32)

    def cols(v):
        return v.reshape(L, -1, 128).transpose(0, 2, 1)

    colv[:, :, 0:8] = cols(f(inp["g_attn_pre"]))
    colv[:, :, 8:16] = cols(f(inp["g_attn_post"]))
    colv[:, :, 16:24] = cols(f(inp["g_ffn_pre"]))
    colv[:, :, 24:32] = cols(f(inp["g_ffn_post"]))
    colv[:, :, 32:80] = cols(f(inp["b_ada"]))
    colv[:, :, 80:82] = cols(f(inp["g_mla_q"]))
    colv[:, :, 82] = f(inp["g_mla_kv"])
    ggq, ggk = f(inp["g_gqa_q"]), f(inp["g_gqa_k"])
    colv[:, :, 83] = np.tile(ggq, (1, 2))
    colv[:, :, 84] = np.tile(_swap(ggq, 64), (1, 2))
    colv[:, :, 85] = np.tile(ggk, (1, 2))
    colv[:, :, 86] = np.tile(_swap(ggk, 64), (1, 2))
    colv[:, :, 87] = np.tile(f(inp["g_diff_sub"]), (1, 2))
    for i, nm in enumerate(("lambda_q1", "lambda_k1", "lambda_q2", "lambda_k2")):
        colv[:, :, 88 + 32 * i:120 + 32 * i] = f(inp[nm])[:, None, :]
    consts = np.zeros((128, 384), np.float32)
    consts[:, 0:128] = np.eye(128, dtype=np.float32)
    consts[:, 128:256] = 1.0
    consts[0:64, 256:320] = 1.0
    consts[64:128, 320:384] = 1.0
    return dict(w_ada_r=wada_r, colv=colv, w_inA=wA, w_inQ=wQ, w_qb_r=wqb_r, w_kvb_r=wkvb_r, w_out_r=wout_r,
                wg_r=np.ascontiguousarray(wg), wu_r=np.ascontiguousarray(wu), wd_r=np.ascontiguousarray(wd), consts=consts)


def make_in_maps(inp, NG):
    shared = prep_shared(inp)
    x = np.asarray(inp["x"], np.float32)
    c = np.asarray(inp["c"], np.float32)
    ctx = np.asarray(inp["ctx"], np.float32)
    c_ctx = np.asarray(inp["c_ctx"], np.float32)
    NTOK = NG * 512
    maps = []
    for core in range(8):
        b, q = core // 4, core % 4
        m = dict(shared)
        m["x_own"] = np.ascontiguousarray(x[b, q * NTOK:(q + 1) * NTOK, :])
        m["ctx_b"] = np.ascontiguousarray(ctx[b])
        cv = np.stack([c[b].reshape(8, 128).T, c_ctx.reshape(8, 128).T], -1)
        m["cvec"] = np.ascontiguousarray(cv.reshape(128, 16))
        m["rope"] = _rope_tabs(np.arange(q * NTOK, (q + 1) * NTOK), 256)
        maps.append(m)
    return maps


_NC_CACHE = {}


def run(inp, NG, ret_res=False):
    if NG not in _NC_CACHE:
        _NC_CACHE[NG] = build(NG)
    nc = _NC_CACHE[NG]
    maps = make_in_maps(inp, NG)
    res = run_bass_kernel_spmd(nc, maps, core_ids=list(range(8)))
    if ret_res:
        return res
    NTOK = NG * 512
    out = np.zeros((2, 4 * NTOK, 1024), np.float32)
    for core in range(8):
        b, q = core // 4, core % 4
        out[b, q * NTOK:(q + 1) * NTOK, :] = res.results[core]["out"]
    return out


def kernel(**inputs):
    return run(inputs, 4)
```
